# Optimizing a Trainium2 kernel written in Bass

```python
import math
import jax, jax.numpy as jnp
from jax import lax
import numpy as np

D_MODEL = 2048
BATCH = 2
SEQ = 4096
DEPTH = 1

N_META = 16
BLOCK = 128
N_PAD = BLOCK - N_META
WINDOW = 128
ATT_HEADS = 16
ATT_KV_HEADS = 4
ATT_GROUP = ATT_HEADS // ATT_KV_HEADS
ATT_HEAD_DIM = D_MODEL // 32
ATT_WIDTH = ATT_HEADS * ATT_HEAD_DIM
ATT_KV_WIDTH = ATT_KV_HEADS * ATT_HEAD_DIM
ML_HEADS = 4
ML_V_DIM = D_MODEL // 8
ML_QK_DIM = ML_V_DIM // 2
ML_WIDTH = ML_HEADS * ML_V_DIM
ML_QK_WIDTH = ML_HEADS * ML_QK_DIM
CONV_WIDTH = 4
GATE_SOFTCAP = 15.0
IN_SIZES = (ATT_WIDTH, ATT_KV_WIDTH, ATT_KV_WIDTH,
            ML_QK_WIDTH, ML_QK_WIDTH, ML_WIDTH,
            ML_WIDTH, ML_HEADS, ML_HEADS)
IN_COLS = sum(IN_SIZES)
MIX_WIDTH = ATT_WIDTH + ML_WIDTH
D_FF = 4 * D_MODEL
N_BUCKETS = 32
MAX_DISTANCE = 128
EPS = 1e-6
NEG = -1e30

kernel_name = "hymba_swa_sink_mlstm_sqrelu"


def rmsnorm(x, g):
    x32 = x.astype(jnp.float32)
    y = x32 * lax.rsqrt(jnp.mean(x32 * x32, axis=-1, keepdims=True) + EPS)
    return (y * g.astype(jnp.float32)).astype(x.dtype)


def t5_bucket(dist):
    max_exact = N_BUCKETS // 2
    d = jnp.maximum(dist, 0)
    ratio = jnp.maximum(d, max_exact).astype(jnp.float32) / max_exact
    large = max_exact + (jnp.log(ratio) / math.log(MAX_DISTANCE / max_exact)
                         * (N_BUCKETS - max_exact)).astype(jnp.int32)
    large = jnp.minimum(large, N_BUCKETS - 1)
    return jnp.where(d < max_exact, d, large)


def positional_tables(rel_bias, n_blocks):
    seq_len = n_blocks * BLOCK
    r = jnp.arange(BLOCK)[:, None]
    c = jnp.arange(2 * BLOCK)[None, :]
    dist_band = r + BLOCK - c
    bias_band = rel_bias.astype(jnp.float32)[t5_bucket(dist_band)]
    bias_band = bias_band.reshape(BLOCK, 2 * BLOCK, ATT_KV_HEADS, ATT_GROUP).transpose(2, 3, 0, 1)
    blk = jnp.arange(n_blocks)[:, None, None]
    k_pos = (blk - 1) * BLOCK + c[None]
    mask_band = (dist_band >= 0) & (dist_band < WINDOW) & (k_pos >= BLOCK)
    q_pos = jnp.arange(seq_len)[:, None]
    m_pos = N_PAD + jnp.arange(N_META)[None, :]
    dist_meta = q_pos - m_pos
    bias_meta = rel_bias.astype(jnp.float32)[t5_bucket(dist_meta)]
    bias_meta = bias_meta.reshape(n_blocks, BLOCK, N_META, ATT_KV_HEADS, ATT_GROUP).transpose(0, 3, 4, 1, 2)
    mask_meta = (dist_meta >= 0).reshape(n_blocks, BLOCK, N_META)
    return bias_band, mask_band, bias_meta, mask_meta


def swa_sink_attention(q, k, v, sinks, bias_band, mask_band, bias_meta, mask_meta):
    B, LP, _ = q.shape
    NB = LP // BLOCK
    q = q.reshape(B, NB, BLOCK, ATT_KV_HEADS, ATT_GROUP, ATT_HEAD_DIM)
    k = k.reshape(B, NB, BLOCK, ATT_KV_HEADS, ATT_HEAD_DIM)
    v = v.reshape(B, NB, BLOCK, ATT_KV_HEADS, ATT_HEAD_DIM)
    k_prev = jnp.concatenate([jnp.zeros_like(k[:, :1]), k[:, :-1]], axis=1)
    v_prev = jnp.concatenate([jnp.zeros_like(v[:, :1]), v[:, :-1]], axis=1)
    k_band = jnp.concatenate([k_prev, k], axis=2)
    v_band = jnp.concatenate([v_prev, v], axis=2)
    k_meta = k[:, 0, N_PAD:]
    v_meta = v[:, 0, N_PAD:]
    scale = ATT_HEAD_DIM ** -0.5
    s_band = jnp.einsum('bnqhgd,bnkhd->bnhgqk', q, k_band).astype(jnp.float32) * scale + bias_band
    s_meta = jnp.einsum('bnqhgd,bmhd->bnhgqm', q, k_meta).astype(jnp.float32) * scale + bias_meta
    logits = jnp.concatenate([s_band, s_meta], axis=-1)
    mask = jnp.concatenate([mask_band, mask_meta], axis=-1)[:, None, None]
    logits = jnp.where(mask, logits, NEG)
    sink = sinks.astype(jnp.float32).reshape(ATT_KV_HEADS, ATT_GROUP)[:, :, None, None]
    m = jnp.maximum(jnp.max(logits, axis=-1, keepdims=True), sink)
    p = jnp.exp(logits - m)
    denom = jnp.sum(p, axis=-1, keepdims=True) + jnp.exp(sink - m)
    p = (p / denom).astype(v.dtype)
    o = (jnp.einsum('bnhgqk,bnkhd->bnqhgd', p[..., :2 * BLOCK], v_band)
         + jnp.einsum('bnhgqm,bmhd->bnqhgd', p[..., 2 * BLOCK:], v_meta))
    return o.reshape(B, LP, ATT_WIDTH)


def causal_conv_silu(x, w, b):
    L = x.shape[1]
    xp = jnp.pad(x, ((0, 0), (CONV_WIDTH - 1, 0), (0, 0)))
    y = b
    for j in range(CONV_WIDTH):
        y = y + xp[:, j:j + L] * w[j]
    return jax.nn.silu(y)


def mlstm_chunkwise(q, k, v, i_pre, f_pre, valid):
    B, LP, _ = q.shape
    NC = LP // BLOCK
    f32 = jnp.float32
    q = q.astype(f32).reshape(B, NC, BLOCK, ML_HEADS, ML_QK_DIM) * (ML_QK_DIM ** -0.5)
    k = k.astype(f32).reshape(B, NC, BLOCK, ML_HEADS, ML_QK_DIM)
    v = v.astype(f32).reshape(B, NC, BLOCK, ML_HEADS, ML_V_DIM)
    vm = valid[None, :, None]
    log_i = jnp.where(vm, i_pre.astype(f32), NEG)
    log_f = jnp.where(vm, jax.nn.log_sigmoid(f_pre.astype(f32)), 0.0)
    log_i = log_i.reshape(B, NC, BLOCK, ML_HEADS).transpose(0, 1, 3, 2)
    log_f = log_f.reshape(B, NC, BLOCK, ML_HEADS).transpose(0, 1, 3, 2)
    b = jnp.cumsum(log_f, axis=-1)
    b_last = b[..., -1]
    g = b_last[..., None] - b + log_i
    m_loc = jnp.max(g, axis=-1)
    w = jnp.exp(g - m_loc[..., None])
    C_loc = jnp.einsum('bnhl,bnlhk,bnlhv->bnhkv', w, k, v)
    n_loc = jnp.einsum('bnhl,bnlhk->bnhk', w, k)

    def step(carry, xs):
        C, n, m = carry
        bl, ml, Cl, nl = xs
        m_new = jnp.maximum(bl + m, ml)
        a = jnp.exp(bl + m - m_new)
        c = jnp.exp(ml - m_new)
        C_new = a[..., None, None] * C + c[..., None, None] * Cl
        n_new = a[..., None] * n + c[..., None] * nl
        return (C_new, n_new, m_new), (C, n, m)

    init = (jnp.zeros((B, ML_HEADS, ML_QK_DIM, ML_V_DIM), f32),
            jnp.zeros((B, ML_HEADS, ML_QK_DIM), f32),
            jnp.zeros((B, ML_HEADS), f32))
    xs = (jnp.moveaxis(b_last, 1, 0), jnp.moveaxis(m_loc, 1, 0),
          jnp.moveaxis(C_loc, 1, 0), jnp.moveaxis(n_loc, 1, 0))
    _, (C_prev, n_prev, m_prev) = lax.scan(step, init, xs)
    C_prev = jnp.moveaxis(C_prev, 0, 1)
    n_prev = jnp.moveaxis(n_prev, 0, 1)
    m_prev = jnp.moveaxis(m_prev, 0, 1)
    D = b[..., :, None] - b[..., None, :] + log_i[..., None, :]
    causal = jnp.tril(jnp.ones((BLOCK, BLOCK), dtype=bool))
    D = jnp.where(causal, D, -jnp.inf)
    inter_log = b + m_prev[..., None]
    m_t = jnp.maximum(inter_log, jnp.max(D, axis=-1))
    S = jnp.einsum('bnthk,bnshk->bnhts', q, k) * jnp.exp(D - m_t[..., None])
    a = jnp.exp(inter_log - m_t)
    a_t = a.transpose(0, 1, 3, 2)[..., None]
    num = (jnp.einsum('bnhts,bnshv->bnthv', S, v)
           + a_t * jnp.einsum('bnthk,bnhkv->bnthv', q, C_prev))
    den = jnp.sum(S, axis=-1) + a * jnp.einsum('bnthk,bnhk->bnht', q, n_prev)
    den = jnp.maximum(jnp.abs(den), jnp.exp(-m_t))
    h = num / den.transpose(0, 1, 3, 2)[..., None]
    return h.reshape(B, LP, ML_HEADS, ML_V_DIM)


def setup_inputs(seed: int = 0) -> dict:
    key = jax.random.key(seed)
    ks = jax.random.split(key, 16)
    f32 = jnp.float32
    nrm = lambda k, shape, s: jax.random.normal(k, shape, f32) * s
    return {
        "x": nrm(ks[0], (BATCH, SEQ, D_MODEL), 1.0),
        "meta_tokens": nrm(ks[1], (N_META, D_MODEL), 1.0),
        "w_in": nrm(ks[2], (DEPTH, D_MODEL, IN_COLS), D_MODEL ** -0.5),
        "conv_w": nrm(ks[3], (DEPTH, CONV_WIDTH, 2 * ML_QK_WIDTH), CONV_WIDTH ** -0.5),
        "conv_b": nrm(ks[4], (DEPTH, 2 * ML_QK_WIDTH), 0.01),
        "b_igate": nrm(ks[5], (DEPTH, ML_HEADS), 0.1),
        "b_fgate": jnp.linspace(3.0, 6.0, ML_HEADS, dtype=f32)[None] + nrm(ks[6], (DEPTH, ML_HEADS), 0.1),
        "attn_sinks": nrm(ks[7], (DEPTH, ATT_HEADS), 0.5),
        "rel_bias": nrm(ks[8], (N_BUCKETS, ATT_HEADS), 0.5),
        "mh_norm": 1.0 + nrm(ks[9], (DEPTH, ML_WIDTH), 0.02),
        "w_out": nrm(ks[10], (DEPTH, MIX_WIDTH, D_MODEL), MIX_WIDTH ** -0.5),
        "norm_mix": 1.0 + nrm(ks[11], (DEPTH, D_MODEL), 0.02),
        "norm_mlp": 1.0 + nrm(ks[12], (DEPTH, D_MODEL), 0.02),
        "w_up": nrm(ks[13], (DEPTH, D_MODEL, D_FF), D_MODEL ** -0.5),
        "w_down": nrm(ks[14], (DEPTH, D_FF, D_MODEL), D_FF ** -0.5),
        "norm_final": 1.0 + nrm(ks[15], (D_MODEL,), 0.02),
    }


def reference(x, meta_tokens, w_in, conv_w, conv_b, b_igate, b_fgate, attn_sinks,
              rel_bias, mh_norm, w_out, norm_mix, norm_mlp, w_up, w_down, norm_final):
    B, S, D = x.shape
    LP = BLOCK + S
    NB = LP // BLOCK
    meta = jnp.broadcast_to(meta_tokens[None].astype(x.dtype), (B, N_META, D))
    h = jnp.concatenate([jnp.zeros((B, N_PAD, D), x.dtype), meta, x], axis=1)
    valid = jnp.arange(LP) >= N_PAD
    bias_band, mask_band, bias_meta, mask_meta = positional_tables(rel_bias, NB)
    offs = np.cumsum(IN_SIZES)[:-1].tolist()

    for l in range(DEPTH):
        u = rmsnorm(h, norm_mix[l])
        proj = u @ w_in[l]
        a_q, a_k, a_v, m_q, m_k, m_v, m_o, m_i, m_f = jnp.split(proj, offs, axis=-1)
        att = swa_sink_attention(a_q, a_k, a_v, attn_sinks[l],
                                 bias_band, mask_band, bias_meta, mask_meta)
        qk = causal_conv_silu(jnp.concatenate([m_q, m_k], axis=-1), conv_w[l], conv_b[l])
        m_q, m_k = qk[..., :ML_QK_WIDTH], qk[..., ML_QK_WIDTH:]
        i_pre = GATE_SOFTCAP * jnp.tanh((m_i + b_igate[l]) / GATE_SOFTCAP)
        f_pre = GATE_SOFTCAP * jnp.tanh((m_f + b_fgate[l]) / GATE_SOFTCAP)
        hm = mlstm_chunkwise(m_q, m_k, m_v, i_pre, f_pre, valid)
        hm = hm * lax.rsqrt(jnp.mean(hm * hm, axis=-1, keepdims=True) + EPS)
        hm = hm.reshape(B, LP, ML_WIDTH) * mh_norm[l].astype(jnp.float32)
        hm = (hm * jax.nn.sigmoid(m_o.astype(jnp.float32))).astype(x.dtype)
        mix = jnp.concatenate([att.astype(x.dtype), hm], axis=-1)
        h = h + mix @ w_out[l]
        u = rmsnorm(h, norm_mlp[l])
        h = h + jnp.square(jax.nn.relu(u @ w_up[l])) @ w_down[l]

    h = rmsnorm(h, norm_final)
    return h[:, BLOCK:]
```

```python
import numpy as np
import concourse.bass as bass
import concourse.mybir as mybir
from concourse.bass_utils import run_bass_kernel_spmd
from contextlib import ExitStack

F32 = mybir.dt.float32
BF16 = mybir.dt.bfloat16
AF = mybir.ActivationFunctionType
ALU = mybir.AluOpType
AX = mybir.AxisListType

D = 2048
KT = 16
T = 1168
OWN0 = 128
NOWN = 1024
META0 = 1152
CH3 = [(0, 384), (384, 384), (768, 400)]
CH2 = [(0, 512), (512, 512)]
IN_COLS = 4616
DFF = 8192
NCV = 672
EPS = 1e-6
NEGM = -30000.0
NSLOT = 5
F_T5LATE = True
F_ATTPIPE = True
F_MLPIPE = True
F_QFILL = True


class Op:
    __slots__ = ("eng", "fn", "deps", "kind", "semkey", "inc", "sigval", "need", "idx")


class Sched:
    def __init__(self):
        self.ops = []
        self.lastw = {}
        self.readers = {}
        self.pend = {}

    def add(self, eng, fn, reads=(), writes=(), kind="c", semkey=None, inc=16):
        op = Op()
        op.eng, op.fn, op.kind, op.semkey, op.inc = eng, fn, kind, semkey, inc
        op.need = kind != "c"
        op.sigval = None
        op.idx = len(self.ops)
        reads = [("ps", r[1]) if r[0] == "ps" else r for r in reads]
        writes = [("ps", r[1]) if r[0] == "ps" else r for r in writes]
        deps = set()
        for r in reads:
            w = self.lastw.get(r)
            if w is not None:
                deps.add((w, True))
            p = self.pend.get(r[0])
            if p:
                for o in p:
                    deps.add((o, True))
        for r in writes:
            w = self.lastw.get(r)
            if w is not None:
                deps.add((w, False))
            for o in self.readers.get(r, ()):
                deps.add((o, False))
            p = self.pend.get(r[0])
            if p:
                for o in p:
                    deps.add((o, True))
        for r in reads:
            self.readers.setdefault(r, []).append(op)
        for r in writes:
            self.lastw[r] = op
            self.readers[r] = []
        fin = {}
        for (o, raw) in deps:
            if o is op:
                continue
            if o.kind == "c" and o.eng == eng and eng == "pe":
                continue
            fin[o.idx] = o
        red = {}
        keep = []
        for o in fin.values():
            if o.kind == "c":
                if o.eng not in red or red[o.eng].idx < o.idx:
                    red[o.eng] = o
            else:
                keep.append(o)
        op.deps = keep + list(red.values())
        for o in op.deps:
            o.need = True
        self.ops.append(op)
        return op

    def collect(self, views):
        out = set()
        for k, w in self.lastw.items():
            if k[0] in views:
                out.add(w)
        for k, rs in self.readers.items():
            if k[0] in views:
                out.update(rs)
        return out

    def alias_after(self, new_views, old_views):
        deps = self.collect(set(old_views))
        for v in new_views:
            self.pend.setdefault(v, set()).update(deps)

    def emit(self, nc, es):
        engs = ["pe", "act", "dve", "pool", "sp"]
        semE = {e: es.enter_context(nc.semaphore("s_" + e)) for e in engs}
        semK = {}
        cntE = {e: 0 for e in engs}
        cntK = {}
        for op in self.ops:
            if op.kind == "c":
                if op.need:
                    cntE[op.eng] += 1
                    op.sigval = cntE[op.eng]
            else:
                if op.semkey not in semK:
                    semK[op.semkey] = es.enter_context(nc.semaphore("k_" + str(op.semkey)))
                    cntK[op.semkey] = 0
                cntK[op.semkey] += op.inc
                op.sigval = cntK[op.semkey]
        block = es.enter_context(nc.Block())
        per = {e: [o for o in self.ops if o.eng == e] for e in engs}

        def run(handle, e):
            waited = {}
            for op in per[e]:
                w = {}
                for d in op.deps:
                    s = semE[d.eng] if d.kind == "c" else semK[d.semkey]
                    k = id(s)
                    if k not in w or w[k][1] < d.sigval:
                        w[k] = (s, d.sigval)
                for k, (s, v) in w.items():
                    if waited.get(k, 0) >= v:
                        continue
                    waited[k] = v
                    handle.wait_ge(s, v)
                ins = op.fn(handle)
                if op.kind == "c":
                    if op.need:
                        ins.then_inc(semE[e], 1)
                else:
                    ins.then_inc(semK[op.semkey], op.inc)
            if e == "sp":
                for k, s in semK.items():
                    handle.wait_ge(s, cntK[k])

        @block.tensor
        def _(h):
            run(h, "pe")

        @block.scalar
        def _(h):
            run(h, "act")

        @block.vector
        def _(h):
            run(h, "dve")

        @block.gpsimd
        def _(h):
            run(h, "pool")

        @block.sync
        def _(h):
            run(h, "sp")


def build_program():
    nc = bass.Bass("TRN2", target_bir_lowering=False)
    es = ExitStack()
    S = Sched()

    def din(name, shape):
        return nc.dram_tensor(name, list(shape), F32, kind="ExternalInput").ap()

    x_in = din("xin", [T, D])
    w_in = din("w_in", [1, D, IN_COLS])
    w_out = din("w_out", [1, D, D])
    w_up = din("w_up", [1, D, DFF])
    w_down = din("w_down", [1, DFF, D])
    rows96 = din("rows96", [96, 128])
    rel33 = din("rel33", [33, 16])
    ohc = din("ohc", [33, NCV])
    small = din("small", [128, 64])
    out_d = nc.dram_tensor("out", [NOWN, D], F32, kind="ExternalOutput").ap()
    vtab = nc.dram_tensor("vtab", [16, NCV], F32)
    sum_in = nc.dram_tensor("sum_in", [512, 260], F32)
    sum_out = nc.dram_tensor("sum_out", [2048, 260], F32)

    def sb(name, shape, dt):
        return es.enter_context(nc.sbuf_tensor(name, list(shape), dt))

    uT = sb("uT", [128, KT, T], BF16)
    mixT = sb("mixT", [128, KT, NOWN], BF16)
    arena = sb("arena", [128, 16384], F32)
    wsl = [sb(f"wsl{i}", [128, KT, 128], BF16) for i in range(NSLOT)]
    xin = [sb(f"xin{i}", [128, 2064], F32) for i in range(2)]
    cst_rows = sb("cst_rows", [96, 128], F32)
    cstT = sb("cstT", [128, 96], F32)
    ident_f = sb("ident_f", [128, 128], F32)
    ident_b = sb("ident_b", [128, 128], BF16)
    U_f = sb("U_f", [128, 128], F32)
    maskneg = sb("maskneg", [128, 128], F32)
    ones_f = sb("ones_f", [128, 128], F32)
    ones_b = sb("ones_b", [128, 128], BF16)
    smalls = sb("smalls", [128, 64], F32)
    expsink = sb("expsink", [128, 16], F32)
    rel_sb = sb("rel_sb", [33, 16], F32)
    ohc_sb = sb("ohc_sb", [33, NCV], F32)
    vtab_sb = sb("vtab_sb", [16, NCV], F32)
    wg = sb("wg", [128, KT, 8], BF16)
    stat = sb("stat", [128, 64], F32)
    rstd_bc = [sb(f"rstd_bc{i}", [128, 512], F32) for i in range(2)]
    gts = sb("gts", [128, 12, 9, 4], F32)
    Cst = sb("Cst", [128, 4, 260], F32)
    Cstb = sb("Cstb", [128, 4, 260], BF16)
    coef = sb("coef", [128, 32], F32)
    brow_sb = sb("brow_sb", [36, 128], F32)
    st2 = sb("st2", [128, 32], F32)
    ogm = sb("ogm", [128, 8, NOWN], BF16)
    OWNCH = [(OWN0, 512), (OWN0 + 512, 512)]

    ps = [es.enter_context(nc.psum_tensor(f"ps{i}", [128, 512], F32)) for i in range(8)]
    psb = [p.bitcast(BF16) for p in ps]

    arena_b = arena.bitcast(BF16)

    def aview(off_bytes, shape, dt):
        h = arena if dt == F32 else arena_b
        esz = 4 if dt == F32 else 2
        assert off_bytes % 4 == 0
        n = int(np.prod(shape[1:]))
        o = off_bytes // esz
        assert off_bytes + n * esz <= 65536, (off_bytes, shape)
        ap = h[:, o:o + n]
        if len(shape) == 3:
            ap = ap.rearrange("p (a b) -> p a b", a=shape[1], b=shape[2])
        elif len(shape) == 4:
            ap = ap.rearrange("p (a b c) -> p a b c", a=shape[1], b=shape[2], c=shape[3])
        return ap

    hT = aview(0, [128, KT, NOWN], F32)
    qT_ml = aview(0, [128, 4, T], BF16)
    kT_ml = aview(9344, [128, 4, T], BF16)
    V_ml = aview(18688, [128, 9, 4, 257], BF16)
    qT_att = aview(0, [128, 8, T], BF16)
    kT_att = aview(18688, [128, 2, T], BF16)
    V_att = aview(23360, [128, 10, 4, 65], BF16)
    R2 = 37192
    xn = [aview(R2 + i * 8192, [128, 2048], F32) for i in range(2)]
    gbc = aview(R2 + 16384, [128, 2048], F32)
    vtmp = [aview(R2 + i * 2336, [128, T], BF16) for i in range(2)]
    cpre = [aview(R2 + 4672 + i * 4684, [128, T + 3], F32) for i in range(2)]
    cacc = [aview(R2 + 14040 + i * 4672, [128, T], F32) for i in range(2)]
    btab = [[aview(R2 + (i * 4 + k) * 2048, [128, 4, 128], F32) for k in range(4)] for i in range(2)]
    PTt = [[aview(R2 + 16384 + (i * 3 + k) * 1024, [128, 512], BF16) for k in range(3)] for i in range(2)]
    att_tok = [aview(R2 + 22528 + i * 512, [128, 4, 64], BF16) for i in range(2)]
    Ulf = [aview(R2 + i * 512, [128, 128], F32) for i in range(4)]
    Et = [aview(R2 + 2048 + i * 2048, [128, 4, 128], F32) for i in range(2)]
    STt = [aview(R2 + 6144 + i * 1024, [128, 4, 128], BF16) for i in range(2)]
    tmpI = [aview(R2 + 8192 + i * 1040, [128, 260], F32) for i in range(2)]
    tot = [aview(R2 + 10272 + i * 1040, [128, 260], F32) for i in range(2)]
    hmn = [aview(R2 + 12352 + i * 512, [128, 256], BF16) for i in range(2)]
    kw = [aview(R2 + 13376 + i * 256, [128, 128], BF16) for i in range(2)]
    junk = aview(R2 + 13888, [128, 256], F32)
    og_tmp = [aview(R2 + i * 1024, [128, 512], BF16) for i in range(2)]
    Cg = [xin[i][:, 0:2064].rearrange("p (a b c) -> p a b c", a=2, b=4, c=258) for i in range(2)]
    sqv = mixT

    ARENA_MIX = ["qT_ml", "kT_ml", "V_ml", "qT_att", "kT_att", "V_att", "xn", "gbc", "vtmp", "cpre", "cacc",
                 "btab", "PT", "att_tok", "Ulf", "E", "ST", "tmpI", "tot", "hmn", "kw", "junk", "og_tmp", "Dm", "At", "qa", "snap"]

    def dma(q, out, in_, reads, writes, key, **kw_):
        return S.add(q, lambda e, out=out, in_=in_, kw_=kw_: e.dma_start(out=out, in_=in_, **kw_),
                     reads=reads, writes=writes, kind="d", semkey=key)

    dma("sp", cst_rows[:, :], rows96[:, :], [], [("cst_rows",)], "c0")
    dma("sp", smalls[:, :], small[:, :], [], [("smalls",)], "c1")
    dma("sp", rel_sb[:, :], rel33[:, :], [], [("rel_sb",)], "c2")
    dma("sp", ohc_sb[:, :], ohc[:, :], [], [("ohc_sb",)], "c3")
    dma("pool", wg[:, :, :], w_in[0, :, 4608:4616].rearrange("(kt p) c -> p kt c", p=128),
        [], [("wg",)], "c4")

    S.add("pool", lambda e: e.memset(ones_f[:, :], 1.0), [], [("ones_f",)])
    for h in range(4):
        S.add("pool", lambda e, h=h: e.memset(Cst[:, h, :], 0.0), [], [("Cst", h)])
    S.add("pool", lambda e: e.memset(ones_b[:, :], 1.0), [], [("ones_b",)])
    S.add("pool", lambda e: e.affine_select(out=ident_f[:, :], in_=ones_f[:, :], pattern=[[-1, 128]],
                                            compare_op=ALU.is_equal, fill=0.0, base=0, channel_multiplier=1),
          [("ones_f",)], [("ident_f",)])
    S.add("pool", lambda e: e.tensor_copy(out=ident_b[:, :], in_=ident_f[:, :]), [("ident_f",)], [("ident_b",)])
    S.add("pool", lambda e: e.affine_select(out=U_f[:, :], in_=ones_f[:, :], pattern=[[1, 128]],
                                            compare_op=ALU.is_ge, fill=0.0, base=0, channel_multiplier=-1),
          [("ones_f",)], [("U_f",)])
    S.add("pool", lambda e: e.tensor_scalar(out=maskneg[:, :], in0=U_f[:, :], scalar1=-1.0, scalar2=-NEGM,
                                            op0=ALU.add, op1=ALU.mult),
          [("U_f",)], [("maskneg",)])

    S.add("pe", lambda e: e.transpose(out=ps[7][:, 0:96], in_=cst_rows[:, :], identity=ident_f[0:96, 0:96]),
          [("cst_rows",), ("ident_f",)], [("ps", 7)])
    S.add("dve", lambda e: e.tensor_copy(out=cstT[:, :], in_=ps[7][:, 0:96]), [("ps", 7)], [("cstT",)])
    C_NMIX, C_NMLP, C_NFIN, C_MH, C_CB, C_CW = 0, 16, 32, 48, 56, 64
    S.add("act", lambda e: e.activation(out=expsink[:, :], in_=smalls[:, 0:16], func=AF.Exp),
          [("smalls",)], [("expsink",)])
    def emit_t5():
        for c in range(2):
            S.add("pe", lambda e, c=c: e.matmul(ps[6 + c][0:16, 0:336], lhsT=rel_sb[:, :],
                                                rhs=ohc_sb[:, c * 336:(c + 1) * 336], start=True, stop=True),
                  [("rel_sb",), ("ohc_sb",)], [("ps", 6 + c)])
            S.add("dve", lambda e, c=c: e.tensor_copy(out=vtab_sb[:, c * 336:(c + 1) * 336],
                                                      in_=ps[6 + c][0:16, 0:336]),
                  [("ps", 6 + c)], [("vtab_sb", c)])
        dma("sp", vtab.ap()[:, :], vtab_sb[:, :], [("vtab_sb", 0), ("vtab_sb", 1)], [("vtab",)], "c5")
        tz = []
        for k, (off, npart) in enumerate([(255, 128), (127, 128), (383 + 15, 16), (526 + 15, 16)]):
            tzk = nc.dram_tensor(f"tz{k}", [16, npart, 128], F32)
            tz.append(tzk)
            dma("sp", tzk.ap()[:, :, :], bass.AP(tensor=vtab, offset=off, ap=[[NCV, 16], [-1, npart], [1, 128]]),
                [("vtab",)], [("tz", k)], ("tzk", k))
        return tz

    if not F_T5LATE:
        tz = emit_t5()
    dma("sp", gbc, bass.AP(tensor=rows96.tensor, offset=0, ap=[[0, 128], [1, 2048]]), [], [("gbc",)], "c6")

    panels = []

    def wcols(wap, c0, n):
        return wap[0, :, c0:c0 + n]

    for i in range(2):
        panels.append([(0, 128, wcols(w_in, 1024 + 128 * i, 128))])
    for i in range(2):
        panels.append([(0, 128, wcols(w_in, 1280 + 128 * i, 128))])
    for a in range(2):
        for b in range(4):
            panels.append([(0, 64, wcols(w_in, 64 * (8 * a + b), 64)), (64, 64, wcols(w_in, 64 * (8 * a + 4 + b), 64))])
    P_ATT = 0
    P_MK = len(panels)
    for h in range(4):
        panels.append([(0, 128, wcols(w_in, 2048 + 128 * h, 128))])
    P_MV = len(panels)
    for i in range(8):
        panels.append([(0, 128, wcols(w_in, 2560 + 128 * i, 128))])
    P_MQ = len(panels)
    for h in range(4):
        panels.append([(0, 128, wcols(w_in, 1536 + 128 * h, 128))])
    P_MO = len(panels)
    for i in range(8):
        panels.append([(0, 128, wcols(w_in, 3584 + 128 * i, 128))])
    P_OUT = len(panels)
    for i in range(16):
        panels.append([(0, 128, wcols(w_out, 128 * i, 128))])
    P_FFN = len(panels)
    for fc in range(4):
        for ft in range(16):
            panels.append([(0, 128, wcols(w_up, fc * 2048 + ft * 128, 128))])
        for ct in range(16):
            panels.append([(0, 128, w_down[0, fc * 2048:(fc + 1) * 2048, ct * 128:(ct + 1) * 128])])
    NP = len(panels)
    issued = [0]

    def issue_upto(n):
        while issued[0] < min(n, NP):
            i = issued[0]
            s = i % NSLOT
            for pi, (c0, ncol, ap) in enumerate(panels[i]):
                wk = [("wsl", s, pi)] if len(panels[i]) == 2 else [("wsl", s, 0), ("wsl", s, 1)]
                dma("pool", wsl[s][:, :, c0:c0 + ncol], ap.rearrange("(kt p) c -> p kt c", p=128),
                    list(first_reads), wk, ("w", s, pi))
            issued[0] += 1

    def wres(i):
        s = i % NSLOT
        return [("wsl", s, 0), ("wsl", s, 1)]

    first_reads = []

    blocks = [(i * 128, 128) for i in range(9)] + [(META0, 16)]
    for bi, (r0, nr) in enumerate(blocks):
        s = bi % 2
        dma("sp", xin[s][0:nr, 0:2048], x_in[r0:r0 + nr, :], [], [("xin", s)], ("x", s))
        ss = stat[0:nr, s:s + 1]
        S.add("act", lambda e, s=s, nr=nr, ss=ss: e.activation(out=xn[s][0:nr, :], in_=xin[s][0:nr, 0:2048],
                                                               func=AF.Square, accum_out=ss),
              [("xin", s)], [("xn", s), ("stat", s)])
        rs = stat[0:nr, 2 + s:3 + s]
        S.add("act", lambda e, ss=ss, rs=rs: e.activation(out=rs, in_=ss, func=AF.Sqrt, scale=1.0 / D, bias=EPS),
              [("stat", s)], [("stat", 2 + s)])
        S.add("dve", lambda e, rs=rs: e.reciprocal(out=rs, in_=rs), [("stat", 2 + s)], [("stat", 2 + s)])
        S.add("dve", lambda e, s=s, nr=nr, rs=rs: e.scalar_tensor_tensor(
            out=xn[s][0:nr, :], in0=xin[s][0:nr, 0:2048], scalar=rs, in1=gbc[0:nr, :], op0=ALU.mult, op1=ALU.mult),
            [("xin", s), ("stat", 2 + s), ("gbc",), ("xn", s)], [("xn", s)])
        for q4 in range(4):
            bank = 6 + (q4 % 2)
            for k4 in range(4):
                kt = q4 * 4 + k4
                S.add("pe", lambda e, s=s, nr=nr, kt=kt, k4=k4, bank=bank: e.transpose(
                    out=ps[bank][:, k4 * 128:k4 * 128 + nr], in_=xn[s][0:nr, kt * 128:(kt + 1) * 128],
                    identity=ident_f[0:nr, 0:nr]),
                    [("xn", s), ("ident_f",)], [("ps", bank, k4)])
            src = ps[bank][:, :].rearrange("p (a b) -> p a b", a=4, b=128)[:, :, 0:nr]
            dst = uT[:, q4 * 4:q4 * 4 + 4, r0:r0 + nr]
            rd = [("ps", bank, k4) for k4 in range(4)]
            if q4 % 2 == 0:
                S.add("act", lambda e, src=src, dst=dst: e.copy(out=dst, in_=src), rd, [("uT", q4, bi)])
            else:
                S.add("dve", lambda e, src=src, dst=dst: e.tensor_copy(out=dst, in_=src), rd, [("uT", q4, bi)])

    if F_T5LATE:
        tz = emit_t5()
    first_reads.extend([("xin", 0), ("xin", 1)])
    issue_upto(NSLOT - 1)
    del first_reads[:]

    def uT_res(c0, n):
        b0, b1 = c0 // 128, (c0 + n - 1) // 128
        return [("uT", q4, min(bi, 9)) for q4 in range(4) for bi in range(b0, b1 + 1)]

    ptile = [0]

    def project_gen(pi, chunks, rhs_of, rhs_res, evac, banks_per=3, fixed_par=None, yield_every=None):
        issue_upto(pi + NSLOT)
        s = pi % NSLOT
        if fixed_par is None:
            par = ptile[0] % 2
            ptile[0] += 1
        else:
            par = fixed_par
        wr = wres(pi)
        cnt = 0
        for kt in range(KT):
            for ci, (c0, n) in enumerate(chunks):
                bank = par * banks_per + ci
                S.add("pe", lambda e, s=s, kt=kt, c0=c0, n=n, bank=bank: e.matmul(
                    ps[bank][:, 0:n], lhsT=wsl[s][:, kt, :], rhs=rhs_of(kt, c0, n), start=(kt == 0), stop=(kt == KT - 1)),
                    wr + rhs_res(kt, c0, n), [("ps", bank)])
                cnt += 1
                if yield_every and cnt % yield_every == 0:
                    yield
        for ci, (c0, n) in enumerate(chunks):
            bank = par * banks_per + ci
            evac(ci, c0, n, ps[bank][:, 0:n], [("ps", bank)])
        yield

    def project(*a, **k):
        for _ in project_gen(*a, **k):
            pass

    def rhs_u(kt, c0, n):
        return uT[:, kt, c0:c0 + n]

    def res_u(kt, c0, n):
        return uT_res(c0, n)

    evflip = [0]

    def copy_evac(dst_of, wres_of, scale=None):
        def f(ci, c0, n, pap, pres):
            dst = dst_of(c0, n)
            evflip[0] += 1
            if scale is not None or evflip[0] % 2 == 0:
                if scale is None:
                    S.add("act", lambda e: e.copy(out=dst, in_=pap), pres, wres_of(ci))
                else:
                    S.add("act", lambda e: e.activation(out=dst, in_=pap, func=AF.Copy, scale=scale), pres, wres_of(ci))
            else:
                S.add("dve", lambda e: e.tensor_copy(out=dst, in_=pap), pres, wres_of(ci))
        return f

    S.alias_after(["qT_att", "kT_att", "V_att", "vtmp"], ["xn", "gbc"])
    for i in range(2):
        project(P_ATT + i, CH3, rhs_u, res_u,
                copy_evac(lambda c0, n, i=i: kT_att[:, i, c0:c0 + n], lambda ci, i=i: [("kT_att", i, ci)]))
    S.add("dve", lambda e: e.memset(V_att[:, :, :, 64:65], 1.0), [], [("V_att", "ones")])
    for i in range(2):
        vs = i % 2
        project(P_ATT + 2 + i, CH3, rhs_u, res_u,
                copy_evac(lambda c0, n, vs=vs: vtmp[vs][:, c0:c0 + n], lambda ci, vs=vs: [("vtmp", vs, ci)]))
        for blk in range(10):
            r0, nr = blocks[blk]
            ci = 0 if r0 < 384 else (1 if r0 < 768 else 2)
            bank, col = (7, blk * 128) if blk < 8 else (6, (blk - 8) * 128)
            S.add("pe", lambda e, vs=vs, r0=r0, nr=nr, bank=bank, col=col: e.transpose(
                out=psb[bank][0:nr, col:col + 128], in_=vtmp[vs][:, r0:r0 + nr], identity=ident_b[:, :]),
                [("vtmp", vs, ci), ("ident_b",)], [("ps", bank)])
        S.add("act", lambda e, i=i: e.copy(out=V_att[:, 0:8, 2 * i:2 * i + 2, 0:64],
                                           in_=psb[7][:, 0:1024].rearrange("p (a b c) -> p a b c", a=8, b=2, c=64)),
              [("ps", 7)], [("V_att", blk, i) for blk in range(8)])
        S.add("dve", lambda e, i=i: e.tensor_copy(out=V_att[:, 8, 2 * i:2 * i + 2, 0:64],
                                                  in_=psb[6][:, 0:128].rearrange("p (a b) -> p a b", a=2, b=64)),
              [("ps", 6)], [("V_att", 8, i)])
        S.add("dve", lambda e, i=i: e.tensor_copy(out=V_att[0:16, 9, 2 * i:2 * i + 2, 0:64],
                                                  in_=psb[6][0:16, 128:256].rearrange("p (a b) -> p a b", a=2, b=64)),
              [("ps", 6)], [("V_att", 9, i)])
    S.add("dve", lambda e: e.tensor_scalar(out=V_att[:, 0, :, :], in0=V_att[:, 0, :, :], scalar1=smalls[:, 30:31],
                                           scalar2=None, op0=ALU.mult),
          [("V_att", 0, 0), ("V_att", 0, 1), ("V_att", "ones"), ("smalls",)], [("V_att", 0, 0), ("V_att", 0, 1), ("V_att", "ones")])
    for a in range(2):
        for b in range(4):
            t8 = a * 4 + b
            project(P_ATT + 4 + t8, CH3, rhs_u, res_u,
                    copy_evac(lambda c0, n, t8=t8: qT_att[:, t8, c0:c0 + n], lambda ci, t8=t8: [("qT_att", t8, ci)],
                              scale=0.125))

    S.alias_after(["btab", "PT", "att_tok"], ["vtmp", "xn", "gbc"])

    def chunk_of(c):
        return 0 if c < 384 else (1 if c < 768 else 2)

    def att_ctx(i):
        g, n = i // 8, i % 8
        a, eh = g // 2, g % 2
        p0 = eh * 64
        ts = g % 2
        tabs = btab[ts]
        qo = OWN0 + 128 * n
        kprev, kcur = 128 * n, 128 * (n + 1)
        pset = i % 2
        sb3 = [pset * 3 + k for k in range(3)]
        spec = [(kprev, 128, tabs[0]), (kcur, 128, tabs[1]), (META0, 16, tabs[2] if n == 0 else tabs[3])]
        return g, n, a, p0, ts, tabs, qo, pset, sb3, spec

    def att_front(i):
        g, n, a, p0, ts, tabs, qo, pset, sb3, spec = att_ctx(i)
        if n == 0:
            for k, npart in enumerate([128, 128, 16, 16]):
                dma("sp", tabs[k][0:npart, :, :], tz[k].ap()[4 * g:4 * g + 4, :, :].rearrange("h c r -> c h r"),
                    [("tz", k)], [("btab", ts, k)], ("bt", ts, k))
        qres = [("qT_att", a * 4 + b, chunk_of(qo)) for b in range(4)]
        for k, (k0, nk, tb) in enumerate(spec):
            bank = sb3[k]
            S.add("pe", lambda e, a=a, p0=p0, k0=k0, nk=nk, qo=qo, bank=bank: e.matmul(
                ps[bank][0:nk, :], lhsT=kT_att[p0:p0 + 64, a, k0:k0 + nk],
                rhs=qT_att[p0:p0 + 64, a * 4:a * 4 + 4, qo:qo + 128], start=True, stop=True),
                [("kT_att", a, chunk_of(k0))] + qres, [("ps", bank)])
            tk = k if k < 2 else (2 if n == 0 else 3)
            S.add("dve", lambda e, bank=bank, nk=nk, tb=tb: e.tensor_tensor(
                out=ps[bank][0:nk, :], in0=ps[bank][0:nk, :], in1=tb[0:nk, :, :].rearrange("p a b -> p (a b)"),
                op=ALU.add), [("ps", bank), ("btab", ts, tk)], [("ps", bank)])
            S.add("act", lambda e, bank=bank, nk=nk, pset=pset, k=k: e.activation(
                out=PTt[pset][k][0:nk, :], in_=ps[bank][0:nk, :], func=AF.Exp),
                [("ps", bank)], [("PT", pset, k)])

    def att_back(i):
        g, n, a, p0, ts, tabs, qo, pset, sb3, spec = att_ctx(i)
        vb = [n, n + 1, 9]
        for b in range(4):
            for k, (k0, nk, tb) in enumerate(spec):
                S.add("pe", lambda e, b=b, k=k, nk=nk, pset=pset, vbk=vb[k], g=g: e.matmul(
                    ps[6][:, b * 65:(b + 1) * 65], lhsT=PTt[pset][k][0:nk, b * 128:(b + 1) * 128],
                    rhs=V_att[0:nk, vbk, g, :], start=(k == 0), stop=(k == 2)),
                    [("PT", pset, k), ("V_att", vb[k], g // 2), ("V_att", "ones")], [("ps", 6)])
        o4 = ps[6][:, 0:260].rearrange("p (a b) -> p a b", a=4, b=65)
        dc = 8 + 4 * pset
        den = stat[:, dc:dc + 4]
        DK = ("stat", "den", pset)
        S.add("dve", lambda e, o4=o4, g=g, den=den: e.tensor_tensor(
            out=den, in0=o4[:, :, 64], in1=expsink[:, 4 * g:4 * g + 4], op=ALU.add),
            [("ps", 6), ("expsink",)], [DK])
        S.add("dve", lambda e, den=den: e.reciprocal(out=den, in_=den), [DK], [DK])
        at = att_tok[pset]
        S.add("dve", lambda e, o4=o4, at=at, dc=dc: e.tensor_tensor(
            out=at, in0=o4[:, :, 0:64],
            in1=bass.AP(tensor=stat, offset=dc, ap=[[64, 128], [1, 4], [0, 64]]), op=ALU.mult),
            [("ps", 6), DK], [("att_tok", pset)])
        for i2 in range(2):
            S.add("pe", lambda e, at=at, i2=i2: e.transpose(
                out=psb[7][:, i2 * 128:(i2 + 1) * 128],
                in_=at[:, 2 * i2:2 * i2 + 2, :].rearrange("p a b -> p (a b)"), identity=ident_b[:, :]),
                [("att_tok", pset), ("ident_b",)], [("ps", 7, i2)])
        S.add("act", lambda e, g=g, n=n: e.copy(
            out=mixT[:, 2 * g:2 * g + 2, n * 128:(n + 1) * 128],
            in_=psb[7][:, 0:256].rearrange("p (a b) -> p a b", a=2, b=128)),
            [("ps", 7, 0), ("ps", 7, 1)], [("mixT", 2 * g, n), ("mixT", 2 * g + 1, n)])

    if F_ATTPIPE:
        att_front(0)
        for i in range(32):
            if i + 1 < 32:
                att_front(i + 1)
            att_back(i)
    else:
        for i in range(32):
            att_front(i)
            att_back(i)

    S.alias_after(["qT_ml", "kT_ml", "V_ml", "vtmp", "cpre", "cacc"],
                  ["qT_att", "kT_att", "V_att", "btab", "PT", "att_tok", "xn", "gbc"])
    for s_ in range(2):
        S.add("dve", lambda e, s_=s_: e.memset(cpre[s_][:, 0:3], 0.0), [], [("cpre", s_, "z")])
    def convproj_gen(which, base, dstT, h, cs, ye=None):
        ctile = h if which == "q" else 4 + h
        yield from project_gen(base + h, CH3, rhs_u, res_u,
                               copy_evac(lambda c0, n, cs=cs: cpre[cs][:, 3 + c0:3 + c0 + n], lambda ci, cs=cs: [("cpre", cs, ci)]),
                               yield_every=ye, fixed_par=(0 if ye else None))
        cr = [("cpre", cs, ci) for ci in range(3)] + [("cpre", cs, "z"), ("cstT",)]
        wcol = lambda j: cstT[:, C_CW + j * 8 + ctile:C_CW + j * 8 + ctile + 1]
        S.add("dve", lambda e, cs=cs, w0=wcol(0), bb=cstT[:, C_CB + ctile:C_CB + ctile + 1]: e.tensor_scalar(
            out=cacc[cs][:, :], in0=cpre[cs][:, 0:T], scalar1=w0, scalar2=bb, op0=ALU.mult, op1=ALU.add),
            cr, [("cacc", cs)])
        for j in range(1, 4):
            S.add("dve", lambda e, cs=cs, j=j, wj=wcol(j): e.scalar_tensor_tensor(
                out=cacc[cs][:, :], in0=cpre[cs][:, j:j + T], scalar=wj, in1=cacc[cs][:, :], op0=ALU.mult, op1=ALU.add),
                cr + [("cacc", cs)], [("cacc", cs)])
        S.add("act", lambda e, cs=cs, h=h, dstT=dstT: e.activation(out=dstT[:, h, :], in_=cacc[cs][:, :], func=AF.Silu),
              [("cacc", cs)], [(which + "T_ml", h)])
        yield

    for h in range(4):
        for _ in convproj_gen("k", P_MK, kT_ml, h, h % 2):
            pass
    S.add("dve", lambda e: e.memset(V_ml[:, :, :, 256:257], 1.0), [], [("V_ml", "ones")])
    for i in range(8):
        vs = i % 2
        h, hf = i // 2, i % 2
        project(P_MV + i, CH3, rhs_u, res_u,
                copy_evac(lambda c0, n, vs=vs: vtmp[vs][:, c0:c0 + n], lambda ci, vs=vs: [("vtmp", vs, ci)]))
        for blk in range(9):
            r0 = blk * 128
            bank, col = (7, blk * 128) if blk < 8 else (6, 0)
            S.add("pe", lambda e, vs=vs, r0=r0, bank=bank, col=col: e.transpose(
                out=psb[bank][:, col:col + 128], in_=vtmp[vs][:, r0:r0 + 128], identity=ident_b[:, :]),
                [("vtmp", vs, chunk_of(r0)), ("ident_b",)], [("ps", bank)])
        S.add("act", lambda e, h=h, hf=hf: e.copy(out=V_ml[:, 0:8, h, hf * 128:(hf + 1) * 128],
                                                  in_=psb[7][:, 0:1024].rearrange("p (a b) -> p a b", a=8, b=128)),
              [("ps", 7)], [("V_ml", blk, h, hf) for blk in range(8)])
        S.add("dve", lambda e, h=h, hf=hf: e.tensor_copy(out=V_ml[:, 8, h, hf * 128:(hf + 1) * 128], in_=psb[6][:, 0:128]),
              [("ps", 6)], [("V_ml", 8, h, hf)])
    for blk in range(9):
        for kt in range(KT):
            S.add("pe", lambda e, blk=blk, kt=kt: e.matmul(
                ps[6][:, blk * 8:(blk + 1) * 8], lhsT=uT[:, kt, blk * 128:(blk + 1) * 128], rhs=wg[:, kt, :],
                start=(kt == 0), stop=(kt == KT - 1)),
                uT_res(blk * 128, 128) + [("wg",)], [("ps", 6)])
    G_GS, G_LI, G_LF, G_B, G_BT, G_W, G_DEC, G_A, G_BIAS, G_T1, G_T2, G_T3 = range(12)
    gp = ps[6][:, 0:72].rearrange("p (a b) -> p a b", a=9, b=8)
    bias_i = bass.AP(tensor=smalls, offset=16, ap=[[64, 128], [0, 9], [1, 4]])
    bias_f = bass.AP(tensor=smalls, offset=20, ap=[[64, 128], [0, 9], [1, 4]])
    valid_bc = bass.AP(tensor=smalls, offset=40, ap=[[64, 128], [1, 9], [0, 4]])
    vneg_bc = bass.AP(tensor=smalls, offset=49, ap=[[64, 128], [1, 9], [0, 4]])
    GR = [("gts", i) for i in range(12)]
    S.add("dve", lambda e: e.tensor_tensor(out=gts[:, G_T1], in0=gp[:, :, 0:4], in1=bias_i, op=ALU.add),
          [("ps", 6), ("smalls",)], [("gts", G_T1)])
    S.add("dve", lambda e: e.tensor_tensor(out=gts[:, G_T2], in0=gp[:, :, 4:8], in1=bias_f, op=ALU.add),
          [("ps", 6), ("smalls",)], [("gts", G_T2)])
    S.add("act", lambda e: e.activation(out=gts[:, G_T1], in_=gts[:, G_T1], func=AF.Tanh, scale=1.0 / 15.0),
          [("gts", G_T1)], [("gts", G_T1)])
    S.add("act", lambda e: e.activation(out=gts[:, G_T2], in_=gts[:, G_T2], func=AF.Tanh, scale=1.0 / 15.0),
          [("gts", G_T2)], [("gts", G_T2)])
    S.add("dve", lambda e: e.scalar_tensor_tensor(out=gts[:, G_LI], in0=gts[:, G_T1], scalar=15.0, in1=valid_bc,
                                                  op0=ALU.mult, op1=ALU.mult), [("gts", G_T1), ("smalls",)], [("gts", G_LI)])
    S.add("dve", lambda e: e.tensor_tensor(out=gts[:, G_LI], in0=gts[:, G_LI], in1=vneg_bc, op=ALU.add),
          [("gts", G_LI), ("smalls",)], [("gts", G_LI)])
    S.add("act", lambda e: e.activation(out=gts[:, G_T3], in_=gts[:, G_T2], func=AF.Exp, scale=-15.0),
          [("gts", G_T2)], [("gts", G_T3)])
    S.add("act", lambda e: e.activation(out=gts[:, G_T3], in_=gts[:, G_T3], func=AF.Ln, bias=1.0),
          [("gts", G_T3)], [("gts", G_T3)])
    S.add("dve", lambda e: e.scalar_tensor_tensor(out=gts[:, G_LF], in0=gts[:, G_T3], scalar=-1.0, in1=valid_bc,
                                                  op0=ALU.mult, op1=ALU.mult), [("gts", G_T3), ("smalls",)], [("gts", G_LF)])
    for blk in range(9):
        S.add("pe", lambda e, blk=blk: e.matmul(ps[7][:, blk * 4:(blk + 1) * 4], lhsT=U_f[:, :], rhs=gts[:, G_LF, blk, :],
                                                start=True, stop=True), [("gts", G_LF), ("U_f",)], [("ps", 7)])
        S.add("pe", lambda e, blk=blk: e.matmul(ps[7][:, 64 + blk * 4:64 + (blk + 1) * 4], lhsT=ones_f[:, :],
                                                rhs=gts[:, G_LF, blk, :], start=True, stop=True),
              [("gts", G_LF), ("ones_f",)], [("ps", 7)])
    pb_ = ps[7][:, 0:36].rearrange("p (a b) -> p a b", a=9, b=4)
    pbt = ps[7][:, 64:100].rearrange("p (a b) -> p a b", a=9, b=4)
    S.add("dve", lambda e: e.tensor_copy(out=gts[:, G_B], in_=pb_), [("ps", 7)], [("gts", G_B)])
    S.add("dve", lambda e: e.tensor_copy(out=gts[:, G_BT], in_=pbt), [("ps", 7)], [("gts", G_BT)])
    S.add("dve", lambda e: e.tensor_tensor(out=gts[:, G_BIAS], in0=gts[:, G_LI], in1=gts[:, G_B], op=ALU.subtract),
          [("gts", G_LI), ("gts", G_B)], [("gts", G_BIAS)])
    S.add("dve", lambda e: e.tensor_tensor(out=gts[:, G_W], in0=gts[:, G_BIAS], in1=gts[:, G_BT], op=ALU.add),
          [("gts", G_BIAS), ("gts", G_BT)], [("gts", G_W)])
    S.add("act", lambda e: e.activation(out=gts[:, G_W], in_=gts[:, G_W], func=AF.Exp), [("gts", G_W)], [("gts", G_W)])
    S.add("act", lambda e: e.activation(out=gts[:, G_DEC], in_=gts[:, G_BT], func=AF.Exp), [("gts", G_BT)], [("gts", G_DEC)])
    S.add("dve", lambda e: e.tensor_scalar(out=gts[:, G_A], in0=gts[:, G_B], scalar1=float(np.log(128.0 ** -0.5)),
                                           scalar2=None, op0=ALU.add), [("gts", G_B)], [("gts", G_A)])
    S.add("act", lambda e: e.activation(out=gts[:, G_A], in_=gts[:, G_A], func=AF.Exp), [("gts", G_A)], [("gts", G_A)])
    brow = nc.dram_tensor("brow", [36, 128], F32)
    S.add("pe", lambda e: e.transpose(out=ps[7][0:36, 128:256], in_=gts[:, G_B].rearrange("p a b -> p (a b)"),
                                      identity=ident_f[:, :]), [("gts", G_B), ("ident_f",)], [("ps", 7)])
    S.add("dve", lambda e: e.tensor_copy(out=brow_sb[:, :], in_=ps[7][0:36, 128:256]), [("ps", 7)], [("brow_sb",)])
    dma("sp", brow.ap()[:, :], brow_sb[:, :], [("brow_sb",)], [("brow",)], "brw")

    S.alias_after(["kw", "junk", "og_tmp"], ["vtmp", "cpre", "cacc", "xn", "gbc", "btab", "PT", "att_tok"])
    Ulf = [aview(R2 + i * 512, [128, 128], F32) for i in range(4)]
    Dm = [aview(R2 + 2048 + i * 2048, [128, 4, 128], F32) for i in range(2)]
    Et = [aview(R2 + 6144 + i * 2048, [128, 4, 128], F32) for i in range(2)]
    STt = [aview(R2 + 10240 + i * 1024, [128, 4, 128], BF16) for i in range(2)]
    At = [aview(R2 + 12288 + i * 2048, [128, 4, 128], F32) for i in range(2)]
    qa = [aview(R2 + 16384 + i * 1024, [128, 4, 128], BF16) for i in range(2)]
    hmn = [aview(R2 + 18432 + i * 2048, [128, 4, 256], BF16) for i in range(2)]
    kw4 = [aview(R2 + 23224 + i * 1024, [128, 4, 128], BF16) for i in range(2)]
    junk = aview(R2 + 25272, [128, 256], F32)
    og_tmp = [aview(R2 + 26296 + i * 1024, [128, 512], BF16) for i in range(2)]
    Bbs = [aview(R2 + 0, [128, 4, 128], F32), Cstb.bitcast(F32)[:, :, :].rearrange("p a b -> p (a b)")[:, 0:512].rearrange("p (a b) -> p a b", a=4, b=128)]
    snapb = [xin[i].bitcast(BF16)[:, 0:4112].rearrange("p (a b c) -> p a b c", a=4, b=4, c=257) for i in range(2)]

    def snap(sn):
        return snapb[sn // 4][:, sn % 4]

    ogf = [0]

    def mo_gen():
        for i in range(8):
            def ev(ci, c0, n, pap, pres, i=i):
                os_ = ogf[0] % 2
                ogf[0] += 1
                S.add("act", lambda e: e.activation(out=og_tmp[os_][:, :], in_=pap, func=AF.Sigmoid), pres, [("og_tmp", os_)])
                o0 = c0 - OWN0
                S.add("dve", lambda e: e.tensor_scalar(out=ogm[:, i, o0:o0 + 512], in0=og_tmp[os_][:, :],
                                                       scalar1=cstT[:, C_MH + i:C_MH + i + 1], scalar2=None, op0=ALU.mult),
                      [("og_tmp", os_), ("cstT",)], [("ogm", i, o0 // 512)])
            yield from project_gen(P_MO + i, OWNCH, rhs_u, res_u, ev, banks_per=2, fixed_par=0, yield_every=8)

    if not F_QFILL:
        for h in range(4):
            for _ in convproj_gen("q", P_MQ, qT_ml, h, 0):
                pass

    def filler_gen():
        if F_QFILL:
            for h in range(4):
                yield from convproj_gen("q", P_MQ, qT_ml, h, 0, ye=8)
        yield from mo_gen()

    mo_it = filler_gen()

    def mo_step(k=1):
        for _ in range(k):
            try:
                next(mo_it)
            except StopIteration:
                return

    def state_step(blk, first, do_snap):
        ks = blk % 2
        c0 = blk * 128
        for h in range(4):
            S.add("pe", lambda e, h=h, c0=c0: e.transpose(out=psb[7][:, h * 128:(h + 1) * 128], in_=kT_ml[:, h, c0:c0 + 128],
                                                          identity=ident_b[:, :]), [("kT_ml", h), ("ident_b",)], [("ps", 7)])
        wbc = bass.AP(tensor=gts, offset=(G_W * 9 + blk) * 4, ap=[[12 * 9 * 4, 128], [1, 4], [0, 128]])
        S.add("dve", lambda e, ks=ks, wbc=wbc: e.tensor_tensor(
            out=kw4[ks], in0=psb[7][:, 0:512].rearrange("p (a b) -> p a b", a=4, b=128), in1=wbc, op=ALU.mult),
            [("ps", 7), ("gts", G_W)], [("kw", ks)])
        mo_step()
        for h in range(4):
            S.add("pe", lambda e, ks=ks, blk=blk, h=h: e.matmul(ps[3 + h][:, 0:257], lhsT=kw4[ks][:, h, :], rhs=V_ml[:, blk, h, :],
                                                               start=True, stop=True),
                  [("kw", ks), ("V_ml", blk, h, 0), ("V_ml", blk, h, 1), ("V_ml", "ones")], [("ps", 3 + h)])
        for h in range(4):
            if first:
                S.add("dve", lambda e, h=h: e.tensor_copy(out=Cst[:, h, 0:257], in_=ps[3 + h][:, 0:257]), [("ps", 3 + h)], [("Cst", h)])
            else:
                S.add("dve", lambda e, h=h, blk=blk: e.scalar_tensor_tensor(
                    out=Cst[:, h, 0:257], in0=Cst[:, h, 0:257], scalar=gts[:, G_DEC, blk, h:h + 1], in1=ps[3 + h][:, 0:257],
                    op0=ALU.mult, op1=ALU.add), [("ps", 3 + h), ("Cst", h), ("gts", G_DEC)], [("Cst", h)])
        if do_snap:
            S.add("act", lambda e, blk=blk: e.copy(out=snap(blk), in_=Cst[:, :, 0:257]), [("Cst", h) for h in range(4)], [("snap", blk)])
        mo_step()

    for blk in range(9):
        state_step(blk, first=(blk == 0), do_snap=False)
    S.add("dve", lambda e: e.tensor_reduce(out=stat[:, 16:20], in_=gts[:, G_BT].rearrange("p a b -> p b a"), axis=AX.X, op=ALU.add),
          [("gts", G_BT)], [("stat", "bt")])
    for h in range(4):
        S.add("dve", lambda e, h=h: e.tensor_copy(out=Cst[:, h, 257:258], in_=stat[:, 16 + h:17 + h]),
              [("stat", "bt"), ("Cst", h)], [("Cst", h)])
    dma("sp", sum_in.ap().rearrange("(h p) c -> p h c", p=128), Cst[:, :, :], [("Cst", h) for h in range(4)], [("sum_in",)], "sm0")
    S.add("pool", lambda e: e.collective_compute("AllGather", ALU.bypass, replica_groups=[[0, 1, 2, 3], [4, 5, 6, 7]],
                                                 ins=[sum_in.ap().opt()], outs=[sum_out.ap().opt()]),
          [("sum_in",)], [("sum_out",)], kind="cc", semkey="cc", inc=1)
    S.alias_after(["Cg"], ["xin"])
    for i2 in range(2):
        src = bass.AP(tensor=sum_out, offset=i2 * 2 * 512 * 260, ap=[[260, 128], [512 * 260, 2], [128 * 260, 4], [1, 258]])
        dma("sp", Cg[i2], src, [("sum_out",)], [("Cg", i2)], ("cg", i2))
    mo_step(42)

    def cgv(i):
        return Cg[i // 2][:, i % 2]
    ex = coef[:, 0:16].rearrange("p (a b) -> p a b", a=4, b=4)
    cf = coef[:, 16:32].rearrange("p (a b) -> p a b", a=4, b=4)
    S.add("dve", lambda e: e.memset(coef[:, 0:16], 0.0), [], [("coef",)])
    M01, M02, M12 = smalls[:, 24:25], smalls[:, 25:26], smalls[:, 26:27]
    CGR = [("Cg", 0), ("Cg", 1), ("smalls",)]
    S.add("dve", lambda e: e.tensor_scalar(out=ex[:, 0, :], in0=cgv(1)[:, :, 257], scalar1=M01, scalar2=None, op0=ALU.mult),
          CGR + [("coef",)], [("coef",)])
    S.add("dve", lambda e: e.scalar_tensor_tensor(out=ex[:, 0, :], in0=cgv(2)[:, :, 257], scalar=M02, in1=ex[:, 0, :],
                                                  op0=ALU.mult, op1=ALU.add), CGR + [("coef",)], [("coef",)])
    S.add("dve", lambda e: e.tensor_scalar(out=ex[:, 1, :], in0=cgv(2)[:, :, 257], scalar1=M12, scalar2=None, op0=ALU.mult),
          CGR + [("coef",)], [("coef",)])
    S.add("act", lambda e: e.activation(out=coef[:, 16:32], in_=coef[:, 0:16], func=AF.Exp), [("coef",)], [("coef", "c")])
    for i in range(3):
        S.add("dve", lambda e, i=i: e.tensor_scalar(out=cf[:, i, :], in0=cf[:, i, :], scalar1=smalls[:, 27 + i:28 + i],
                                                    scalar2=None, op0=ALU.mult), [("coef", "c"), ("smalls",)], [("coef", "c")])
    for h in range(4):
        S.add("dve", lambda e, h=h: e.tensor_scalar(out=Cst[:, h, 0:257], in0=cgv(0)[:, h, 0:257], scalar1=cf[:, 0, h:h + 1],
                                                    scalar2=None, op0=ALU.mult), CGR + [("coef", "c"), ("Cst", h)], [("Cst", h)])
        for i in range(1, 3):
            S.add("dve", lambda e, h=h, i=i: e.scalar_tensor_tensor(
                out=Cst[:, h, 0:257], in0=cgv(i)[:, h, 0:257], scalar=cf[:, i, h:h + 1], in1=Cst[:, h, 0:257],
                op0=ALU.mult, op1=ALU.add), CGR + [("coef", "c"), ("Cst", h)], [("Cst", h)])

    S.alias_after(["snap"], ["Cg", "xin"])
    for blk in range(8):
        state_step(blk, first=False, do_snap=True)
    mo_step(1000)
    uTf = uT.bitcast(F32)
    S.alias_after(["xpre"], ["uT"])
    xpre = [uTf[:, 4 * n:4 * n + 4, 0:512] for n in range(4)]
    for n in range(4):
        dma("sp", xpre[n], x_in[OWN0 + n * 128:OWN0 + (n + 1) * 128, :].rearrange("p (a b) -> p a b", a=4, b=512),
            [], [("xpre", n)], ("xp", n))

    S.alias_after(["Bbs", "Ulf", "Dm", "E", "ST", "At", "qa", "hmn"], ["vtmp", "cpre", "cacc", "xn", "gbc", "btab", "PT", "att_tok"])
    SCALE_ML = 128.0 ** -0.5
    LNS = float(np.log(SCALE_ML))
    maskneg_bc = bass.AP(tensor=maskneg, offset=0, ap=[[128, 128], [0, 4], [1, 128]])
    def ml_front(blk):
        c0 = blk * 128
        es_ = blk % 2
        bkq, bbb = es_, 2 + es_
        for h in range(4):
            S.add("pe", lambda e, h=h, c0=c0, bkq=bkq: e.matmul(ps[bkq][:, h * 128:(h + 1) * 128], lhsT=kT_ml[:, h, c0:c0 + 128],
                                                                rhs=qT_ml[:, h, c0:c0 + 128], start=True, stop=True),
                  [("kT_ml", h), ("qT_ml", h)], [("ps", bkq)])
        Bb = Bbs[es_]

        def bload(b_):
            dma("sp", Bbs[b_ % 2], bass.AP(tensor=brow, offset=b_ * 4 * 128, ap=[[0, 128], [128, 4], [1, 128]]),
                [("brow",)], [("Bbs", b_ % 2)], ("bb", b_ % 2))
        if blk == 1:
            bload(1)
        if blk + 1 <= 8:
            bload(blk + 1)
        pK = ps[bkq][:, :].rearrange("p (a b) -> p a b", a=4, b=128)
        S.add("act", lambda e, es_=es_, Bb=Bb: e.activation(out=At[es_], in_=Bb, func=AF.Exp, bias=LNS), [("Bbs", es_)], [("At", es_)])
        S.add("dve", lambda e, es_=es_, Bb=Bb: e.tensor_tensor(out=Dm[es_], in0=Bb, in1=maskneg_bc, op=ALU.add),
              [("Bbs", es_), ("maskneg",)], [("Dm", es_)])
        for h in range(4):
            S.add("act", lambda e, h=h, blk=blk, es_=es_: e.activation(
                out=Et[es_][:, h, :], in_=Dm[es_][:, h, :], func=AF.Exp, bias=gts[:, G_BIAS, blk, h:h + 1]),
                [("Dm", es_), ("gts", G_BIAS)], [("E", es_, h)])
        S.add("dve", lambda e, es_=es_, pK=pK: e.scalar_tensor_tensor(out=STt[es_], in0=pK, scalar=SCALE_ML, in1=Et[es_],
                                                                      op0=ALU.mult, op1=ALU.mult),
              [("ps", bkq)] + [("E", es_, h) for h in range(4)], [("ST", es_)])
        S.add("dve", lambda e, es_=es_, c0=c0: e.tensor_tensor(out=qa[es_], in0=qT_ml[:, :, c0:c0 + 128], in1=At[es_], op=ALU.mult),
              [("qT_ml", h) for h in range(4)] + [("At", es_)], [("qa", es_)])

    def ml_back(blk):
        n = blk - 1
        es_ = blk % 2
        sn = snap(blk - 1)
        for h in range(4):
            bn = 4 + h // 2
            cN = (h % 2) * 256
            S.add("pe", lambda e, h=h, es_=es_, blk=blk, bn=bn, cN=cN: e.matmul(
                ps[bn][:, cN:cN + 256], lhsT=STt[es_][:, h, :], rhs=V_ml[:, blk, h, 0:256], start=True, stop=False),
                [("ST", es_), ("V_ml", blk, h, 0), ("V_ml", blk, h, 1)], [("ps", bn)])
            S.add("pe", lambda e, h=h, es_=es_, bn=bn, cN=cN, sn=sn: e.matmul(
                ps[bn][:, cN:cN + 256], lhsT=qa[es_][:, h, :], rhs=sn[:, h, 0:256], start=False, stop=True),
                [("qa", es_), ("snap", blk - 1)], [("ps", bn)])
            S.add("pe", lambda e, h=h, es_=es_: e.matmul(ps[6][:, h:h + 1], lhsT=STt[es_][:, h, :], rhs=ones_b[:, 0:1],
                                                         start=True, stop=False), [("ST", es_), ("ones_b",)], [("ps", 6)])
            S.add("pe", lambda e, h=h, es_=es_, sn=sn: e.matmul(ps[6][:, h:h + 1], lhsT=qa[es_][:, h, :], rhs=sn[:, h, 256:257],
                                                                start=False, stop=True), [("qa", es_), ("snap", blk - 1)], [("ps", 6)])
        sd = st2[:, es_ * 16:(es_ + 1) * 16]
        SK = ("st2", es_)
        S.add("act", lambda e, sd=sd: e.copy(out=sd[:, 0:4], in_=ps[6][:, 0:4]), [("ps", 6)], [SK])
        S.add("dve", lambda e, sd=sd: e.scalar_tensor_tensor(out=sd[:, 0:4], in0=sd[:, 0:4], scalar=-1.0, in1=sd[:, 0:4],
                                                             op0=ALU.mult, op1=ALU.max), [SK], [SK])
        S.add("dve", lambda e, sd=sd: e.tensor_scalar(out=sd[:, 0:4], in0=sd[:, 0:4], scalar1=1.0, scalar2=None, op0=ALU.max), [SK], [SK])
        S.add("dve", lambda e, sd=sd: e.reciprocal(out=sd[:, 0:4], in_=sd[:, 0:4]), [SK], [SK])
        for h in range(4):
            bn = 4 + h // 2
            cN = (h % 2) * 256
            S.add("act", lambda e, h=h, bn=bn, cN=cN, sd=sd: e.activation(out=junk[:, :], in_=ps[bn][:, cN:cN + 256], func=AF.Square,
                                                                          accum_out=sd[:, 4 + h:5 + h]),
                  [("ps", bn)], [("junk",), ("st2s", es_, h)])
        SS = [("st2s", es_, h) for h in range(4)]
        S.add("dve", lambda e, sd=sd: e.tensor_tensor(out=sd[:, 8:12], in0=sd[:, 0:4], in1=sd[:, 0:4], op=ALU.mult), [SK], [("st2b", es_)])
        S.add("dve", lambda e, sd=sd: e.tensor_tensor(out=sd[:, 8:12], in0=sd[:, 8:12], in1=sd[:, 4:8], op=ALU.mult),
              [("st2b", es_)] + SS, [("st2b", es_)])
        S.add("act", lambda e, sd=sd: e.activation(out=sd[:, 8:12], in_=sd[:, 8:12], func=AF.Sqrt, scale=1.0 / 256.0, bias=EPS),
              [("st2b", es_)], [("st2b", es_)])
        S.add("dve", lambda e, sd=sd: e.reciprocal(out=sd[:, 8:12], in_=sd[:, 8:12]), [("st2b", es_)], [("st2b", es_)])
        S.add("dve", lambda e, sd=sd: e.tensor_tensor(out=sd[:, 12:16], in0=sd[:, 8:12], in1=sd[:, 0:4], op=ALU.mult),
              [("st2b", es_), SK], [("st2c", es_)])
        for hh in range(2):
            fbc = bass.AP(tensor=st2, offset=es_ * 16 + 12 + 2 * hh, ap=[[32, 128], [1, 2], [0, 256]])
            S.add("dve", lambda e, hh=hh, es_=es_, fbc=fbc: e.tensor_tensor(
                out=hmn[es_][:, 2 * hh:2 * hh + 2, :], in0=ps[4 + hh][:, :].rearrange("p (a b) -> p a b", a=2, b=256), in1=fbc,
                op=ALU.mult), [("ps", 4 + hh), ("st2c", es_)], [("hmn", es_, hh)])
        for h in range(4):
            for i2 in range(2):
                S.add("pe", lambda e, h=h, i2=i2, es_=es_: e.transpose(
                    out=psb[7][:, (2 * h + i2) * 128:(2 * h + i2 + 1) * 128], in_=hmn[es_][:, h, i2 * 128:(i2 + 1) * 128],
                    identity=ident_b[:, :]), [("hmn", es_, h // 2), ("ident_b",)], [("ps", 7)])
        S.add("dve", lambda e, n=n: e.tensor_tensor(
            out=mixT[:, 8:16, n * 128:(n + 1) * 128], in0=psb[7][:, :].rearrange("p (a b) -> p a b", a=8, b=128),
            in1=ogm[:, :, n * 128:(n + 1) * 128], op=ALU.mult),
            [("ps", 7)] + [("ogm", i, n // 4) for i in range(8)], [("mixT", 8 + i, n) for i in range(8)])

    if F_MLPIPE:
        ml_front(1)
        for blk in range(1, 9):
            if blk + 1 <= 8:
                ml_front(blk + 1)
            ml_back(blk)
    else:
        for blk in range(1, 9):
            ml_front(blk)
            ml_back(blk)

    S.alias_after(["hT", "xin"], ARENA_MIX + ["Cg"])
    def xload(n):
        dma("sp", xin[n % 2][:, 0:2048], x_in[OWN0 + n * 128:OWN0 + (n + 1) * 128, :], [], [("xin", n % 2)], ("x", n % 2))
    xload(4)
    xload(5)
    for n in range(8):
        s = n % 2
        if n in (5, 6):
            xload(n + 1)
        for q4 in range(4):
            bank = 6 + (q4 % 2)
            for k4 in range(4):
                kt = q4 * 4 + k4
                if n < 4:
                    xsrc = xpre[n][:, q4, k4 * 128:(k4 + 1) * 128]
                    xres = ("xpre", n)
                else:
                    xsrc = xin[s][:, kt * 128:(kt + 1) * 128]
                    xres = ("xin", s)
                S.add("pe", lambda e, xsrc=xsrc, k4=k4, bank=bank: e.transpose(
                    out=ps[bank][:, k4 * 128:(k4 + 1) * 128], in_=xsrc, identity=ident_f[:, :]),
                    [xres, ("ident_f",)], [("ps", bank, k4)])
            src = ps[bank][:, :].rearrange("p (a b) -> p a b", a=4, b=128)
            dst = hT[:, q4 * 4:q4 * 4 + 4, n * 128:(n + 1) * 128]
            rd = [("ps", bank, k4) for k4 in range(4)]
            wr_ = [("hT", q4 * 4 + k4, n) for k4 in range(4)]
            if q4 % 2 == 0:
                S.add("act", lambda e, src=src, dst=dst: e.copy(out=dst, in_=src), rd, wr_)
            else:
                S.add("dve", lambda e, src=src, dst=dst: e.tensor_copy(out=dst, in_=src), rd, wr_)

    def acc_evac(ct):
        def f(ci, c0, n, pap, pres):
            hres = [("hT", ct, c0 // 128 + k) for k in range(4)]
            S.add("dve", lambda e: e.tensor_tensor(out=hT[:, ct, c0:c0 + n], in0=pap, in1=hT[:, ct, c0:c0 + n], op=ALU.add),
                  pres + hres, hres)
        return f

    def rhs_mix(kt, c0, n):
        return mixT[:, kt, c0:c0 + n]

    def res_mix(kt, c0, n):
        return [("mixT", kt, c0 // 128 + k) for k in range(4)]

    for ct in range(16):
        project(P_OUT + ct, CH2, rhs_mix, res_mix, acc_evac(ct), banks_per=2)

    def fm_rstd(ci, c0):
        for kt in range(KT):
            hres = [("hT", kt, c0 // 128 + k) for k in range(4)]
            sres = [("mixT", kt, c0 // 128 + k) for k in range(4)]
            S.add("act", lambda e, kt=kt: e.activation(out=sqv[:, kt, c0:c0 + 512], in_=hT[:, kt, c0:c0 + 512], func=AF.Square),
                  hres, sres)
            S.add("pe", lambda e, kt=kt: e.matmul(ps[6][:, :], lhsT=ones_b[:, :], rhs=sqv[:, kt, c0:c0 + 512],
                                                  start=(kt == 0), stop=(kt == KT - 1)), sres + [("ones_b",)], [("ps", 6)])
        S.add("act", lambda e: e.activation(out=rstd_bc[ci][:, :], in_=ps[6][:, :], func=AF.Sqrt, scale=1.0 / D, bias=EPS),
              [("ps", 6)], [("rstd", ci)])
        S.add("dve", lambda e: e.reciprocal(out=rstd_bc[ci][:, :], in_=rstd_bc[ci][:, :]), [("rstd", ci)], [("rstd", ci)])

    S.alias_after(["uT"], ["xpre"])
    for ci, (c0, n) in enumerate(CH2):
        fm_rstd(ci, c0)
        for kt in range(KT):
            hres = [("hT", kt, c0 // 128 + k) for k in range(4)]
            ures = [("uT", kt // 4, c0 // 128 + k) for k in range(4)]
            S.add("dve", lambda e, kt=kt, c0=c0, ci=ci: e.scalar_tensor_tensor(
                out=uT[:, kt, c0:c0 + 512], in0=hT[:, kt, c0:c0 + 512], scalar=cstT[:, C_NMLP + kt:C_NMLP + kt + 1],
                in1=rstd_bc[ci][:, :], op0=ALU.mult, op1=ALU.mult), hres + [("rstd", ci), ("cstT",)], ures)

    relu_t = rstd_bc
    rf = [0]
    pidx = P_FFN
    for fc in range(4):
        for ft in range(16):
            def ev(ci, c0, n, pap, pres, ft=ft):
                rs_ = rf[0] % 2
                rf[0] += 1
                S.add("act", lambda e: e.activation(out=relu_t[rs_][:, :], in_=pap, func=AF.Relu), pres, [("rstd", rs_)])
                S.add("dve", lambda e: e.tensor_tensor(out=mixT[:, ft, c0:c0 + n], in0=relu_t[rs_][:, :], in1=relu_t[rs_][:, :],
                                                       op=ALU.mult), [("rstd", rs_)], [("mixT", ft, c0 // 128 + k) for k in range(4)])
            project(pidx, CH2, rhs_u, res_u, ev, banks_per=2)
            pidx += 1
        for ct in range(16):
            project(pidx, CH2, rhs_mix, res_mix, acc_evac(ct), banks_per=2)
            pidx += 1

    mixb = mixT.bitcast(F32)
    for ci, (c0, n) in enumerate(CH2):
        fm_rstd(ci, c0)
    S.alias_after(["oT"], ["xin", "Cg"])
    S.alias_after(["ostage"], ["mixT"])
    for n in range(8):
        ci, lc = n // 4, (n % 4) * 128
        s = n % 2
        oTt = xin[s][:, 0:2048]
        for kt in range(KT):
            S.add("dve", lambda e, kt=kt, n=n, ci=ci, lc=lc, oTt=oTt: e.scalar_tensor_tensor(
                out=oTt[:, kt * 128:(kt + 1) * 128], in0=hT[:, kt, n * 128:(n + 1) * 128],
                scalar=cstT[:, C_NFIN + kt:C_NFIN + kt + 1], in1=rstd_bc[ci][:, lc:lc + 128], op0=ALU.mult, op1=ALU.mult),
                [("hT", kt, n), ("rstd", ci), ("cstT",)], [("oT", s, kt)])
        for q4 in range(4):
            bank = 4 + (q4 % 2)
            for k4 in range(4):
                kt = q4 * 4 + k4
                S.add("pe", lambda e, kt=kt, k4=k4, bank=bank, oTt=oTt: e.transpose(
                    out=ps[bank][:, k4 * 128:(k4 + 1) * 128], in_=oTt[:, kt * 128:(kt + 1) * 128], identity=ident_f[:, :]),
                    [("oT", s, kt), ("ident_f",)], [("ps", bank, k4)])
            dst = mixb[:, s * 4 + q4, :]
            rd = [("ps", bank, k4) for k4 in range(4)]
            if q4 % 2 == 0:
                S.add("act", lambda e, dst=dst, bank=bank: e.copy(out=dst, in_=ps[bank][:, :]), rd, [("ostage", s, q4)])
            else:
                S.add("dve", lambda e, dst=dst, bank=bank: e.tensor_copy(out=dst, in_=ps[bank][:, :]), rd, [("ostage", s, q4)])
        dma("sp", out_d[n * 128:(n + 1) * 128, :], mixb[:, s * 4:(s + 1) * 4, :].rearrange("p a b -> p (a b)"),
            [("ostage", s, q4) for q4 in range(4)], [("outd", n)], ("o", s))
    S.emit(nc, es)
    es.close()
    return nc


def _t5_bucket(d):
    d = np.maximum(d, 0)
    ratio = (np.maximum(d, 16).astype(np.float32) / np.float32(16)).astype(np.float32)
    large = 16 + (np.log(ratio).astype(np.float32) / np.float32(np.log(128 / 16)) * np.float32(16)).astype(np.int32)
    large = np.minimum(large, 31)
    return np.where(d < 16, d, large)


def _ohc(j):
    oh = np.zeros((33, NCV), np.float32)
    e = np.arange(383)
    d = e - 127
    valid = (d >= 0) & (d < 128)
    bk = _t5_bucket(d)
    for i in range(383):
        if valid[i]:
            oh[bk[i], i] = 1.0
        else:
            oh[32, i] = NEGM
    ee = np.arange(143)
    if j == 0:
        bk = _t5_bucket(ee + 1)
    else:
        bk = np.full(143, 31)
    for i in range(143):
        oh[bk[i], 383 + i] = 1.0
        oh[31, 526 + i] = 1.0
    return oh


_NC_CACHE = {}


def kernel(x, meta_tokens, w_in, conv_w, conv_b, b_igate, b_fgate, attn_sinks, rel_bias, mh_norm, w_out,
           norm_mix, norm_mlp, w_up, w_down, norm_final):
    f = lambda a: np.ascontiguousarray(np.asarray(a, dtype=np.float32))
    x, meta_tokens, w_in, w_out, w_up, w_down = f(x), f(meta_tokens), f(w_in), f(w_out), f(w_up), f(w_down)
    rows96 = np.concatenate([f(norm_mix).reshape(16, 128), f(norm_mlp).reshape(16, 128), f(norm_final).reshape(16, 128),
                             f(mh_norm).reshape(8, 128), f(conv_b).reshape(8, 128), f(conv_w).reshape(32, 128)], axis=0)
    rel33 = np.concatenate([f(rel_bias), np.ones((1, 16), np.float32)], axis=0)
    lead = np.concatenate([np.zeros((112, D), np.float32), meta_tokens], axis=0)
    in_maps = []
    for c in range(8):
        b, j = c // 4, c % 4
        prev = lead if j == 0 else x[b, 1024 * j - 128:1024 * j]
        xin = np.concatenate([prev, x[b, 1024 * j:1024 * (j + 1)], meta_tokens], axis=0)
        small = np.zeros((128, 64), np.float32)
        small[:, 0:16] = f(attn_sinks).reshape(1, 16)
        small[:, 16:20] = f(b_igate).reshape(1, 4)
        small[:, 20:24] = f(b_fgate).reshape(1, 4)
        small[:, 24] = 1.0 if j > 1 else 0.0
        small[:, 25] = 1.0 if j > 2 else 0.0
        small[:, 26] = 1.0 if j > 2 else 0.0
        for i in range(3):
            small[:, 27 + i] = 1.0 if i < j else 0.0
        small[:, 30] = 0.0 if j == 0 else 1.0
        valid = np.ones((128, 9), np.float32)
        if j == 0:
            valid[:112, 0] = 0.0
        else:
            valid[:, 0] = 0.0
        small[:, 40:49] = valid
        small[:, 49:58] = (valid - 1.0) * (-NEGM)
        in_maps.append({"xin": np.ascontiguousarray(xin), "w_in": w_in, "w_out": w_out, "w_up": w_up, "w_down": w_down,
                        "rows96": rows96, "rel33": rel33, "ohc": _ohc(j), "small": small})
    if "nc" not in _NC_CACHE:
        _NC_CACHE["nc"] = build_program()
    res = run_bass_kernel_spmd(_NC_CACHE["nc"], in_maps, core_ids=list(range(8)))
    out = np.zeros((2, 4096, D), np.float32)
    for c in range(8):
        b, j = c // 4, c % 4
        out[b, 1024 * j:1024 * (j + 1)] = res.results[c]["out"]
    return out


def _deadlock_check(S):
    engs = ["pe", "act", "dve", "pool", "sp"]
    per = {e: [o for o in S.ops if o.eng == e] for e in engs}
    ptr = {e: 0 for e in engs}
    done = set()
    prog = True
    while prog:
        prog = False
        for e in engs:
            while ptr[e] < len(per[e]):
                op = per[e][ptr[e]]
                if all(d.idx in done for d in op.deps):
                    done.add(op.idx)
                    ptr[e] += 1
                    prog = True
                else:
                    break
    stuck = {e: (ptr[e], len(per[e])) for e in engs}
    return stuck, {e: per[e][ptr[e]] for e in engs if ptr[e] < len(per[e])}
```

```python
import numpy as np
import concourse.bass as bass
import concourse.mybir as mybir
from concourse.bass_utils import run_bass_kernel_spmd
from contextlib import ExitStack

F32 = mybir.dt.float32
BF16 = mybir.dt.bfloat16
AF = mybir.ActivationFunctionType
ALU = mybir.AluOpType
AX = mybir.AxisListType

D = 2048
KT = 16
T = 1168
OWN0 = 128
NOWN = 1024
META0 = 1152
CH3 = [(0, 384), (384, 384), (768, 400)]
CH2 = [(0, 512), (512, 512)]
IN_COLS = 4616
DFF = 8192
NCV = 672
EPS = 1e-6
NEGM = -30000.0
NSLOT = 5
F_T5LATE = True
F_ATTPIPE = True
F_MLPIPE = True
F_QFILL = True


class Op:
    __slots__ = ("eng", "fn", "deps", "kind", "semkey", "inc", "sigval", "need", "idx")


class Sched:
    def __init__(self):
        self.ops = []
        self.lastw = {}
        self.readers = {}
        self.pend = {}

    def add(self, eng, fn, reads=(), writes=(), kind="c", semkey=None, inc=16):
        op = Op()
        op.eng, op.fn, op.kind, op.semkey, op.inc = eng, fn, kind, semkey, inc
        op.need = kind != "c"
        op.sigval = None
        op.idx = len(self.ops)
        reads = [("ps", r[1]) if r[0] == "ps" else r for r in reads]
        writes = [("ps", r[1]) if r[0] == "ps" else r for r in writes]
        deps = set()
        for r in reads:
            w = self.lastw.get(r)
            if w is not None:
                deps.add((w, True))
            p = self.pend.get(r[0])
            if p:
                for o in p:
                    deps.add((o, True))
        for r in writes:
            w = self.lastw.get(r)
            if w is not None:
                deps.add((w, False))
            for o in self.readers.get(r, ()):
                deps.add((o, False))
            p = self.pend.get(r[0])
            if p:
                for o in p:
                    deps.add((o, True))
        for r in reads:
            self.readers.setdefault(r, []).append(op)
        for r in writes:
            self.lastw[r] = op
            self.readers[r] = []
        fin = {}
        for (o, raw) in deps:
            if o is op:
                continue
            if o.kind == "c" and o.eng == eng and eng == "pe":
                continue
            fin[o.idx] = o
        red = {}
        keep = []
        for o in fin.values():
            if o.kind == "c":
                if o.eng not in red or red[o.eng].idx < o.idx:
                    red[o.eng] = o
            else:
                keep.append(o)
        op.deps = keep + list(red.values())
        for o in op.deps:
            o.need = True
        self.ops.append(op)
        return op

    def collect(self, views):
        out = set()
        for k, w in self.lastw.items():
            if k[0] in views:
                out.add(w)
        for k, rs in self.readers.items():
            if k[0] in views:
                out.update(rs)
        return out

    def alias_after(self, new_views, old_views):
        deps = self.collect(set(old_views))
        for v in new_views:
            self.pend.setdefault(v, set()).update(deps)

    def emit(self, nc, es):
        engs = ["pe", "act", "dve", "pool", "sp"]
        semE = {e: es.enter_context(nc.semaphore("s_" + e)) for e in engs}
        semK = {}
        cntE = {e: 0 for e in engs}
        cntK = {}
        for op in self.ops:
            if op.kind == "c":
                if op.need:
                    cntE[op.eng] += 1
                    op.sigval = cntE[op.eng]
            else:
                if op.semkey not in semK:
                    semK[op.semkey] = es.enter_context(nc.semaphore("k_" + str(op.semkey)))
                    cntK[op.semkey] = 0
                cntK[op.semkey] += op.inc
                op.sigval = cntK[op.semkey]
        block = es.enter_context(nc.Block())
        per = {e: [o for o in self.ops if o.eng == e] for e in engs}

        def run(handle, e):
            waited = {}
            for op in per[e]:
                w = {}
                for d in op.deps:
                    s = semE[d.eng] if d.kind == "c" else semK[d.semkey]
                    k = id(s)
                    if k not in w or w[k][1] < d.sigval:
                        w[k] = (s, d.sigval)
                for k, (s, v) in w.items():
                    if waited.get(k, 0) >= v:
                        continue
                    waited[k] = v
                    handle.wait_ge(s, v)
                ins = op.fn(handle)
                if op.kind == "c":
                    if op.need:
                        ins.then_inc(semE[e], 1)
                else:
                    ins.then_inc(semK[op.semkey], op.inc)
            if e == "sp":
                for k, s in semK.items():
                    handle.wait_ge(s, cntK[k])

        @block.tensor
        def _(h):
            run(h, "pe")

        @block.scalar
        def _(h):
            run(h, "act")

        @block.vector
        def _(h):
            run(h, "dve")

        @block.gpsimd
        def _(h):
            run(h, "pool")

        @block.sync
        def _(h):
            run(h, "sp")


def build_program():
    nc = bass.Bass("TRN2", target_bir_lowering=False)
    es = ExitStack()
    S = Sched()

    def din(name, shape):
        return nc.dram_tensor(name, list(shape), F32, kind="ExternalInput").ap()

    x_in = din("xin", [T, D])
    w_in = din("w_in", [1, D, IN_COLS])
    w_out = din("w_out", [1, D, D])
    w_up = din("w_up", [1, D, DFF])
    w_down = din("w_down", [1, DFF, D])
    rows96 = din("rows96", [96, 128])
    rel33 = din("rel33", [33, 16])
    ohc = din("ohc", [33, NCV])
    small = din("small", [128, 64])
    out_d = nc.dram_tensor("out", [NOWN, D], F32, kind="ExternalOutput").ap()
    vtab = nc.dram_tensor("vtab", [16, NCV], F32)
    sum_in = nc.dram_tensor("sum_in", [512, 260], F32)
    sum_out = nc.dram_tensor("sum_out", [2048, 260], F32)

    def sb(name, shape, dt):
        return es.enter_context(nc.sbuf_tensor(name, list(shape), dt))

    uT = sb("uT", [128, KT, T], BF16)
    mixT = sb("mixT", [128, KT, NOWN], BF16)
    arena = sb("arena", [128, 16384], F32)
    wsl = [sb(f"wsl{i}", [128, KT, 128], BF16) for i in range(NSLOT)]
    xin = [sb(f"xin{i}", [128, 2064], F32) for i in range(2)]
    cst_rows = sb("cst_rows", [96, 128], F32)
    cstT = sb("cstT", [128, 96], F32)
    ident_f = sb("ident_f", [128, 128], F32)
    ident_b = sb("ident_b", [128, 128], BF16)
    U_f = sb("U_f", [128, 128], F32)
    maskneg = sb("maskneg", [128, 128], F32)
    ones_f = sb("ones_f", [128, 128], F32)
    ones_b = sb("ones_b", [128, 128], BF16)
    smalls = sb("smalls", [128, 64], F32)
    expsink = sb("expsink", [128, 16], F32)
    rel_sb = sb("rel_sb", [33, 16], F32)
    ohc_sb = sb("ohc_sb", [33, NCV], F32)
    vtab_sb = sb("vtab_sb", [16, NCV], F32)
    wg = sb("wg", [128, KT, 8], BF16)
    stat = sb("stat", [128, 64], F32)
    rstd_bc = [sb(f"rstd_bc{i}", [128, 512], F32) for i in range(2)]
    gts = sb("gts", [128, 12, 9, 4], F32)
    Cst = sb("Cst", [128, 4, 260], F32)
    Cstb = sb("Cstb", [128, 4, 260], BF16)
    coef = sb("coef", [128, 32], F32)
    brow_sb = sb("brow_sb", [36, 128], F32)
    st2 = sb("st2", [128, 32], F32)
    ogm = sb("ogm", [128, 8, NOWN], BF16)
    OWNCH = [(OWN0, 512), (OWN0 + 512, 512)]

    ps = [es.enter_context(nc.psum_tensor(f"ps{i}", [128, 512], F32)) for i in range(8)]
    psb = [p.bitcast(BF16) for p in ps]

    arena_b = arena.bitcast(BF16)

    def aview(off_bytes, shape, dt):
        h = arena if dt == F32 else arena_b
        esz = 4 if dt == F32 else 2
        assert off_bytes % 4 == 0
        n = int(np.prod(shape[1:]))
        o = off_bytes // esz
        assert off_bytes + n * esz <= 65536, (off_bytes, shape)
        ap = h[:, o:o + n]
        if len(shape) == 3:
            ap = ap.rearrange("p (a b) -> p a b", a=shape[1], b=shape[2])
        elif len(shape) == 4:
            ap = ap.rearrange("p (a b c) -> p a b c", a=shape[1], b=shape[2], c=shape[3])
        return ap

    hT = aview(0, [128, KT, NOWN], F32)
    qT_ml = aview(0, [128, 4, T], BF16)
    kT_ml = aview(9344, [128, 4, T], BF16)
    V_ml = aview(18688, [128, 9, 4, 257], BF16)
    qT_att = aview(0, [128, 8, T], BF16)
    kT_att = aview(18688, [128, 2, T], BF16)
    V_att = aview(23360, [128, 10, 4, 65], BF16)
    R2 = 37192
    xn = [aview(R2 + i * 8192, [128, 2048], F32) for i in range(2)]
    gbc = aview(R2 + 16384, [128, 2048], F32)
    vtmp = [aview(R2 + i * 2336, [128, T], BF16) for i in range(2)]
    cpre = [aview(R2 + 4672 + i * 4684, [128, T + 3], F32) for i in range(2)]
    cacc = [aview(R2 + 14040 + i * 4672, [128, T], F32) for i in range(2)]
    btab = [[aview(R2 + (i * 4 + k) * 2048, [128, 4, 128], F32) for k in range(4)] for i in range(2)]
    PTt = [[aview(R2 + 16384 + (i * 3 + k) * 1024, [128, 512], BF16) for k in range(3)] for i in range(2)]
    att_tok = [aview(R2 + 22528 + i * 512, [128, 4, 64], BF16) for i in range(2)]
    Ulf = [aview(R2 + i * 512, [128, 128], F32) for i in range(4)]
    Et = [aview(R2 + 2048 + i * 2048, [128, 4, 128], F32) for i in range(2)]
    STt = [aview(R2 + 6144 + i * 1024, [128, 4, 128], BF16) for i in range(2)]
    tmpI = [aview(R2 + 8192 + i * 1040, [128, 260], F32) for i in range(2)]
    tot = [aview(R2 + 10272 + i * 1040, [128, 260], F32) for i in range(2)]
    hmn = [aview(R2 + 12352 + i * 512, [128, 256], BF16) for i in range(2)]
    kw = [aview(R2 + 13376 + i * 256, [128, 128], BF16) for i in range(2)]
    junk = aview(R2 + 13888, [128, 256], F32)
    og_tmp = [aview(R2 + i * 1024, [128, 512], BF16) for i in range(2)]
    Cg = [xin[i][:, 0:2064].rearrange("p (a b c) -> p a b c", a=2, b=4, c=258) for i in range(2)]
    sqv = mixT

    ARENA_MIX = ["qT_ml", "kT_ml", "V_ml", "qT_att", "kT_att", "V_att", "xn", "gbc", "vtmp", "cpre", "cacc",
                 "btab", "PT", "att_tok", "Ulf", "E", "ST", "tmpI", "tot", "hmn", "kw", "junk", "og_tmp", "Dm", "At", "qa", "snap"]

    def dma(q, out, in_, reads, writes, key, **kw_):
        return S.add(q, lambda e, out=out, in_=in_, kw_=kw_: e.dma_start(out=out, in_=in_, **kw_),
                     reads=reads, writes=writes, kind="d", semkey=key)

    dma("sp", cst_rows[:, :], rows96[:, :], [], [("cst_rows",)], "c0")
    dma("sp", smalls[:, :], small[:, :], [], [("smalls",)], "c1")
    dma("sp", rel_sb[:, :], rel33[:, :], [], [("rel_sb",)], "c2")
    dma("sp", ohc_sb[:, :], ohc[:, :], [], [("ohc_sb",)], "c3")
    dma("pool", wg[:, :, :], w_in[0, :, 4608:4616].rearrange("(kt p) c -> p kt c", p=128),
        [], [("wg",)], "c4")

    S.add("pool", lambda e: e.memset(ones_f[:, :], 1.0), [], [("ones_f",)])
    for h in range(4):
        S.add("pool", lambda e, h=h: e.memset(Cst[:, h, :], 0.0), [], [("Cst", h)])
    S.add("pool", lambda e: e.memset(ones_b[:, :], 1.0), [], [("ones_b",)])
    S.add("pool", lambda e: e.affine_select(out=ident_f[:, :], in_=ones_f[:, :], pattern=[[-1, 128]],
                                            compare_op=ALU.is_equal, fill=0.0, base=0, channel_multiplier=1),
          [("ones_f",)], [("ident_f",)])
    S.add("pool", lambda e: e.tensor_copy(out=ident_b[:, :], in_=ident_f[:, :]), [("ident_f",)], [("ident_b",)])
    S.add("pool", lambda e: e.affine_select(out=U_f[:, :], in_=ones_f[:, :], pattern=[[1, 128]],
                                            compare_op=ALU.is_ge, fill=0.0, base=0, channel_multiplier=-1),
          [("ones_f",)], [("U_f",)])
    S.add("pool", lambda e: e.tensor_scalar(out=maskneg[:, :], in0=U_f[:, :], scalar1=-1.0, scalar2=-NEGM,
                                            op0=ALU.add, op1=ALU.mult),
          [("U_f",)], [("maskneg",)])

    S.add("pe", lambda e: e.transpose(out=ps[7][:, 0:96], in_=cst_rows[:, :], identity=ident_f[0:96, 0:96]),
          [("cst_rows",), ("ident_f",)], [("ps", 7)])
    S.add("dve", lambda e: e.tensor_copy(out=cstT[:, :], in_=ps[7][:, 0:96]), [("ps", 7)], [("cstT",)])
    C_NMIX, C_NMLP, C_NFIN, C_MH, C_CB, C_CW = 0, 16, 32, 48, 56, 64
    S.add("act", lambda e: e.activation(out=expsink[:, :], in_=smalls[:, 0:16], func=AF.Exp),
          [("smalls",)], [("expsink",)])
    def emit_t5():
        for c in range(2):
            S.add("pe", lambda e, c=c: e.matmul(ps[6 + c][0:16, 0:336], lhsT=rel_sb[:, :],
                                                rhs=ohc_sb[:, c * 336:(c + 1) * 336], start=True, stop=True),
                  [("rel_sb",), ("ohc_sb",)], [("ps", 6 + c)])
            S.add("dve", lambda e, c=c: e.tensor_copy(out=vtab_sb[:, c * 336:(c + 1) * 336],
                                                      in_=ps[6 + c][0:16, 0:336]),
                  [("ps", 6 + c)], [("vtab_sb", c)])
        dma("sp", vtab.ap()[:, :], vtab_sb[:, :], [("vtab_sb", 0), ("vtab_sb", 1)], [("vtab",)], "c5")
        tz = []
        for k, (off, npart) in enumerate([(255, 128), (127, 128), (383 + 15, 16), (526 + 15, 16)]):
            tzk = nc.dram_tensor(f"tz{k}", [16, npart, 128], F32)
            tz.append(tzk)
            dma("sp", tzk.ap()[:, :, :], bass.AP(tensor=vtab, offset=off, ap=[[NCV, 16], [-1, npart], [1, 128]]),
                [("vtab",)], [("tz", k)], ("tzk", k))
        return tz

    if not F_T5LATE:
        tz = emit_t5()
    dma("sp", gbc, bass.AP(tensor=rows96.tensor, offset=0, ap=[[0, 128], [1, 2048]]), [], [("gbc",)], "c6")

    panels = []

    def wcols(wap, c0, n):
        return wap[0, :, c0:c0 + n]

    for i in range(2):
        panels.append([(0, 128, wcols(w_in, 1024 + 128 * i, 128))])
    for i in range(2):
        panels.append([(0, 128, wcols(w_in, 1280 + 128 * i, 128))])
    for a in range(2):
        for b in range(4):
            panels.append([(0, 64, wcols(w_in, 64 * (8 * a + b), 64)), (64, 64, wcols(w_in, 64 * (8 * a + 4 + b), 64))])
    P_ATT = 0
    P_MK = len(panels)
    for h in range(4):
        panels.append([(0, 128, wcols(w_in, 2048 + 128 * h, 128))])
    P_MV = len(panels)
    for i in range(8):
        panels.append([(0, 128, wcols(w_in, 2560 + 128 * i, 128))])
    P_MQ = len(panels)
    for h in range(4):
        panels.append([(0, 128, wcols(w_in, 1536 + 128 * h, 128))])
    P_MO = len(panels)
    for i in range(8):
        panels.append([(0, 128, wcols(w_in, 3584 + 128 * i, 128))])
    P_OUT = len(panels)
    for i in range(16):
        panels.append([(0, 128, wcols(w_out, 128 * i, 128))])
    P_FFN = len(panels)
    for fc in range(4):
        for ft in range(16):
            panels.append([(0, 128, wcols(w_up, fc * 2048 + ft * 128, 128))])
        for ct in range(16):
            panels.append([(0, 128, w_down[0, fc * 2048:(fc + 1) * 2048, ct * 128:(ct + 1) * 128])])
    NP = len(panels)
    issued = [0]

    def issue_upto(n):
        while issued[0] < min(n, NP):
            i = issued[0]
            s = i % NSLOT
            for pi, (c0, ncol, ap) in enumerate(panels[i]):
                wk = [("wsl", s, pi)] if len(panels[i]) == 2 else [("wsl", s, 0), ("wsl", s, 1)]
                dma("pool", wsl[s][:, :, c0:c0 + ncol], ap.rearrange("(kt p) c -> p kt c", p=128),
                    [], wk, ("w", s, pi))
            issued[0] += 1

    def wres(i):
        s = i % NSLOT
        return [("wsl", s, 0), ("wsl", s, 1)]

    issue_upto(NSLOT - 1)

    blocks = [(i * 128, 128) for i in range(9)] + [(META0, 16)]
    for bi, (r0, nr) in enumerate(blocks):
        s = bi % 2
        dma("sp", xin[s][0:nr, 0:2048], x_in[r0:r0 + nr, :], [], [("xin", s)], ("x", s))
        ss = stat[0:nr, s:s + 1]
        S.add("act", lambda e, s=s, nr=nr, ss=ss: e.activation(out=xn[s][0:nr, :], in_=xin[s][0:nr, 0:2048],
                                                               func=AF.Square, accum_out=ss),
              [("xin", s)], [("xn", s), ("stat", s)])
        rs = stat[0:nr, 2 + s:3 + s]
        S.add("act", lambda e, ss=ss, rs=rs: e.activation(out=rs, in_=ss, func=AF.Sqrt, scale=1.0 / D, bias=EPS),
              [("stat", s)], [("stat", 2 + s)])
        S.add("dve", lambda e, rs=rs: e.reciprocal(out=rs, in_=rs), [("stat", 2 + s)], [("stat", 2 + s)])
        S.add("dve", lambda e, s=s, nr=nr, rs=rs: e.scalar_tensor_tensor(
            out=xn[s][0:nr, :], in0=xin[s][0:nr, 0:2048], scalar=rs, in1=gbc[0:nr, :], op0=ALU.mult, op1=ALU.mult),
            [("xin", s), ("stat", 2 + s), ("gbc",), ("xn", s)], [("xn", s)])
        for q4 in range(4):
            bank = 6 + (q4 % 2)
            for k4 in range(4):
                kt = q4 * 4 + k4
                S.add("pe", lambda e, s=s, nr=nr, kt=kt, k4=k4, bank=bank: e.transpose(
                    out=ps[bank][:, k4 * 128:k4 * 128 + nr], in_=xn[s][0:nr, kt * 128:(kt + 1) * 128],
                    identity=ident_f[0:nr, 0:nr]),
                    [("xn", s), ("ident_f",)], [("ps", bank, k4)])
            src = ps[bank][:, :].rearrange("p (a b) -> p a b", a=4, b=128)[:, :, 0:nr]
            dst = uT[:, q4 * 4:q4 * 4 + 4, r0:r0 + nr]
            rd = [("ps", bank, k4) for k4 in range(4)]
            if q4 % 2 == 0:
                S.add("act", lambda e, src=src, dst=dst: e.copy(out=dst, in_=src), rd, [("uT", q4, bi)])
            else:
                S.add("dve", lambda e, src=src, dst=dst: e.tensor_copy(out=dst, in_=src), rd, [("uT", q4, bi)])

    if F_T5LATE:
        tz = emit_t5()

    def uT_res(c0, n):
        b0, b1 = c0 // 128, (c0 + n - 1) // 128
        return [("uT", q4, min(bi, 9)) for q4 in range(4) for bi in range(b0, b1 + 1)]

    ptile = [0]

    def project_gen(pi, chunks, rhs_of, rhs_res, evac, banks_per=3, fixed_par=None, yield_every=None):
        issue_upto(pi + NSLOT)
        s = pi % NSLOT
        if fixed_par is None:
            par = ptile[0] % 2
            ptile[0] += 1
        else:
            par = fixed_par
        wr = wres(pi)
        cnt = 0
        for kt in range(KT):
            for ci, (c0, n) in enumerate(chunks):
                bank = par * banks_per + ci
                S.add("pe", lambda e, s=s, kt=kt, c0=c0, n=n, bank=bank: e.matmul(
                    ps[bank][:, 0:n], lhsT=wsl[s][:, kt, :], rhs=rhs_of(kt, c0, n), start=(kt == 0), stop=(kt == KT - 1)),
                    wr + rhs_res(kt, c0, n), [("ps", bank)])
                cnt += 1
                if yield_every and cnt % yield_every == 0:
                    yield
        for ci, (c0, n) in enumerate(chunks):
            bank = par * banks_per + ci
            evac(ci, c0, n, ps[bank][:, 0:n], [("ps", bank)])
        yield

    def project(*a, **k):
        for _ in project_gen(*a, **k):
            pass

    def rhs_u(kt, c0, n):
        return uT[:, kt, c0:c0 + n]

    def res_u(kt, c0, n):
        return uT_res(c0, n)

    evflip = [0]

    def copy_evac(dst_of, wres_of, scale=None):
        def f(ci, c0, n, pap, pres):
            dst = dst_of(c0, n)
            evflip[0] += 1
            if scale is not None or evflip[0] % 2 == 0:
                if scale is None:
                    S.add("act", lambda e: e.copy(out=dst, in_=pap), pres, wres_of(ci))
                else:
                    S.add("act", lambda e: e.activation(out=dst, in_=pap, func=AF.Copy, scale=scale), pres, wres_of(ci))
            else:
                S.add("dve", lambda e: e.tensor_copy(out=dst, in_=pap), pres, wres_of(ci))
        return f

    S.alias_after(["qT_att", "kT_att", "V_att", "vtmp"], ["xn", "gbc"])
    for i in range(2):
        project(P_ATT + i, CH3, rhs_u, res_u,
                copy_evac(lambda c0, n, i=i: kT_att[:, i, c0:c0 + n], lambda ci, i=i: [("kT_att", i, ci)]))
    S.add("dve", lambda e: e.memset(V_att[:, :, :, 64:65], 1.0), [], [("V_att", "ones")])
    for i in range(2):
        vs = i % 2
        project(P_ATT + 2 + i, CH3, rhs_u, res_u,
                copy_evac(lambda c0, n, vs=vs: vtmp[vs][:, c0:c0 + n], lambda ci, vs=vs: [("vtmp", vs, ci)]))
        for blk in range(10):
            r0, nr = blocks[blk]
            ci = 0 if r0 < 384 else (1 if r0 < 768 else 2)
            bank, col = (7, blk * 128) if blk < 8 else (6, (blk - 8) * 128)
            S.add("pe", lambda e, vs=vs, r0=r0, nr=nr, bank=bank, col=col: e.transpose(
                out=psb[bank][0:nr, col:col + 128], in_=vtmp[vs][:, r0:r0 + nr], identity=ident_b[:, :]),
                [("vtmp", vs, ci), ("ident_b",)], [("ps", bank)])
        S.add("act", lambda e, i=i: e.copy(out=V_att[:, 0:8, 2 * i:2 * i + 2, 0:64],
                                           in_=psb[7][:, 0:1024].rearrange("p (a b c) -> p a b c", a=8, b=2, c=64)),
              [("ps", 7)], [("V_att", blk, i) for blk in range(8)])
        S.add("dve", lambda e, i=i: e.tensor_copy(out=V_att[:, 8, 2 * i:2 * i + 2, 0:64],
                                                  in_=psb[6][:, 0:128].rearrange("p (a b) -> p a b", a=2, b=64)),
              [("ps", 6)], [("V_att", 8, i)])
        S.add("dve", lambda e, i=i: e.tensor_copy(out=V_att[0:16, 9, 2 * i:2 * i + 2, 0:64],
                                                  in_=psb[6][0:16, 128:256].rearrange("p (a b) -> p a b", a=2, b=64)),
              [("ps", 6)], [("V_att", 9, i)])
    S.add("dve", lambda e: e.tensor_scalar(out=V_att[:, 0, :, :], in0=V_att[:, 0, :, :], scalar1=smalls[:, 30:31],
                                           scalar2=None, op0=ALU.mult),
          [("V_att", 0, 0), ("V_att", 0, 1), ("V_att", "ones"), ("smalls",)], [("V_att", 0, 0), ("V_att", 0, 1), ("V_att", "ones")])
    for a in range(2):
        for b in range(4):
            t8 = a * 4 + b
            project(P_ATT + 4 + t8, CH3, rhs_u, res_u,
                    copy_evac(lambda c0, n, t8=t8: qT_att[:, t8, c0:c0 + n], lambda ci, t8=t8: [("qT_att", t8, ci)],
                              scale=0.125))

    S.alias_after(["btab", "PT", "att_tok"], ["vtmp", "xn", "gbc"])

    def chunk_of(c):
        return 0 if c < 384 else (1 if c < 768 else 2)

    def att_ctx(i):
        g, n = i // 8, i % 8
        a, eh = g // 2, g % 2
        p0 = eh * 64
        ts = g % 2
        tabs = btab[ts]
        qo = OWN0 + 128 * n
        kprev, kcur = 128 * n, 128 * (n + 1)
        pset = i % 2
        sb3 = [pset * 3 + k for k in range(3)]
        spec = [(kprev, 128, tabs[0]), (kcur, 128, tabs[1]), (META0, 16, tabs[2] if n == 0 else tabs[3])]
        return g, n, a, p0, ts, tabs, qo, pset, sb3, spec

    def att_front(i):
        g, n, a, p0, ts, tabs, qo, pset, sb3, spec = att_ctx(i)
        if n == 0:
            for k, npart in enumerate([128, 128, 16, 16]):
                dma("sp", tabs[k][0:npart, :, :], tz[k].ap()[4 * g:4 * g + 4, :, :].rearrange("h c r -> c h r"),
                    [("tz", k)], [("btab", ts, k)], ("bt", ts, k))
        qres = [("qT_att", a * 4 + b, chunk_of(qo)) for b in range(4)]
        for k, (k0, nk, tb) in enumerate(spec):
            bank = sb3[k]
            S.add("pe", lambda e, a=a, p0=p0, k0=k0, nk=nk, qo=qo, bank=bank: e.matmul(
                ps[bank][0:nk, :], lhsT=kT_att[p0:p0 + 64, a, k0:k0 + nk],
                rhs=qT_att[p0:p0 + 64, a * 4:a * 4 + 4, qo:qo + 128], start=True, stop=True),
                [("kT_att", a, chunk_of(k0))] + qres, [("ps", bank)])
            tk = k if k < 2 else (2 if n == 0 else 3)
            S.add("dve", lambda e, bank=bank, nk=nk, tb=tb: e.tensor_tensor(
                out=ps[bank][0:nk, :], in0=ps[bank][0:nk, :], in1=tb[0:nk, :, :].rearrange("p a b -> p (a b)"),
                op=ALU.add), [("ps", bank), ("btab", ts, tk)], [("ps", bank)])
            S.add("act", lambda e, bank=bank, nk=nk, pset=pset, k=k: e.activation(
                out=PTt[pset][k][0:nk, :], in_=ps[bank][0:nk, :], func=AF.Exp),
                [("ps", bank)], [("PT", pset, k)])

    def att_back(i):
        g, n, a, p0, ts, tabs, qo, pset, sb3, spec = att_ctx(i)
        vb = [n, n + 1, 9]
        for b in range(4):
            for k, (k0, nk, tb) in enumerate(spec):
                S.add("pe", lambda e, b=b, k=k, nk=nk, pset=pset, vbk=vb[k], g=g: e.matmul(
                    ps[6][:, b * 65:(b + 1) * 65], lhsT=PTt[pset][k][0:nk, b * 128:(b + 1) * 128],
                    rhs=V_att[0:nk, vbk, g, :], start=(k == 0), stop=(k == 2)),
                    [("PT", pset, k), ("V_att", vb[k], g // 2), ("V_att", "ones")], [("ps", 6)])
        o4 = ps[6][:, 0:260].rearrange("p (a b) -> p a b", a=4, b=65)
        dc = 8 + 4 * pset
        den = stat[:, dc:dc + 4]
        DK = ("stat", "den", pset)
        S.add("dve", lambda e, o4=o4, g=g, den=den: e.tensor_tensor(
            out=den, in0=o4[:, :, 64], in1=expsink[:, 4 * g:4 * g + 4], op=ALU.add),
            [("ps", 6), ("expsink",)], [DK])
        S.add("dve", lambda e, den=den: e.reciprocal(out=den, in_=den), [DK], [DK])
        at = att_tok[pset]
        S.add("dve", lambda e, o4=o4, at=at, dc=dc: e.tensor_tensor(
            out=at, in0=o4[:, :, 0:64],
            in1=bass.AP(tensor=stat, offset=dc, ap=[[64, 128], [1, 4], [0, 64]]), op=ALU.mult),
            [("ps", 6), DK], [("att_tok", pset)])
        for i2 in range(2):
            S.add("pe", lambda e, at=at, i2=i2: e.transpose(
                out=psb[7][:, i2 * 128:(i2 + 1) * 128],
                in_=at[:, 2 * i2:2 * i2 + 2, :].rearrange("p a b -> p (a b)"), identity=ident_b[:, :]),
                [("att_tok", pset), ("ident_b",)], [("ps", 7, i2)])
        S.add("act", lambda e, g=g, n=n: e.copy(
            out=mixT[:, 2 * g:2 * g + 2, n * 128:(n + 1) * 128],
            in_=psb[7][:, 0:256].rearrange("p (a b) -> p a b", a=2, b=128)),
            [("ps", 7, 0), ("ps", 7, 1)], [("mixT", 2 * g, n), ("mixT", 2 * g + 1, n)])

    if F_ATTPIPE:
        att_front(0)
        for i in range(32):
            if i + 1 < 32:
                att_front(i + 1)
            att_back(i)
    else:
        for i in range(32):
            att_front(i)
            att_back(i)

    S.alias_after(["qT_ml", "kT_ml", "V_ml", "vtmp", "cpre", "cacc"],
                  ["qT_att", "kT_att", "V_att", "btab", "PT", "att_tok", "xn", "gbc"])
    for s_ in range(2):
        S.add("dve", lambda e, s_=s_: e.memset(cpre[s_][:, 0:3], 0.0), [], [("cpre", s_, "z")])
    def convproj_gen(which, base, dstT, h, cs, ye=None):
        ctile = h if which == "q" else 4 + h
        yield from project_gen(base + h, CH3, rhs_u, res_u,
                               copy_evac(lambda c0, n, cs=cs: cpre[cs][:, 3 + c0:3 + c0 + n], lambda ci, cs=cs: [("cpre", cs, ci)]),
                               yield_every=ye, fixed_par=(0 if ye else None))
        cr = [("cpre", cs, ci) for ci in range(3)] + [("cpre", cs, "z"), ("cstT",)]
        wcol = lambda j: cstT[:, C_CW + j * 8 + ctile:C_CW + j * 8 + ctile + 1]
        S.add("dve", lambda e, cs=cs, w0=wcol(0), bb=cstT[:, C_CB + ctile:C_CB + ctile + 1]: e.tensor_scalar(
            out=cacc[cs][:, :], in0=cpre[cs][:, 0:T], scalar1=w0, scalar2=bb, op0=ALU.mult, op1=ALU.add),
            cr, [("cacc", cs)])
        for j in range(1, 4):
            S.add("dve", lambda e, cs=cs, j=j, wj=wcol(j): e.scalar_tensor_tensor(
                out=cacc[cs][:, :], in0=cpre[cs][:, j:j + T], scalar=wj, in1=cacc[cs][:, :], op0=ALU.mult, op1=ALU.add),
                cr + [("cacc", cs)], [("cacc", cs)])
        S.add("act", lambda e, cs=cs, h=h, dstT=dstT: e.activation(out=dstT[:, h, :], in_=cacc[cs][:, :], func=AF.Silu),
              [("cacc", cs)], [(which + "T_ml", h)])
        yield

    for h in range(4):
        for _ in convproj_gen("k", P_MK, kT_ml, h, h % 2):
            pass
    S.add("dve", lambda e: e.memset(V_ml[:, :, :, 256:257], 1.0), [], [("V_ml", "ones")])
    for i in range(8):
        vs = i % 2
        h, hf = i // 2, i % 2
        project(P_MV + i, CH3, rhs_u, res_u,
                copy_evac(lambda c0, n, vs=vs: vtmp[vs][:, c0:c0 + n], lambda ci, vs=vs: [("vtmp", vs, ci)]))
        for blk in range(9):
            r0 = blk * 128
            bank, col = (7, blk * 128) if blk < 8 else (6, 0)
            S.add("pe", lambda e, vs=vs, r0=r0, bank=bank, col=col: e.transpose(
                out=psb[bank][:, col:col + 128], in_=vtmp[vs][:, r0:r0 + 128], identity=ident_b[:, :]),
                [("vtmp", vs, chunk_of(r0)), ("ident_b",)], [("ps", bank)])
        S.add("act", lambda e, h=h, hf=hf: e.copy(out=V_ml[:, 0:8, h, hf * 128:(hf + 1) * 128],
                                                  in_=psb[7][:, 0:1024].rearrange("p (a b) -> p a b", a=8, b=128)),
              [("ps", 7)], [("V_ml", blk, h, hf) for blk in range(8)])
        S.add("dve", lambda e, h=h, hf=hf: e.tensor_copy(out=V_ml[:, 8, h, hf * 128:(hf + 1) * 128], in_=psb[6][:, 0:128]),
              [("ps", 6)], [("V_ml", 8, h, hf)])
    for blk in range(9):
        for kt in range(KT):
            S.add("pe", lambda e, blk=blk, kt=kt: e.matmul(
                ps[6][:, blk * 8:(blk + 1) * 8], lhsT=uT[:, kt, blk * 128:(blk + 1) * 128], rhs=wg[:, kt, :],
                start=(kt == 0), stop=(kt == KT - 1)),
                uT_res(blk * 128, 128) + [("wg",)], [("ps", 6)])
    G_GS, G_LI, G_LF, G_B, G_BT, G_W, G_DEC, G_A, G_BIAS, G_T1, G_T2, G_T3 = range(12)
    gp = ps[6][:, 0:72].rearrange("p (a b) -> p a b", a=9, b=8)
    bias_i = bass.AP(tensor=smalls, offset=16, ap=[[64, 128], [0, 9], [1, 4]])
    bias_f = bass.AP(tensor=smalls, offset=20, ap=[[64, 128], [0, 9], [1, 4]])
    valid_bc = bass.AP(tensor=smalls, offset=40, ap=[[64, 128], [1, 9], [0, 4]])
    vneg_bc = bass.AP(tensor=smalls, offset=49, ap=[[64, 128], [1, 9], [0, 4]])
    GR = [("gts", i) for i in range(12)]
    S.add("dve", lambda e: e.tensor_tensor(out=gts[:, G_T1], in0=gp[:, :, 0:4], in1=bias_i, op=ALU.add),
          [("ps", 6), ("smalls",)], [("gts", G_T1)])
    S.add("dve", lambda e: e.tensor_tensor(out=gts[:, G_T2], in0=gp[:, :, 4:8], in1=bias_f, op=ALU.add),
          [("ps", 6), ("smalls",)], [("gts", G_T2)])
    S.add("act", lambda e: e.activation(out=gts[:, G_T1], in_=gts[:, G_T1], func=AF.Tanh, scale=1.0 / 15.0),
          [("gts", G_T1)], [("gts", G_T1)])
    S.add("act", lambda e: e.activation(out=gts[:, G_T2], in_=gts[:, G_T2], func=AF.Tanh, scale=1.0 / 15.0),
          [("gts", G_T2)], [("gts", G_T2)])
    S.add("dve", lambda e: e.scalar_tensor_tensor(out=gts[:, G_LI], in0=gts[:, G_T1], scalar=15.0, in1=valid_bc,
                                                  op0=ALU.mult, op1=ALU.mult), [("gts", G_T1), ("smalls",)], [("gts", G_LI)])
    S.add("dve", lambda e: e.tensor_tensor(out=gts[:, G_LI], in0=gts[:, G_LI], in1=vneg_bc, op=ALU.add),
          [("gts", G_LI), ("smalls",)], [("gts", G_LI)])
    S.add("act", lambda e: e.activation(out=gts[:, G_T3], in_=gts[:, G_T2], func=AF.Exp, scale=-15.0),
          [("gts", G_T2)], [("gts", G_T3)])
    S.add("act", lambda e: e.activation(out=gts[:, G_T3], in_=gts[:, G_T3], func=AF.Ln, bias=1.0),
          [("gts", G_T3)], [("gts", G_T3)])
    S.add("dve", lambda e: e.scalar_tensor_tensor(out=gts[:, G_LF], in0=gts[:, G_T3], scalar=-1.0, in1=valid_bc,
                                                  op0=ALU.mult, op1=ALU.mult), [("gts", G_T3), ("smalls",)], [("gts", G_LF)])
    for blk in range(9):
        S.add("pe", lambda e, blk=blk: e.matmul(ps[7][:, blk * 4:(blk + 1) * 4], lhsT=U_f[:, :], rhs=gts[:, G_LF, blk, :],
                                                start=True, stop=True), [("gts", G_LF), ("U_f",)], [("ps", 7)])
        S.add("pe", lambda e, blk=blk: e.matmul(ps[7][:, 64 + blk * 4:64 + (blk + 1) * 4], lhsT=ones_f[:, :],
                                                rhs=gts[:, G_LF, blk, :], start=True, stop=True),
              [("gts", G_LF), ("ones_f",)], [("ps", 7)])
    pb_ = ps[7][:, 0:36].rearrange("p (a b) -> p a b", a=9, b=4)
    pbt = ps[7][:, 64:100].rearrange("p (a b) -> p a b", a=9, b=4)
    S.add("dve", lambda e: e.tensor_copy(out=gts[:, G_B], in_=pb_), [("ps", 7)], [("gts", G_B)])
    S.add("dve", lambda e: e.tensor_copy(out=gts[:, G_BT], in_=pbt), [("ps", 7)], [("gts", G_BT)])
    S.add("dve", lambda e: e.tensor_tensor(out=gts[:, G_BIAS], in0=gts[:, G_LI], in1=gts[:, G_B], op=ALU.subtract),
          [("gts", G_LI), ("gts", G_B)], [("gts", G_BIAS)])
    S.add("dve", lambda e: e.tensor_tensor(out=gts[:, G_W], in0=gts[:, G_BIAS], in1=gts[:, G_BT], op=ALU.add),
          [("gts", G_BIAS), ("gts", G_BT)], [("gts", G_W)])
    S.add("act", lambda e: e.activation(out=gts[:, G_W], in_=gts[:, G_W], func=AF.Exp), [("gts", G_W)], [("gts", G_W)])
    S.add("act", lambda e: e.activation(out=gts[:, G_DEC], in_=gts[:, G_BT], func=AF.Exp), [("gts", G_BT)], [("gts", G_DEC)])
    S.add("dve", lambda e: e.tensor_scalar(out=gts[:, G_A], in0=gts[:, G_B], scalar1=float(np.log(128.0 ** -0.5)),
                                           scalar2=None, op0=ALU.add), [("gts", G_B)], [("gts", G_A)])
    S.add("act", lambda e: e.activation(out=gts[:, G_A], in_=gts[:, G_A], func=AF.Exp), [("gts", G_A)], [("gts", G_A)])
    brow = nc.dram_tensor("brow", [36, 128], F32)
    S.add("pe", lambda e: e.transpose(out=ps[7][0:36, 128:256], in_=gts[:, G_B].rearrange("p a b -> p (a b)"),
                                      identity=ident_f[:, :]), [("gts", G_B), ("ident_f",)], [("ps", 7)])
    S.add("dve", lambda e: e.tensor_copy(out=brow_sb[:, :], in_=ps[7][0:36, 128:256]), [("ps", 7)], [("brow_sb",)])
    dma("sp", brow.ap()[:, :], brow_sb[:, :], [("brow_sb",)], [("brow",)], "brw")

    S.alias_after(["kw", "junk", "og_tmp"], ["vtmp", "cpre", "cacc", "xn", "gbc", "btab", "PT", "att_tok"])
    Ulf = [aview(R2 + i * 512, [128, 128], F32) for i in range(4)]
    Dm = [aview(R2 + 2048 + i * 2048, [128, 4, 128], F32) for i in range(2)]
    Et = [aview(R2 + 6144 + i * 2048, [128, 4, 128], F32) for i in range(2)]
    STt = [aview(R2 + 10240 + i * 1024, [128, 4, 128], BF16) for i in range(2)]
    At = [aview(R2 + 12288 + i * 2048, [128, 4, 128], F32) for i in range(2)]
    qa = [aview(R2 + 16384 + i * 1024, [128, 4, 128], BF16) for i in range(2)]
    hmn = [aview(R2 + 18432 + i * 2048, [128, 4, 256], BF16) for i in range(2)]
    kw4 = [aview(R2 + 23224 + i * 1024, [128, 4, 128], BF16) for i in range(2)]
    junk = aview(R2 + 25272, [128, 256], F32)
    og_tmp = [aview(R2 + 26296 + i * 1024, [128, 512], BF16) for i in range(2)]
    Bbs = [aview(R2 + 0, [128, 4, 128], F32), Cstb.bitcast(F32)[:, :, :].rearrange("p a b -> p (a b)")[:, 0:512].rearrange("p (a b) -> p a b", a=4, b=128)]
    snapb = [xin[i].bitcast(BF16)[:, 0:4112].rearrange("p (a b c) -> p a b c", a=4, b=4, c=257) for i in range(2)]

    def snap(sn):
        return snapb[sn // 4][:, sn % 4]

    ogf = [0]

    def mo_gen():
        for i in range(8):
            def ev(ci, c0, n, pap, pres, i=i):
                os_ = ogf[0] % 2
                ogf[0] += 1
                S.add("act", lambda e: e.activation(out=og_tmp[os_][:, :], in_=pap, func=AF.Sigmoid), pres, [("og_tmp", os_)])
                o0 = c0 - OWN0
                S.add("dve", lambda e: e.tensor_scalar(out=ogm[:, i, o0:o0 + 512], in0=og_tmp[os_][:, :],
                                                       scalar1=cstT[:, C_MH + i:C_MH + i + 1], scalar2=None, op0=ALU.mult),
                      [("og_tmp", os_), ("cstT",)], [("ogm", i, o0 // 512)])
            yield from project_gen(P_MO + i, OWNCH, rhs_u, res_u, ev, banks_per=2, fixed_par=0, yield_every=8)

    if not F_QFILL:
        for h in range(4):
            for _ in convproj_gen("q", P_MQ, qT_ml, h, 0):
                pass

    def filler_gen():
        if F_QFILL:
            for h in range(4):
                yield from convproj_gen("q", P_MQ, qT_ml, h, 0, ye=8)
        yield from mo_gen()

    mo_it = filler_gen()

    def mo_step(k=1):
        for _ in range(k):
            try:
                next(mo_it)
            except StopIteration:
                return

    def state_step(blk, first, do_snap):
        ks = blk % 2
        c0 = blk * 128
        for h in range(4):
            S.add("pe", lambda e, h=h, c0=c0: e.transpose(out=psb[7][:, h * 128:(h + 1) * 128], in_=kT_ml[:, h, c0:c0 + 128],
                                                          identity=ident_b[:, :]), [("kT_ml", h), ("ident_b",)], [("ps", 7)])
        wbc = bass.AP(tensor=gts, offset=(G_W * 9 + blk) * 4, ap=[[12 * 9 * 4, 128], [1, 4], [0, 128]])
        S.add("dve", lambda e, ks=ks, wbc=wbc: e.tensor_tensor(
            out=kw4[ks], in0=psb[7][:, 0:512].rearrange("p (a b) -> p a b", a=4, b=128), in1=wbc, op=ALU.mult),
            [("ps", 7), ("gts", G_W)], [("kw", ks)])
        mo_step()
        for h in range(4):
            S.add("pe", lambda e, ks=ks, blk=blk, h=h: e.matmul(ps[3 + h][:, 0:257], lhsT=kw4[ks][:, h, :], rhs=V_ml[:, blk, h, :],
                                                               start=True, stop=True),
                  [("kw", ks), ("V_ml", blk, h, 0), ("V_ml", blk, h, 1), ("V_ml", "ones")], [("ps", 3 + h)])
        for h in range(4):
            if first:
                S.add("dve", lambda e, h=h: e.tensor_copy(out=Cst[:, h, 0:257], in_=ps[3 + h][:, 0:257]), [("ps", 3 + h)], [("Cst", h)])
            else:
                S.add("dve", lambda e, h=h, blk=blk: e.scalar_tensor_tensor(
                    out=Cst[:, h, 0:257], in0=Cst[:, h, 0:257], scalar=gts[:, G_DEC, blk, h:h + 1], in1=ps[3 + h][:, 0:257],
                    op0=ALU.mult, op1=ALU.add), [("ps", 3 + h), ("Cst", h), ("gts", G_DEC)], [("Cst", h)])
        if do_snap:
            S.add("act", lambda e, blk=blk: e.copy(out=snap(blk), in_=Cst[:, :, 0:257]), [("Cst", h) for h in range(4)], [("snap", blk)])
        mo_step()

    for blk in range(9):
        state_step(blk, first=(blk == 0), do_snap=False)
    S.add("dve", lambda e: e.tensor_reduce(out=stat[:, 16:20], in_=gts[:, G_BT].rearrange("p a b -> p b a"), axis=AX.X, op=ALU.add),
          [("gts", G_BT)], [("stat", "bt")])
    for h in range(4):
        S.add("dve", lambda e, h=h: e.tensor_copy(out=Cst[:, h, 257:258], in_=stat[:, 16 + h:17 + h]),
              [("stat", "bt"), ("Cst", h)], [("Cst", h)])
    dma("sp", sum_in.ap().rearrange("(h p) c -> p h c", p=128), Cst[:, :, :], [("Cst", h) for h in range(4)], [("sum_in",)], "sm0")
    S.add("pool", lambda e: e.collective_compute("AllGather", ALU.bypass, replica_groups=[[0, 1, 2, 3], [4, 5, 6, 7]],
                                                 ins=[sum_in.ap().opt()], outs=[sum_out.ap().opt()]),
          [("sum_in",)], [("sum_out",)], kind="cc", semkey="cc", inc=1)
    S.alias_after(["Cg"], ["xin"])
    for i2 in range(2):
        src = bass.AP(tensor=sum_out, offset=i2 * 2 * 512 * 260, ap=[[260, 128], [512 * 260, 2], [128 * 260, 4], [1, 258]])
        dma("sp", Cg[i2], src, [("sum_out",)], [("Cg", i2)], ("cg", i2))
    mo_step(42)

    def cgv(i):
        return Cg[i // 2][:, i % 2]
    ex = coef[:, 0:16].rearrange("p (a b) -> p a b", a=4, b=4)
    cf = coef[:, 16:32].rearrange("p (a b) -> p a b", a=4, b=4)
    S.add("dve", lambda e: e.memset(coef[:, 0:16], 0.0), [], [("coef",)])
    M01, M02, M12 = smalls[:, 24:25], smalls[:, 25:26], smalls[:, 26:27]
    CGR = [("Cg", 0), ("Cg", 1), ("smalls",)]
    S.add("dve", lambda e: e.tensor_scalar(out=ex[:, 0, :], in0=cgv(1)[:, :, 257], scalar1=M01, scalar2=None, op0=ALU.mult),
          CGR + [("coef",)], [("coef",)])
    S.add("dve", lambda e: e.scalar_tensor_tensor(out=ex[:, 0, :], in0=cgv(2)[:, :, 257], scalar=M02, in1=ex[:, 0, :],
                                                  op0=ALU.mult, op1=ALU.add), CGR + [("coef",)], [("coef",)])
    S.add("dve", lambda e: e.tensor_scalar(out=ex[:, 1, :], in0=cgv(2)[:, :, 257], scalar1=M12, scalar2=None, op0=ALU.mult),
          CGR + [("coef",)], [("coef",)])
    S.add("act", lambda e: e.activation(out=coef[:, 16:32], in_=coef[:, 0:16], func=AF.Exp), [("coef",)], [("coef", "c")])
    for i in range(3):
        S.add("dve", lambda e, i=i: e.tensor_scalar(out=cf[:, i, :], in0=cf[:, i, :], scalar1=smalls[:, 27 + i:28 + i],
                                                    scalar2=None, op0=ALU.mult), [("coef", "c"), ("smalls",)], [("coef", "c")])
    for h in range(4):
        S.add("dve", lambda e, h=h: e.tensor_scalar(out=Cst[:, h, 0:257], in0=cgv(0)[:, h, 0:257], scalar1=cf[:, 0, h:h + 1],
                                                    scalar2=None, op0=ALU.mult), CGR + [("coef", "c"), ("Cst", h)], [("Cst", h)])
        for i in range(1, 3):
            S.add("dve", lambda e, h=h, i=i: e.scalar_tensor_tensor(
                out=Cst[:, h, 0:257], in0=cgv(i)[:, h, 0:257], scalar=cf[:, i, h:h + 1], in1=Cst[:, h, 0:257],
                op0=ALU.mult, op1=ALU.add), CGR + [("coef", "c"), ("Cst", h)], [("Cst", h)])

    S.alias_after(["snap"], ["Cg", "xin"])
    for blk in range(8):
        state_step(blk, first=False, do_snap=True)
    mo_step(1000)
    uTf = uT.bitcast(F32)
    S.alias_after(["xpre"], ["uT"])
    xpre = [uTf[:, 4 * n:4 * n + 4, 0:512] for n in range(4)]
    for n in range(4):
        dma("sp", xpre[n], x_in[OWN0 + n * 128:OWN0 + (n + 1) * 128, :].rearrange("p (a b) -> p a b", a=4, b=512),
            [], [("xpre", n)], ("xp", n))

    S.alias_after(["Bbs", "Ulf", "Dm", "E", "ST", "At", "qa", "hmn"], ["vtmp", "cpre", "cacc", "xn", "gbc", "btab", "PT", "att_tok"])
    SCALE_ML = 128.0 ** -0.5
    LNS = float(np.log(SCALE_ML))
    maskneg_bc = bass.AP(tensor=maskneg, offset=0, ap=[[128, 128], [0, 4], [1, 128]])
    def ml_front(blk):
        c0 = blk * 128
        es_ = blk % 2
        bkq, bbb = es_, 2 + es_
        for h in range(4):
            S.add("pe", lambda e, h=h, c0=c0, bkq=bkq: e.matmul(ps[bkq][:, h * 128:(h + 1) * 128], lhsT=kT_ml[:, h, c0:c0 + 128],
                                                                rhs=qT_ml[:, h, c0:c0 + 128], start=True, stop=True),
                  [("kT_ml", h), ("qT_ml", h)], [("ps", bkq)])
        Bb = Bbs[es_]

        def bload(b_):
            dma("sp", Bbs[b_ % 2], bass.AP(tensor=brow, offset=b_ * 4 * 128, ap=[[0, 128], [128, 4], [1, 128]]),
                [("brow",)], [("Bbs", b_ % 2)], ("bb", b_ % 2))
        if blk == 1:
            bload(1)
        if blk + 1 <= 8:
            bload(blk + 1)
        pK = ps[bkq][:, :].rearrange("p (a b) -> p a b", a=4, b=128)
        S.add("act", lambda e, es_=es_, Bb=Bb: e.activation(out=At[es_], in_=Bb, func=AF.Exp, bias=LNS), [("Bbs", es_)], [("At", es_)])
        S.add("dve", lambda e, es_=es_, Bb=Bb: e.tensor_tensor(out=Dm[es_], in0=Bb, in1=maskneg_bc, op=ALU.add),
              [("Bbs", es_), ("maskneg",)], [("Dm", es_)])
        for h in range(4):
            S.add("act", lambda e, h=h, blk=blk, es_=es_: e.activation(
                out=Et[es_][:, h, :], in_=Dm[es_][:, h, :], func=AF.Exp, bias=gts[:, G_BIAS, blk, h:h + 1]),
                [("Dm", es_), ("gts", G_BIAS)], [("E", es_, h)])
        S.add("dve", lambda e, es_=es_, pK=pK: e.scalar_tensor_tensor(out=STt[es_], in0=pK, scalar=SCALE_ML, in1=Et[es_],
                                                                      op0=ALU.mult, op1=ALU.mult),
              [("ps", bkq)] + [("E", es_, h) for h in range(4)], [("ST", es_)])
        S.add("dve", lambda e, es_=es_, c0=c0: e.tensor_tensor(out=qa[es_], in0=qT_ml[:, :, c0:c0 + 128], in1=At[es_], op=ALU.mult),
              [("qT_ml", h) for h in range(4)] + [("At", es_)], [("qa", es_)])

    def ml_back(blk):
        n = blk - 1
        es_ = blk % 2
        sn = snap(blk - 1)
        for h in range(4):
            bn = 4 + h // 2
            cN = (h % 2) * 256
            S.add("pe", lambda e, h=h, es_=es_, blk=blk, bn=bn, cN=cN: e.matmul(
                ps[bn][:, cN:cN + 256], lhsT=STt[es_][:, h, :], rhs=V_ml[:, blk, h, 0:256], start=True, stop=False),
                [("ST", es_), ("V_ml", blk, h, 0), ("V_ml", blk, h, 1)], [("ps", bn)])
            S.add("pe", lambda e, h=h, es_=es_, bn=bn, cN=cN, sn=sn: e.matmul(
                ps[bn][:, cN:cN + 256], lhsT=qa[es_][:, h, :], rhs=sn[:, h, 0:256], start=False, stop=True),
                [("qa", es_), ("snap", blk - 1)], [("ps", bn)])
            S.add("pe", lambda e, h=h, es_=es_: e.matmul(ps[6][:, h:h + 1], lhsT=STt[es_][:, h, :], rhs=ones_b[:, 0:1],
                                                         start=True, stop=False), [("ST", es_), ("ones_b",)], [("ps", 6)])
            S.add("pe", lambda e, h=h, es_=es_, sn=sn: e.matmul(ps[6][:, h:h + 1], lhsT=qa[es_][:, h, :], rhs=sn[:, h, 256:257],
                                                                start=False, stop=True), [("qa", es_), ("snap", blk - 1)], [("ps", 6)])
        sd = st2[:, es_ * 16:(es_ + 1) * 16]
        SK = ("st2", es_)
        S.add("act", lambda e, sd=sd: e.copy(out=sd[:, 0:4], in_=ps[6][:, 0:4]), [("ps", 6)], [SK])
        S.add("dve", lambda e, sd=sd: e.scalar_tensor_tensor(out=sd[:, 0:4], in0=sd[:, 0:4], scalar=-1.0, in1=sd[:, 0:4],
                                                             op0=ALU.mult, op1=ALU.max), [SK], [SK])
        S.add("dve", lambda e, sd=sd: e.tensor_scalar(out=sd[:, 0:4], in0=sd[:, 0:4], scalar1=1.0, scalar2=None, op0=ALU.max), [SK], [SK])
        S.add("dve", lambda e, sd=sd: e.reciprocal(out=sd[:, 0:4], in_=sd[:, 0:4]), [SK], [SK])
        for h in range(4):
            bn = 4 + h // 2
            cN = (h % 2) * 256
            S.add("act", lambda e, h=h, bn=bn, cN=cN, sd=sd: e.activation(out=junk[:, :], in_=ps[bn][:, cN:cN + 256], func=AF.Square,
                                                                          accum_out=sd[:, 4 + h:5 + h]),
                  [("ps", bn)], [("junk",), ("st2s", es_, h)])
        SS = [("st2s", es_, h) for h in range(4)]
        S.add("dve", lambda e, sd=sd: e.tensor_tensor(out=sd[:, 8:12], in0=sd[:, 0:4], in1=sd[:, 0:4], op=ALU.mult), [SK], [("st2b", es_)])
        S.add("dve", lambda e, sd=sd: e.tensor_tensor(out=sd[:, 8:12], in0=sd[:, 8:12], in1=sd[:, 4:8], op=ALU.mult),
              [("st2b", es_)] + SS, [("st2b", es_)])
        S.add("act", lambda e, sd=sd: e.activation(out=sd[:, 8:12], in_=sd[:, 8:12], func=AF.Sqrt, scale=1.0 / 256.0, bias=EPS),
              [("st2b", es_)], [("st2b", es_)])
        S.add("dve", lambda e, sd=sd: e.reciprocal(out=sd[:, 8:12], in_=sd[:, 8:12]), [("st2b", es_)], [("st2b", es_)])
        S.add("dve", lambda e, sd=sd: e.tensor_tensor(out=sd[:, 12:16], in0=sd[:, 8:12], in1=sd[:, 0:4], op=ALU.mult),
              [("st2b", es_), SK], [("st2c", es_)])
        for hh in range(2):
            fbc = bass.AP(tensor=st2, offset=es_ * 16 + 12 + 2 * hh, ap=[[32, 128], [1, 2], [0, 256]])
            S.add("dve", lambda e, hh=hh, es_=es_, fbc=fbc: e.tensor_tensor(
                out=hmn[es_][:, 2 * hh:2 * hh + 2, :], in0=ps[4 + hh][:, :].rearrange("p (a b) -> p a b", a=2, b=256), in1=fbc,
                op=ALU.mult), [("ps", 4 + hh), ("st2c", es_)], [("hmn", es_, hh)])
        for h in range(4):
            for i2 in range(2):
                S.add("pe", lambda e, h=h, i2=i2, es_=es_: e.transpose(
                    out=psb[7][:, (2 * h + i2) * 128:(2 * h + i2 + 1) * 128], in_=hmn[es_][:, h, i2 * 128:(i2 + 1) * 128],
                    identity=ident_b[:, :]), [("hmn", es_, h // 2), ("ident_b",)], [("ps", 7)])
        S.add("dve", lambda e, n=n: e.tensor_tensor(
            out=mixT[:, 8:16, n * 128:(n + 1) * 128], in0=psb[7][:, :].rearrange("p (a b) -> p a b", a=8, b=128),
            in1=ogm[:, :, n * 128:(n + 1) * 128], op=ALU.mult),
            [("ps", 7)] + [("ogm", i, n // 4) for i in range(8)], [("mixT", 8 + i, n) for i in range(8)])

    if F_MLPIPE:
        ml_front(1)
        for blk in range(1, 9):
            if blk + 1 <= 8:
                ml_front(blk + 1)
            ml_back(blk)
    else:
        for blk in range(1, 9):
            ml_front(blk)
            ml_back(blk)

    S.alias_after(["hT", "xin"], ARENA_MIX + ["Cg"])
    def xload(n):
        dma("sp", xin[n % 2][:, 0:2048], x_in[OWN0 + n * 128:OWN0 + (n + 1) * 128, :], [], [("xin", n % 2)], ("x", n % 2))
    xload(4)
    xload(5)
    S.alias_after(["xog"], ["ogm", "kpre", "kacc"])
    ogf32 = ogm.bitcast(F32)
    xog = [ogf32[:, 4 * i:4 * i + 4, :] for i in range(2)]
    for i in range(2):
        dma("sp", xog[i], x_in[OWN0 + (6 + i) * 128:OWN0 + (7 + i) * 128, :].rearrange("p (a b) -> p a b", a=4, b=512),
            [], [("xog", i)], ("xg", i))
    xacc_i = [0]

    def xacc(n, q4):
        bank = 4 + (xacc_i[0] % 4)
        xacc_i[0] += 1
        for k4 in range(4):
            kt = q4 * 4 + k4
            if n < 4:
                xsrc, xres = xpre[n][:, q4, k4 * 128:(k4 + 1) * 128], ("xpre", n)
            elif n < 6:
                xsrc, xres = xin[n % 2][:, kt * 128:(kt + 1) * 128], ("xin", n % 2)
            else:
                xsrc, xres = xog[n - 6][:, q4, k4 * 128:(k4 + 1) * 128], ("xog", n - 6)
            S.add("pe", lambda e, xsrc=xsrc, k4=k4, bank=bank: e.transpose(
                out=ps[bank][:, k4 * 128:(k4 + 1) * 128], in_=xsrc, identity=ident_f[:, :]),
                [xres, ("ident_f",)], [("ps", bank, k4)])
        src = ps[bank][:, :].rearrange("p (a b) -> p a b", a=4, b=128)
        dst = hT[:, q4 * 4:q4 * 4 + 4, n * 128:(n + 1) * 128]
        hk = [("hT", q4 * 4 + k4, n) for k4 in range(4)]
        S.add("dve", lambda e, src=src, dst=dst: e.tensor_tensor(out=dst, in0=src, in1=dst, op=ALU.add),
              [("ps", bank, k4) for k4 in range(4)] + hk, hk)

    cpf = [0]

    def copy_hT(ct):
        def f(ci, c0, n, pap, pres):
            hres = [("hT", ct, c0 // 128 + k) for k in range(4)]
            cpf[0] += 1
            if cpf[0] % 2 == 0:
                S.add("act", lambda e: e.copy(out=hT[:, ct, c0:c0 + n], in_=pap), pres, hres)
            else:
                S.add("dve", lambda e: e.tensor_copy(out=hT[:, ct, c0:c0 + n], in_=pap), pres, hres)
        return f

    def acc_evac(ct):
        def f(ci, c0, n, pap, pres):
            hres = [("hT", ct, c0 // 128 + k) for k in range(4)]
            S.add("dve", lambda e: e.tensor_tensor(out=hT[:, ct, c0:c0 + n], in0=pap, in1=hT[:, ct, c0:c0 + n], op=ALU.add),
                  pres + hres, hres)
        return f

    def rhs_mix(kt, c0, n):
        return mixT[:, kt, c0:c0 + n]

    def res_mix(kt, c0, n):
        return [("mixT", kt, c0 // 128 + k) for k in range(4)]

    for g4 in range(4):
        for ct in range(4 * g4, 4 * g4 + 4):
            project(P_OUT + ct, CH2, rhs_mix, res_mix, copy_hT(ct), banks_per=2)
        for n in range(8):
            xacc(n, g4)

    def fm_rstd(ci, c0):
        for kt in range(KT):
            hres = [("hT", kt, c0 // 128 + k) for k in range(4)]
            sres = [("mixT", kt, c0 // 128 + k) for k in range(4)]
            S.add("act", lambda e, kt=kt: e.activation(out=sqv[:, kt, c0:c0 + 512], in_=hT[:, kt, c0:c0 + 512], func=AF.Square),
                  hres, sres)
            S.add("pe", lambda e, kt=kt: e.matmul(ps[6][:, :], lhsT=ones_b[:, :], rhs=sqv[:, kt, c0:c0 + 512],
                                                  start=(kt == 0), stop=(kt == KT - 1)), sres + [("ones_b",)], [("ps", 6)])
        S.add("act", lambda e: e.activation(out=rstd_bc[ci][:, :], in_=ps[6][:, :], func=AF.Sqrt, scale=1.0 / D, bias=EPS),
              [("ps", 6)], [("rstd", ci)])
        S.add("dve", lambda e: e.reciprocal(out=rstd_bc[ci][:, :], in_=rstd_bc[ci][:, :]), [("rstd", ci)], [("rstd", ci)])

    S.alias_after(["uT"], ["xpre"])
    for ci, (c0, n) in enumerate(CH2):
        fm_rstd(ci, c0)
        for kt in range(KT):
            hres = [("hT", kt, c0 // 128 + k) for k in range(4)]
            ures = [("uT", kt // 4, c0 // 128 + k) for k in range(4)]
            S.add("dve", lambda e, kt=kt, c0=c0, ci=ci: e.scalar_tensor_tensor(
                out=uT[:, kt, c0:c0 + 512], in0=hT[:, kt, c0:c0 + 512], scalar=cstT[:, C_NMLP + kt:C_NMLP + kt + 1],
                in1=rstd_bc[ci][:, :], op0=ALU.mult, op1=ALU.mult), hres + [("rstd", ci), ("cstT",)], ures)

    relu_t = rstd_bc
    rf = [0]
    pidx = P_FFN
    for fc in range(4):
        for ft in range(16):
            def ev(ci, c0, n, pap, pres, ft=ft):
                rs_ = rf[0] % 2
                rf[0] += 1
                S.add("act", lambda e: e.activation(out=relu_t[rs_][:, :], in_=pap, func=AF.Relu), pres, [("rstd", rs_)])
                S.add("dve", lambda e: e.tensor_tensor(out=mixT[:, ft, c0:c0 + n], in0=relu_t[rs_][:, :], in1=relu_t[rs_][:, :],
                                                       op=ALU.mult), [("rstd", rs_)], [("mixT", ft, c0 // 128 + k) for k in range(4)])
            project(pidx, CH2, rhs_u, res_u, ev, banks_per=2)
            pidx += 1
        for ct in range(16):
            project(pidx, CH2, rhs_mix, res_mix, acc_evac(ct), banks_per=2)
            pidx += 1

    mixb = mixT.bitcast(F32)
    for ci, (c0, n) in enumerate(CH2):
        fm_rstd(ci, c0)
    S.alias_after(["oT"], ["xin", "Cg"])
    S.alias_after(["ostage"], ["mixT"])
    for n in range(8):
        ci, lc = n // 4, (n % 4) * 128
        s = n % 2
        oTt = xin[s][:, 0:2048]
        for kt in range(KT):
            S.add("dve", lambda e, kt=kt, n=n, ci=ci, lc=lc, oTt=oTt: e.scalar_tensor_tensor(
                out=oTt[:, kt * 128:(kt + 1) * 128], in0=hT[:, kt, n * 128:(n + 1) * 128],
                scalar=cstT[:, C_NFIN + kt:C_NFIN + kt + 1], in1=rstd_bc[ci][:, lc:lc + 128], op0=ALU.mult, op1=ALU.mult),
                [("hT", kt, n), ("rstd", ci), ("cstT",)], [("oT", s, kt)])
        for q4 in range(4):
            bank = 4 + (q4 % 2)
            for k4 in range(4):
                kt = q4 * 4 + k4
                S.add("pe", lambda e, kt=kt, k4=k4, bank=bank, oTt=oTt: e.transpose(
                    out=ps[bank][:, k4 * 128:(k4 + 1) * 128], in_=oTt[:, kt * 128:(kt + 1) * 128], identity=ident_f[:, :]),
                    [("oT", s, kt), ("ident_f",)], [("ps", bank, k4)])
            dst = mixb[:, s * 4 + q4, :]
            rd = [("ps", bank, k4) for k4 in range(4)]
            if q4 % 2 == 0:
                S.add("act", lambda e, dst=dst, bank=bank: e.copy(out=dst, in_=ps[bank][:, :]), rd, [("ostage", s, q4)])
            else:
                S.add("dve", lambda e, dst=dst, bank=bank: e.tensor_copy(out=dst, in_=ps[bank][:, :]), rd, [("ostage", s, q4)])
        dma("sp", out_d[n * 128:(n + 1) * 128, :], mixb[:, s * 4:(s + 1) * 4, :].rearrange("p a b -> p (a b)"),
            [("ostage", s, q4) for q4 in range(4)], [("outd", n)], ("o", s))
    S.emit(nc, es)
    es.close()
    return nc


def _t5_bucket(d):
    d = np.maximum(d, 0)
    ratio = (np.maximum(d, 16).astype(np.float32) / np.float32(16)).astype(np.float32)
    large = 16 + (np.log(ratio).astype(np.float32) / np.float32(np.log(128 / 16)) * np.float32(16)).astype(np.int32)
    large = np.minimum(large, 31)
    return np.where(d < 16, d, large)


def _ohc(j):
    oh = np.zeros((33, NCV), np.float32)
    e = np.arange(383)
    d = e - 127
    valid = (d >= 0) & (d < 128)
    bk = _t5_bucket(d)
    for i in range(383):
        if valid[i]:
            oh[bk[i], i] = 1.0
        else:
            oh[32, i] = NEGM
    ee = np.arange(143)
    if j == 0:
        bk = _t5_bucket(ee + 1)
    else:
        bk = np.full(143, 31)
    for i in range(143):
        oh[bk[i], 383 + i] = 1.0
        oh[31, 526 + i] = 1.0
    return oh


_NC_CACHE = {}


def kernel(x, meta_tokens, w_in, conv_w, conv_b, b_igate, b_fgate, attn_sinks, rel_bias, mh_norm, w_out,
           norm_mix, norm_mlp, w_up, w_down, norm_final):
    f = lambda a: np.ascontiguousarray(np.asarray(a, dtype=np.float32))
    x, meta_tokens, w_in, w_out, w_up, w_down = f(x), f(meta_tokens), f(w_in), f(w_out), f(w_up), f(w_down)
    rows96 = np.concatenate([f(norm_mix).reshape(16, 128), f(norm_mlp).reshape(16, 128), f(norm_final).reshape(16, 128),
                             f(mh_norm).reshape(8, 128), f(conv_b).reshape(8, 128), f(conv_w).reshape(32, 128)], axis=0)
    rel33 = np.concatenate([f(rel_bias), np.ones((1, 16), np.float32)], axis=0)
    lead = np.concatenate([np.zeros((112, D), np.float32), meta_tokens], axis=0)
    in_maps = []
    for c in range(8):
        b, j = c // 4, c % 4
        prev = lead if j == 0 else x[b, 1024 * j - 128:1024 * j]
        xin = np.concatenate([prev, x[b, 1024 * j:1024 * (j + 1)], meta_tokens], axis=0)
        small = np.zeros((128, 64), np.float32)
        small[:, 0:16] = f(attn_sinks).reshape(1, 16)
        small[:, 16:20] = f(b_igate).reshape(1, 4)
        small[:, 20:24] = f(b_fgate).reshape(1, 4)
        small[:, 24] = 1.0 if j > 1 else 0.0
        small[:, 25] = 1.0 if j > 2 else 0.0
        small[:, 26] = 1.0 if j > 2 else 0.0
        for i in range(3):
            small[:, 27 + i] = 1.0 if i < j else 0.0
        small[:, 30] = 0.0 if j == 0 else 1.0
        valid = np.ones((128, 9), np.float32)
        if j == 0:
            valid[:112, 0] = 0.0
        else:
            valid[:, 0] = 0.0
        small[:, 40:49] = valid
        small[:, 49:58] = (valid - 1.0) * (-NEGM)
        in_maps.append({"xin": np.ascontiguousarray(xin), "w_in": w_in, "w_out": w_out, "w_up": w_up, "w_down": w_down,
                        "rows96": rows96, "rel33": rel33, "ohc": _ohc(j), "small": small})
    if "nc" not in _NC_CACHE:
        _NC_CACHE["nc"] = build_program()
    res = run_bass_kernel_spmd(_NC_CACHE["nc"], in_maps, core_ids=list(range(8)))
    out = np.zeros((2, 4096, D), np.float32)
    for c in range(8):
        b, j = c // 4, c % 4
        out[b, 1024 * j:1024 * (j + 1)] = res.results[c]["out"]
    return out


def _deadlock_check(S):
    engs = ["pe", "act", "dve", "pool", "sp"]
    per = {e: [o for o in S.ops if o.eng == e] for e in engs}
    ptr = {e: 0 for e in engs}
    done = set()
    prog = True
    while prog:
        prog = False
        for e in engs:
            while ptr[e] < len(per[e]):
                op = per[e][ptr[e]]
                if all(d.idx in done for d in op.deps):
                    done.add(op.idx)
                    ptr[e] += 1
                    prog = True
                else:
                    break
    stuck = {e: (ptr[e], len(per[e])) for e in engs}
    return stuck, {e: per[e][ptr[e]] for e in engs if ptr[e] < len(per[e])}
```

```python
import numpy as np
import concourse.bass as bass
import concourse.mybir as mybir
from concourse.bass_utils import run_bass_kernel_spmd
from contextlib import ExitStack

F32 = mybir.dt.float32
BF16 = mybir.dt.bfloat16
AF = mybir.ActivationFunctionType
ALU = mybir.AluOpType
AX = mybir.AxisListType

D = 2048
KT = 16
T = 1168
OWN0 = 128
NOWN = 1024
META0 = 1152
CH3 = [(0, 384), (384, 384), (768, 400)]
CH2 = [(0, 512), (512, 512)]
IN_COLS = 4616
DFF = 8192
NCV = 672
EPS = 1e-6
NEGM = -30000.0
NSLOT = 5
F_T5LATE = True
F_ATTPIPE = True
F_MLPIPE = True
F_QFILL = True


class Op:
    __slots__ = ("eng", "fn", "deps", "kind", "semkey", "inc", "sigval", "need", "idx")


class Sched:
    def __init__(self):
        self.ops = []
        self.lastw = {}
        self.readers = {}
        self.pend = {}

    def add(self, eng, fn, reads=(), writes=(), kind="c", semkey=None, inc=16):
        op = Op()
        op.eng, op.fn, op.kind, op.semkey, op.inc = eng, fn, kind, semkey, inc
        op.need = kind != "c"
        op.sigval = None
        op.idx = len(self.ops)
        reads = [("ps", r[1]) if r[0] == "ps" else r for r in reads]
        writes = [("ps", r[1]) if r[0] == "ps" else r for r in writes]
        deps = set()
        for r in reads:
            w = self.lastw.get(r)
            if w is not None:
                deps.add((w, True))
            p = self.pend.get(r[0])
            if p:
                for o in p:
                    deps.add((o, True))
        for r in writes:
            w = self.lastw.get(r)
            if w is not None:
                deps.add((w, False))
            for o in self.readers.get(r, ()):
                deps.add((o, False))
            p = self.pend.get(r[0])
            if p:
                for o in p:
                    deps.add((o, True))
        for r in reads:
            self.readers.setdefault(r, []).append(op)
        for r in writes:
            self.lastw[r] = op
            self.readers[r] = []
        fin = {}
        for (o, raw) in deps:
            if o is op:
                continue
            if o.kind == "c" and o.eng == eng and eng == "pe":
                continue
            fin[o.idx] = o
        red = {}
        keep = []
        for o in fin.values():
            if o.kind == "c":
                if o.eng not in red or red[o.eng].idx < o.idx:
                    red[o.eng] = o
            else:
                keep.append(o)
        op.deps = keep + list(red.values())
        for o in op.deps:
            o.need = True
        self.ops.append(op)
        return op

    def collect(self, views):
        out = set()
        for k, w in self.lastw.items():
            if k[0] in views:
                out.add(w)
        for k, rs in self.readers.items():
            if k[0] in views:
                out.update(rs)
        return out

    def alias_after(self, new_views, old_views):
        deps = self.collect(set(old_views))
        for v in new_views:
            self.pend.setdefault(v, set()).update(deps)

    def emit(self, nc, es):
        engs = ["pe", "act", "dve", "pool", "sp"]
        semE = {e: es.enter_context(nc.semaphore("s_" + e)) for e in engs}
        semK = {}
        cntE = {e: 0 for e in engs}
        cntK = {}
        for op in self.ops:
            if op.kind == "c":
                if op.need:
                    cntE[op.eng] += 1
                    op.sigval = cntE[op.eng]
            else:
                if op.semkey not in semK:
                    semK[op.semkey] = es.enter_context(nc.semaphore("k_" + str(op.semkey)))
                    cntK[op.semkey] = 0
                cntK[op.semkey] += op.inc
                op.sigval = cntK[op.semkey]
        block = es.enter_context(nc.Block())
        per = {e: [o for o in self.ops if o.eng == e] for e in engs}

        def run(handle, e):
            waited = {}
            for op in per[e]:
                w = {}
                for d in op.deps:
                    s = semE[d.eng] if d.kind == "c" else semK[d.semkey]
                    k = id(s)
                    if k not in w or w[k][1] < d.sigval:
                        w[k] = (s, d.sigval)
                for k, (s, v) in w.items():
                    if waited.get(k, 0) >= v:
                        continue
                    waited[k] = v
                    handle.wait_ge(s, v)
                ins = op.fn(handle)
                if op.kind == "c":
                    if op.need:
                        ins.then_inc(semE[e], 1)
                else:
                    ins.then_inc(semK[op.semkey], op.inc)
            if e == "sp":
                for k, s in semK.items():
                    handle.wait_ge(s, cntK[k])

        @block.tensor
        def _(h):
            run(h, "pe")

        @block.scalar
        def _(h):
            run(h, "act")

        @block.vector
        def _(h):
            run(h, "dve")

        @block.gpsimd
        def _(h):
            run(h, "pool")

        @block.sync
        def _(h):
            run(h, "sp")


def build_program():
    nc = bass.Bass("TRN2", target_bir_lowering=False)
    es = ExitStack()
    S = Sched()

    def din(name, shape):
        return nc.dram_tensor(name, list(shape), F32, kind="ExternalInput").ap()

    x_in = din("xin", [T, D])
    w_in = din("w_in", [1, D, IN_COLS])
    w_out = din("w_out", [1, D, D])
    w_up = din("w_up", [1, D, DFF])
    w_down = din("w_down", [1, DFF, D])
    rows96 = din("rows96", [96, 128])
    rel33 = din("rel33", [33, 16])
    ohc = din("ohc", [33, NCV])
    small = din("small", [128, 64])
    out_d = nc.dram_tensor("out", [NOWN, D], F32, kind="ExternalOutput").ap()
    vtab = nc.dram_tensor("vtab", [16, NCV], F32)
    sum_in = nc.dram_tensor("sum_in", [512, 260], F32)
    sum_out = nc.dram_tensor("sum_out", [2048, 260], F32)

    def sb(name, shape, dt):
        return es.enter_context(nc.sbuf_tensor(name, list(shape), dt))

    uT = sb("uT", [128, KT, T], BF16)
    mixT = sb("mixT", [128, KT, NOWN], BF16)
    arena = sb("arena", [128, 16384], F32)
    wsl = [sb(f"wsl{i}", [128, KT, 128], BF16) for i in range(NSLOT)]
    xin = [sb(f"xin{i}", [128, 2064], F32) for i in range(2)]
    cst_rows = sb("cst_rows", [96, 128], F32)
    cstT = sb("cstT", [128, 96], F32)
    ident_f = sb("ident_f", [128, 128], F32)
    ident_b = sb("ident_b", [128, 128], BF16)
    U_f = sb("U_f", [128, 128], F32)
    maskneg = sb("maskneg", [128, 128], F32)
    ones_f = sb("ones_f", [128, 128], F32)
    ones_b = sb("ones_b", [128, 128], BF16)
    smalls = sb("smalls", [128, 64], F32)
    expsink = sb("expsink", [128, 16], F32)
    rel_sb = sb("rel_sb", [33, 16], F32)
    ohc_sb = sb("ohc_sb", [33, NCV], F32)
    vtab_sb = sb("vtab_sb", [16, NCV], F32)
    wg = sb("wg", [128, KT, 8], BF16)
    stat = sb("stat", [128, 64], F32)
    rstd_bc = [sb(f"rstd_bc{i}", [128, 512], F32) for i in range(2)]
    gts = sb("gts", [128, 12, 9, 4], F32)
    Cst = sb("Cst", [128, 4, 260], F32)
    Cstb = sb("Cstb", [128, 4, 260], BF16)
    coef = sb("coef", [128, 32], F32)
    brow_sb = sb("brow_sb", [36, 128], F32)
    st2 = sb("st2", [128, 32], F32)
    ogm = sb("ogm", [128, 8, NOWN], BF16)
    OWNCH = [(OWN0, 512), (OWN0 + 512, 512)]

    ps = [es.enter_context(nc.psum_tensor(f"ps{i}", [128, 512], F32)) for i in range(8)]
    psb = [p.bitcast(BF16) for p in ps]

    arena_b = arena.bitcast(BF16)

    def aview(off_bytes, shape, dt):
        h = arena if dt == F32 else arena_b
        esz = 4 if dt == F32 else 2
        assert off_bytes % 4 == 0
        n = int(np.prod(shape[1:]))
        o = off_bytes // esz
        assert off_bytes + n * esz <= 65536, (off_bytes, shape)
        ap = h[:, o:o + n]
        if len(shape) == 3:
            ap = ap.rearrange("p (a b) -> p a b", a=shape[1], b=shape[2])
        elif len(shape) == 4:
            ap = ap.rearrange("p (a b c) -> p a b c", a=shape[1], b=shape[2], c=shape[3])
        return ap

    hT = aview(0, [128, KT, NOWN], F32)
    qT_ml = aview(0, [128, 4, T], BF16)
    kT_ml = aview(9344, [128, 4, T], BF16)
    V_ml = aview(18688, [128, 9, 4, 257], BF16)
    qT_att = aview(0, [128, 8, T], BF16)
    kT_att = aview(18688, [128, 2, T], BF16)
    V_att = aview(23360, [128, 10, 4, 65], BF16)
    R2 = 37192
    xn = [aview(R2 + i * 8192, [128, 2048], F32) for i in range(2)]
    gbc = aview(R2 + 16384, [128, 2048], F32)
    vtmp = [aview(R2 + i * 2336, [128, T], BF16) for i in range(2)]
    cpre = [aview(R2 + 4672 + i * 4684, [128, T + 3], F32) for i in range(2)]
    cacc = [aview(R2 + 14040 + i * 4672, [128, T], F32) for i in range(2)]
    btab = [[aview(R2 + (i * 4 + k) * 2048, [128, 4, 128], F32) for k in range(4)] for i in range(2)]
    PTt = [[aview(R2 + 16384 + (i * 3 + k) * 1024, [128, 512], BF16) for k in range(3)] for i in range(2)]
    att_tok = [aview(R2 + 22528 + i * 512, [128, 4, 64], BF16) for i in range(2)]
    Ulf = [aview(R2 + i * 512, [128, 128], F32) for i in range(4)]
    Et = [aview(R2 + 2048 + i * 2048, [128, 4, 128], F32) for i in range(2)]
    STt = [aview(R2 + 6144 + i * 1024, [128, 4, 128], BF16) for i in range(2)]
    tmpI = [aview(R2 + 8192 + i * 1040, [128, 260], F32) for i in range(2)]
    tot = [aview(R2 + 10272 + i * 1040, [128, 260], F32) for i in range(2)]
    hmn = [aview(R2 + 12352 + i * 512, [128, 256], BF16) for i in range(2)]
    kw = [aview(R2 + 13376 + i * 256, [128, 128], BF16) for i in range(2)]
    junk = aview(R2 + 13888, [128, 256], F32)
    og_tmp = [aview(R2 + i * 1024, [128, 512], BF16) for i in range(2)]
    Cg = [xin[i][:, 0:2064].rearrange("p (a b c) -> p a b c", a=2, b=4, c=258) for i in range(2)]
    sqv = mixT

    ARENA_MIX = ["qT_ml", "kT_ml", "V_ml", "qT_att", "kT_att", "V_att", "xn", "gbc", "vtmp", "cpre", "cacc",
                 "btab", "PT", "att_tok", "Ulf", "E", "ST", "tmpI", "tot", "hmn", "kw", "junk", "og_tmp", "Dm", "At", "qa", "snap"]

    def dma(q, out, in_, reads, writes, key, **kw_):
        return S.add(q, lambda e, out=out, in_=in_, kw_=kw_: e.dma_start(out=out, in_=in_, **kw_),
                     reads=reads, writes=writes, kind="d", semkey=key)

    dma("sp", cst_rows[:, :], rows96[:, :], [], [("cst_rows",)], "c0")
    dma("sp", smalls[:, :], small[:, :], [], [("smalls",)], "c1")
    dma("sp", rel_sb[:, :], rel33[:, :], [], [("rel_sb",)], "c2")
    dma("sp", ohc_sb[:, :], ohc[:, :], [], [("ohc_sb",)], "c3")
    dma("pool", wg[:, :, :], w_in[0, :, 4608:4616].rearrange("(kt p) c -> p kt c", p=128),
        [], [("wg",)], "c4")

    S.add("pool", lambda e: e.memset(ones_f[:, :], 1.0), [], [("ones_f",)])
    for h in range(4):
        S.add("pool", lambda e, h=h: e.memset(Cst[:, h, :], 0.0), [], [("Cst", h)])
    S.add("pool", lambda e: e.memset(ones_b[:, :], 1.0), [], [("ones_b",)])
    S.add("pool", lambda e: e.affine_select(out=ident_f[:, :], in_=ones_f[:, :], pattern=[[-1, 128]],
                                            compare_op=ALU.is_equal, fill=0.0, base=0, channel_multiplier=1),
          [("ones_f",)], [("ident_f",)])
    S.add("pool", lambda e: e.tensor_copy(out=ident_b[:, :], in_=ident_f[:, :]), [("ident_f",)], [("ident_b",)])
    S.add("pool", lambda e: e.affine_select(out=U_f[:, :], in_=ones_f[:, :], pattern=[[1, 128]],
                                            compare_op=ALU.is_ge, fill=0.0, base=0, channel_multiplier=-1),
          [("ones_f",)], [("U_f",)])
    S.add("pool", lambda e: e.tensor_scalar(out=maskneg[:, :], in0=U_f[:, :], scalar1=-1.0, scalar2=-NEGM,
                                            op0=ALU.add, op1=ALU.mult),
          [("U_f",)], [("maskneg",)])

    S.add("pe", lambda e: e.transpose(out=ps[7][:, 0:96], in_=cst_rows[:, :], identity=ident_f[0:96, 0:96]),
          [("cst_rows",), ("ident_f",)], [("ps", 7)])
    S.add("dve", lambda e: e.tensor_copy(out=cstT[:, :], in_=ps[7][:, 0:96]), [("ps", 7)], [("cstT",)])
    C_NMIX, C_NMLP, C_NFIN, C_MH, C_CB, C_CW = 0, 16, 32, 48, 56, 64
    S.add("act", lambda e: e.activation(out=expsink[:, :], in_=smalls[:, 0:16], func=AF.Exp),
          [("smalls",)], [("expsink",)])
    def emit_t5():
        for c in range(2):
            S.add("pe", lambda e, c=c: e.matmul(ps[6 + c][0:16, 0:336], lhsT=rel_sb[:, :],
                                                rhs=ohc_sb[:, c * 336:(c + 1) * 336], start=True, stop=True),
                  [("rel_sb",), ("ohc_sb",)], [("ps", 6 + c)])
            S.add("dve", lambda e, c=c: e.tensor_copy(out=vtab_sb[:, c * 336:(c + 1) * 336],
                                                      in_=ps[6 + c][0:16, 0:336]),
                  [("ps", 6 + c)], [("vtab_sb", c)])
        dma("sp", vtab.ap()[:, :], vtab_sb[:, :], [("vtab_sb", 0), ("vtab_sb", 1)], [("vtab",)], "c5")
        tz = []
        for k, (off, npart) in enumerate([(255, 128), (127, 128), (383 + 15, 16), (526 + 15, 16)]):
            tzk = nc.dram_tensor(f"tz{k}", [16, npart, 128], F32)
            tz.append(tzk)
            dma("sp", tzk.ap()[:, :, :], bass.AP(tensor=vtab, offset=off, ap=[[NCV, 16], [-1, npart], [1, 128]]),
                [("vtab",)], [("tz", k)], ("tzk", k))
        return tz

    if not F_T5LATE:
        tz = emit_t5()
    dma("sp", gbc, bass.AP(tensor=rows96.tensor, offset=0, ap=[[0, 128], [1, 2048]]), [], [("gbc",)], "c6")

    panels = []

    def wcols(wap, c0, n):
        return wap[0, :, c0:c0 + n]

    for i in range(2):
        panels.append([(0, 128, wcols(w_in, 1024 + 128 * i, 128))])
    for i in range(2):
        panels.append([(0, 128, wcols(w_in, 1280 + 128 * i, 128))])
    for a in range(2):
        for b in range(4):
            panels.append([(0, 64, wcols(w_in, 64 * (8 * a + b), 64)), (64, 64, wcols(w_in, 64 * (8 * a + 4 + b), 64))])
    P_ATT = 0
    P_MK = len(panels)
    for h in range(4):
        panels.append([(0, 128, wcols(w_in, 2048 + 128 * h, 128))])
    P_MV = len(panels)
    for i in range(8):
        panels.append([(0, 128, wcols(w_in, 2560 + 128 * i, 128))])
    P_MQ = len(panels)
    for h in range(4):
        panels.append([(0, 128, wcols(w_in, 1536 + 128 * h, 128))])
    P_MO = len(panels)
    for i in range(8):
        panels.append([(0, 128, wcols(w_in, 3584 + 128 * i, 128))])
    P_OUT = len(panels)
    for i in range(16):
        panels.append([(0, 128, wcols(w_out, 128 * i, 128))])
    P_FFN = len(panels)
    for fc in range(4):
        for ft in range(16):
            panels.append([(0, 128, wcols(w_up, fc * 2048 + ft * 128, 128))])
        for ct in range(16):
            panels.append([(0, 128, w_down[0, fc * 2048:(fc + 1) * 2048, ct * 128:(ct + 1) * 128])])
    NP = len(panels)
    issued = [0]

    def issue_upto(n):
        while issued[0] < min(n, NP):
            i = issued[0]
            s = i % NSLOT
            for pi, (c0, ncol, ap) in enumerate(panels[i]):
                wk = [("wsl", s, pi)] if len(panels[i]) == 2 else [("wsl", s, 0), ("wsl", s, 1)]
                dma("pool", wsl[s][:, :, c0:c0 + ncol], ap.rearrange("(kt p) c -> p kt c", p=128),
                    [], wk, ("w", s, pi))
            issued[0] += 1

    def wres(i):
        s = i % NSLOT
        return [("wsl", s, 0), ("wsl", s, 1)]

    issue_upto(NSLOT - 1)

    blocks = [(i * 128, 128) for i in range(9)] + [(META0, 16)]
    for bi, (r0, nr) in enumerate(blocks):
        s = bi % 2
        dma("sp", xin[s][0:nr, 0:2048], x_in[r0:r0 + nr, :], [], [("xin", s)], ("x", s))
        ss = stat[0:nr, s:s + 1]
        S.add("act", lambda e, s=s, nr=nr, ss=ss: e.activation(out=xn[s][0:nr, :], in_=xin[s][0:nr, 0:2048],
                                                               func=AF.Square, accum_out=ss),
              [("xin", s)], [("xn", s), ("stat", s)])
        rs = stat[0:nr, 2 + s:3 + s]
        S.add("act", lambda e, ss=ss, rs=rs: e.activation(out=rs, in_=ss, func=AF.Sqrt, scale=1.0 / D, bias=EPS),
              [("stat", s)], [("stat", 2 + s)])
        S.add("dve", lambda e, rs=rs: e.reciprocal(out=rs, in_=rs), [("stat", 2 + s)], [("stat", 2 + s)])
        S.add("dve", lambda e, s=s, nr=nr, rs=rs: e.scalar_tensor_tensor(
            out=xn[s][0:nr, :], in0=xin[s][0:nr, 0:2048], scalar=rs, in1=gbc[0:nr, :], op0=ALU.mult, op1=ALU.mult),
            [("xin", s), ("stat", 2 + s), ("gbc",), ("xn", s)], [("xn", s)])
        for q4 in range(4):
            bank = 6 + (q4 % 2)
            for k4 in range(4):
                kt = q4 * 4 + k4
                S.add("pe", lambda e, s=s, nr=nr, kt=kt, k4=k4, bank=bank: e.transpose(
                    out=ps[bank][:, k4 * 128:k4 * 128 + nr], in_=xn[s][0:nr, kt * 128:(kt + 1) * 128],
                    identity=ident_f[0:nr, 0:nr]),
                    [("xn", s), ("ident_f",)], [("ps", bank, k4)])
            src = ps[bank][:, :].rearrange("p (a b) -> p a b", a=4, b=128)[:, :, 0:nr]
            dst = uT[:, q4 * 4:q4 * 4 + 4, r0:r0 + nr]
            rd = [("ps", bank, k4) for k4 in range(4)]
            if q4 % 2 == 0:
                S.add("act", lambda e, src=src, dst=dst: e.copy(out=dst, in_=src), rd, [("uT", q4, bi)])
            else:
                S.add("dve", lambda e, src=src, dst=dst: e.tensor_copy(out=dst, in_=src), rd, [("uT", q4, bi)])

    if F_T5LATE:
        tz = emit_t5()

    def uT_res(c0, n):
        b0, b1 = c0 // 128, (c0 + n - 1) // 128
        return [("uT", q4, min(bi, 9)) for q4 in range(4) for bi in range(b0, b1 + 1)]

    ptile = [0]

    def project_gen(pi, chunks, rhs_of, rhs_res, evac, banks_per=3, fixed_par=None, yield_every=None, no_prefetch=False):
        if not no_prefetch:
            issue_upto(pi + NSLOT)
        s = pi % NSLOT
        if fixed_par is None:
            par = ptile[0] % 2
            ptile[0] += 1
        else:
            par = fixed_par
        wr = wres(pi)
        cnt = 0
        for kt in range(KT):
            for ci, (c0, n) in enumerate(chunks):
                bank = par * banks_per + ci
                S.add("pe", lambda e, s=s, kt=kt, c0=c0, n=n, bank=bank: e.matmul(
                    ps[bank][:, 0:n], lhsT=wsl[s][:, kt, :], rhs=rhs_of(kt, c0, n), start=(kt == 0), stop=(kt == KT - 1)),
                    wr + rhs_res(kt, c0, n), [("ps", bank)])
                cnt += 1
                if yield_every and cnt % yield_every == 0:
                    yield
        for ci, (c0, n) in enumerate(chunks):
            bank = par * banks_per + ci
            evac(ci, c0, n, ps[bank][:, 0:n], [("ps", bank)])
        yield

    def project(*a, **k):
        for _ in project_gen(*a, **k):
            pass

    def rhs_u(kt, c0, n):
        return uT[:, kt, c0:c0 + n]

    def res_u(kt, c0, n):
        return uT_res(c0, n)

    evflip = [0]

    def copy_evac(dst_of, wres_of, scale=None):
        def f(ci, c0, n, pap, pres):
            dst = dst_of(c0, n)
            evflip[0] += 1
            if scale is not None or evflip[0] % 2 == 0:
                if scale is None:
                    S.add("act", lambda e: e.copy(out=dst, in_=pap), pres, wres_of(ci))
                else:
                    S.add("act", lambda e: e.activation(out=dst, in_=pap, func=AF.Copy, scale=scale), pres, wres_of(ci))
            else:
                S.add("dve", lambda e: e.tensor_copy(out=dst, in_=pap), pres, wres_of(ci))
        return f

    S.alias_after(["qT_att", "kT_att", "V_att", "vtmp"], ["xn", "gbc"])
    for i in range(2):
        project(P_ATT + i, CH3, rhs_u, res_u,
                copy_evac(lambda c0, n, i=i: kT_att[:, i, c0:c0 + n], lambda ci, i=i: [("kT_att", i, ci)]))
    S.add("dve", lambda e: e.memset(V_att[:, :, :, 64:65], 1.0), [], [("V_att", "ones")])
    for i in range(2):
        vs = i % 2
        project(P_ATT + 2 + i, CH3, rhs_u, res_u,
                copy_evac(lambda c0, n, vs=vs: vtmp[vs][:, c0:c0 + n], lambda ci, vs=vs: [("vtmp", vs, ci)]))
        for blk in range(10):
            r0, nr = blocks[blk]
            ci = 0 if r0 < 384 else (1 if r0 < 768 else 2)
            bank, col = (7, blk * 128) if blk < 8 else (6, (blk - 8) * 128)
            S.add("pe", lambda e, vs=vs, r0=r0, nr=nr, bank=bank, col=col: e.transpose(
                out=psb[bank][0:nr, col:col + 128], in_=vtmp[vs][:, r0:r0 + nr], identity=ident_b[:, :]),
                [("vtmp", vs, ci), ("ident_b",)], [("ps", bank)])
        S.add("act", lambda e, i=i: e.copy(out=V_att[:, 0:8, 2 * i:2 * i + 2, 0:64],
                                           in_=psb[7][:, 0:1024].rearrange("p (a b c) -> p a b c", a=8, b=2, c=64)),
              [("ps", 7)], [("V_att", blk, i) for blk in range(8)])
        S.add("dve", lambda e, i=i: e.tensor_copy(out=V_att[:, 8, 2 * i:2 * i + 2, 0:64],
                                                  in_=psb[6][:, 0:128].rearrange("p (a b) -> p a b", a=2, b=64)),
              [("ps", 6)], [("V_att", 8, i)])
        S.add("dve", lambda e, i=i: e.tensor_copy(out=V_att[0:16, 9, 2 * i:2 * i + 2, 0:64],
                                                  in_=psb[6][0:16, 128:256].rearrange("p (a b) -> p a b", a=2, b=64)),
              [("ps", 6)], [("V_att", 9, i)])
    S.add("dve", lambda e: e.tensor_scalar(out=V_att[:, 0, :, :], in0=V_att[:, 0, :, :], scalar1=smalls[:, 30:31],
                                           scalar2=None, op0=ALU.mult),
          [("V_att", 0, 0), ("V_att", 0, 1), ("V_att", "ones"), ("smalls",)], [("V_att", 0, 0), ("V_att", 0, 1), ("V_att", "ones")])
    for a in range(2):
        for b in range(4):
            t8 = a * 4 + b
            project(P_ATT + 4 + t8, CH3, rhs_u, res_u,
                    copy_evac(lambda c0, n, t8=t8: qT_att[:, t8, c0:c0 + n], lambda ci, t8=t8: [("qT_att", t8, ci)],
                              scale=0.125))

    S.alias_after(["btab", "PT", "att_tok"], ["vtmp", "xn", "gbc"])

    def chunk_of(c):
        return 0 if c < 384 else (1 if c < 768 else 2)

    def att_ctx(i):
        g, n = i // 8, i % 8
        a, eh = g // 2, g % 2
        p0 = eh * 64
        ts = g % 2
        tabs = btab[ts]
        qo = OWN0 + 128 * n
        kprev, kcur = 128 * n, 128 * (n + 1)
        pset = i % 2
        sb3 = [pset * 3 + k for k in range(3)]
        spec = [(kprev, 128, tabs[0]), (kcur, 128, tabs[1]), (META0, 16, tabs[2] if n == 0 else tabs[3])]
        return g, n, a, p0, ts, tabs, qo, pset, sb3, spec

    def att_front(i):
        g, n, a, p0, ts, tabs, qo, pset, sb3, spec = att_ctx(i)
        if n == 0:
            for k, npart in enumerate([128, 128, 16, 16]):
                dma("sp", tabs[k][0:npart, :, :], tz[k].ap()[4 * g:4 * g + 4, :, :].rearrange("h c r -> c h r"),
                    [("tz", k)], [("btab", ts, k)], ("bt", ts, k))
        qres = [("qT_att", a * 4 + b, chunk_of(qo)) for b in range(4)]
        for k, (k0, nk, tb) in enumerate(spec):
            bank = sb3[k]
            S.add("pe", lambda e, a=a, p0=p0, k0=k0, nk=nk, qo=qo, bank=bank: e.matmul(
                ps[bank][0:nk, :], lhsT=kT_att[p0:p0 + 64, a, k0:k0 + nk],
                rhs=qT_att[p0:p0 + 64, a * 4:a * 4 + 4, qo:qo + 128], start=True, stop=True),
                [("kT_att", a, chunk_of(k0))] + qres, [("ps", bank)])
            tk = k if k < 2 else (2 if n == 0 else 3)
            S.add("dve", lambda e, bank=bank, nk=nk, tb=tb: e.tensor_tensor(
                out=ps[bank][0:nk, :], in0=ps[bank][0:nk, :], in1=tb[0:nk, :, :].rearrange("p a b -> p (a b)"),
                op=ALU.add), [("ps", bank), ("btab", ts, tk)], [("ps", bank)])
            S.add("act", lambda e, bank=bank, nk=nk, pset=pset, k=k: e.activation(
                out=PTt[pset][k][0:nk, :], in_=ps[bank][0:nk, :], func=AF.Exp),
                [("ps", bank)], [("PT", pset, k)])

    def att_back(i):
        g, n, a, p0, ts, tabs, qo, pset, sb3, spec = att_ctx(i)
        vb = [n, n + 1, 9]
        for b in range(4):
            for k, (k0, nk, tb) in enumerate(spec):
                S.add("pe", lambda e, b=b, k=k, nk=nk, pset=pset, vbk=vb[k], g=g: e.matmul(
                    ps[6][:, b * 65:(b + 1) * 65], lhsT=PTt[pset][k][0:nk, b * 128:(b + 1) * 128],
                    rhs=V_att[0:nk, vbk, g, :], start=(k == 0), stop=(k == 2)),
                    [("PT", pset, k), ("V_att", vb[k], g // 2), ("V_att", "ones")], [("ps", 6)])
        o4 = ps[6][:, 0:260].rearrange("p (a b) -> p a b", a=4, b=65)
        dc = 8 + 4 * pset
        den = stat[:, dc:dc + 4]
        DK = ("stat", "den", pset)
        S.add("dve", lambda e, o4=o4, g=g, den=den: e.tensor_tensor(
            out=den, in0=o4[:, :, 64], in1=expsink[:, 4 * g:4 * g + 4], op=ALU.add),
            [("ps", 6), ("expsink",)], [DK])
        S.add("dve", lambda e, den=den: e.reciprocal(out=den, in_=den), [DK], [DK])
        at = att_tok[pset]
        S.add("dve", lambda e, o4=o4, at=at, dc=dc: e.tensor_tensor(
            out=at, in0=o4[:, :, 0:64],
            in1=bass.AP(tensor=stat, offset=dc, ap=[[64, 128], [1, 4], [0, 64]]), op=ALU.mult),
            [("ps", 6), DK], [("att_tok", pset)])
        for i2 in range(2):
            S.add("pe", lambda e, at=at, i2=i2: e.transpose(
                out=psb[7][:, i2 * 128:(i2 + 1) * 128],
                in_=at[:, 2 * i2:2 * i2 + 2, :].rearrange("p a b -> p (a b)"), identity=ident_b[:, :]),
                [("att_tok", pset), ("ident_b",)], [("ps", 7, i2)])
        S.add("act", lambda e, g=g, n=n: e.copy(
            out=mixT[:, 2 * g:2 * g + 2, n * 128:(n + 1) * 128],
            in_=psb[7][:, 0:256].rearrange("p (a b) -> p a b", a=2, b=128)),
            [("ps", 7, 0), ("ps", 7, 1)], [("mixT", 2 * g, n), ("mixT", 2 * g + 1, n)])

    if F_ATTPIPE:
        att_front(0)
        for i in range(32):
            if i + 1 < 32:
                att_front(i + 1)
            att_back(i)
    else:
        for i in range(32):
            att_front(i)
            att_back(i)

    S.alias_after(["qT_ml", "kT_ml", "V_ml", "vtmp", "cpre", "cacc"],
                  ["qT_att", "kT_att", "V_att", "btab", "PT", "att_tok", "xn", "gbc"])
    for s_ in range(2):
        S.add("dve", lambda e, s_=s_: e.memset(cpre[s_][:, 0:3], 0.0), [], [("cpre", s_, "z")])
    def convproj_gen(which, base, dstT, h, cs, ye=None):
        ctile = h if which == "q" else 4 + h
        yield from project_gen(base + h, CH3, rhs_u, res_u,
                               copy_evac(lambda c0, n, cs=cs: cpre[cs][:, 3 + c0:3 + c0 + n], lambda ci, cs=cs: [("cpre", cs, ci)]),
                               yield_every=ye, fixed_par=(0 if ye else None))
        cr = [("cpre", cs, ci) for ci in range(3)] + [("cpre", cs, "z"), ("cstT",)]
        wcol = lambda j: cstT[:, C_CW + j * 8 + ctile:C_CW + j * 8 + ctile + 1]
        S.add("dve", lambda e, cs=cs, w0=wcol(0), bb=cstT[:, C_CB + ctile:C_CB + ctile + 1]: e.tensor_scalar(
            out=cacc[cs][:, :], in0=cpre[cs][:, 0:T], scalar1=w0, scalar2=bb, op0=ALU.mult, op1=ALU.add),
            cr, [("cacc", cs)])
        for j in range(1, 4):
            S.add("dve", lambda e, cs=cs, j=j, wj=wcol(j): e.scalar_tensor_tensor(
                out=cacc[cs][:, :], in0=cpre[cs][:, j:j + T], scalar=wj, in1=cacc[cs][:, :], op0=ALU.mult, op1=ALU.add),
                cr + [("cacc", cs)], [("cacc", cs)])
        S.add("act", lambda e, cs=cs, h=h, dstT=dstT: e.activation(out=dstT[:, h, :], in_=cacc[cs][:, :], func=AF.Silu),
              [("cacc", cs)], [(which + "T_ml", h)])
        yield

    for h in range(4):
        for _ in convproj_gen("k", P_MK, kT_ml, h, h % 2):
            pass
    S.add("dve", lambda e: e.memset(V_ml[:, :, :, 256:257], 1.0), [], [("V_ml", "ones")])
    for i in range(8):
        vs = i % 2
        h, hf = i // 2, i % 2
        project(P_MV + i, CH3, rhs_u, res_u,
                copy_evac(lambda c0, n, vs=vs: vtmp[vs][:, c0:c0 + n], lambda ci, vs=vs: [("vtmp", vs, ci)]))
        for blk in range(9):
            r0 = blk * 128
            bank, col = (7, blk * 128) if blk < 8 else (6, 0)
            S.add("pe", lambda e, vs=vs, r0=r0, bank=bank, col=col: e.transpose(
                out=psb[bank][:, col:col + 128], in_=vtmp[vs][:, r0:r0 + 128], identity=ident_b[:, :]),
                [("vtmp", vs, chunk_of(r0)), ("ident_b",)], [("ps", bank)])
        S.add("act", lambda e, h=h, hf=hf: e.copy(out=V_ml[:, 0:8, h, hf * 128:(hf + 1) * 128],
                                                  in_=psb[7][:, 0:1024].rearrange("p (a b) -> p a b", a=8, b=128)),
              [("ps", 7)], [("V_ml", blk, h, hf) for blk in range(8)])
        S.add("dve", lambda e, h=h, hf=hf: e.tensor_copy(out=V_ml[:, 8, h, hf * 128:(hf + 1) * 128], in_=psb[6][:, 0:128]),
              [("ps", 6)], [("V_ml", 8, h, hf)])
    for blk in range(9):
        for kt in range(KT):
            S.add("pe", lambda e, blk=blk, kt=kt: e.matmul(
                ps[6][:, blk * 8:(blk + 1) * 8], lhsT=uT[:, kt, blk * 128:(blk + 1) * 128], rhs=wg[:, kt, :],
                start=(kt == 0), stop=(kt == KT - 1)),
                uT_res(blk * 128, 128) + [("wg",)], [("ps", 6)])
    G_GS, G_LI, G_LF, G_B, G_BT, G_W, G_DEC, G_A, G_BIAS, G_T1, G_T2, G_T3 = range(12)
    gp = ps[6][:, 0:72].rearrange("p (a b) -> p a b", a=9, b=8)
    bias_i = bass.AP(tensor=smalls, offset=16, ap=[[64, 128], [0, 9], [1, 4]])
    bias_f = bass.AP(tensor=smalls, offset=20, ap=[[64, 128], [0, 9], [1, 4]])
    valid_bc = bass.AP(tensor=smalls, offset=40, ap=[[64, 128], [1, 9], [0, 4]])
    vneg_bc = bass.AP(tensor=smalls, offset=49, ap=[[64, 128], [1, 9], [0, 4]])
    GR = [("gts", i) for i in range(12)]
    S.add("dve", lambda e: e.tensor_tensor(out=gts[:, G_T1], in0=gp[:, :, 0:4], in1=bias_i, op=ALU.add),
          [("ps", 6), ("smalls",)], [("gts", G_T1)])
    S.add("dve", lambda e: e.tensor_tensor(out=gts[:, G_T2], in0=gp[:, :, 4:8], in1=bias_f, op=ALU.add),
          [("ps", 6), ("smalls",)], [("gts", G_T2)])
    S.add("act", lambda e: e.activation(out=gts[:, G_T1], in_=gts[:, G_T1], func=AF.Tanh, scale=1.0 / 15.0),
          [("gts", G_T1)], [("gts", G_T1)])
    S.add("act", lambda e: e.activation(out=gts[:, G_T2], in_=gts[:, G_T2], func=AF.Tanh, scale=1.0 / 15.0),
          [("gts", G_T2)], [("gts", G_T2)])
    S.add("dve", lambda e: e.scalar_tensor_tensor(out=gts[:, G_LI], in0=gts[:, G_T1], scalar=15.0, in1=valid_bc,
                                                  op0=ALU.mult, op1=ALU.mult), [("gts", G_T1), ("smalls",)], [("gts", G_LI)])
    S.add("dve", lambda e: e.tensor_tensor(out=gts[:, G_LI], in0=gts[:, G_LI], in1=vneg_bc, op=ALU.add),
          [("gts", G_LI), ("smalls",)], [("gts", G_LI)])
    S.add("act", lambda e: e.activation(out=gts[:, G_T3], in_=gts[:, G_T2], func=AF.Exp, scale=-15.0),
          [("gts", G_T2)], [("gts", G_T3)])
    S.add("act", lambda e: e.activation(out=gts[:, G_T3], in_=gts[:, G_T3], func=AF.Ln, bias=1.0),
          [("gts", G_T3)], [("gts", G_T3)])
    S.add("dve", lambda e: e.scalar_tensor_tensor(out=gts[:, G_LF], in0=gts[:, G_T3], scalar=-1.0, in1=valid_bc,
                                                  op0=ALU.mult, op1=ALU.mult), [("gts", G_T3), ("smalls",)], [("gts", G_LF)])
    for blk in range(9):
        S.add("pe", lambda e, blk=blk: e.matmul(ps[7][:, blk * 4:(blk + 1) * 4], lhsT=U_f[:, :], rhs=gts[:, G_LF, blk, :],
                                                start=True, stop=True), [("gts", G_LF), ("U_f",)], [("ps", 7)])
        S.add("pe", lambda e, blk=blk: e.matmul(ps[7][:, 64 + blk * 4:64 + (blk + 1) * 4], lhsT=ones_f[:, :],
                                                rhs=gts[:, G_LF, blk, :], start=True, stop=True),
              [("gts", G_LF), ("ones_f",)], [("ps", 7)])
    pb_ = ps[7][:, 0:36].rearrange("p (a b) -> p a b", a=9, b=4)
    pbt = ps[7][:, 64:100].rearrange("p (a b) -> p a b", a=9, b=4)
    S.add("dve", lambda e: e.tensor_copy(out=gts[:, G_B], in_=pb_), [("ps", 7)], [("gts", G_B)])
    S.add("dve", lambda e: e.tensor_copy(out=gts[:, G_BT], in_=pbt), [("ps", 7)], [("gts", G_BT)])
    S.add("dve", lambda e: e.tensor_tensor(out=gts[:, G_BIAS], in0=gts[:, G_LI], in1=gts[:, G_B], op=ALU.subtract),
          [("gts", G_LI), ("gts", G_B)], [("gts", G_BIAS)])
    S.add("dve", lambda e: e.tensor_tensor(out=gts[:, G_W], in0=gts[:, G_BIAS], in1=gts[:, G_BT], op=ALU.add),
          [("gts", G_BIAS), ("gts", G_BT)], [("gts", G_W)])
    S.add("act", lambda e: e.activation(out=gts[:, G_W], in_=gts[:, G_W], func=AF.Exp), [("gts", G_W)], [("gts", G_W)])
    S.add("act", lambda e: e.activation(out=gts[:, G_DEC], in_=gts[:, G_BT], func=AF.Exp), [("gts", G_BT)], [("gts", G_DEC)])
    S.add("dve", lambda e: e.tensor_scalar(out=gts[:, G_A], in0=gts[:, G_B], scalar1=float(np.log(128.0 ** -0.5)),
                                           scalar2=None, op0=ALU.add), [("gts", G_B)], [("gts", G_A)])
    S.add("act", lambda e: e.activation(out=gts[:, G_A], in_=gts[:, G_A], func=AF.Exp), [("gts", G_A)], [("gts", G_A)])
    brow = nc.dram_tensor("brow", [36, 128], F32)
    S.add("pe", lambda e: e.transpose(out=ps[7][0:36, 128:256], in_=gts[:, G_B].rearrange("p a b -> p (a b)"),
                                      identity=ident_f[:, :]), [("gts", G_B), ("ident_f",)], [("ps", 7)])
    S.add("dve", lambda e: e.tensor_copy(out=brow_sb[:, :], in_=ps[7][0:36, 128:256]), [("ps", 7)], [("brow_sb",)])
    dma("sp", brow.ap()[:, :], brow_sb[:, :], [("brow_sb",)], [("brow",)], "brw")

    S.alias_after(["kw", "junk", "og_tmp"], ["vtmp", "cpre", "cacc", "xn", "gbc", "btab", "PT", "att_tok"])
    Ulf = [aview(R2 + i * 512, [128, 128], F32) for i in range(4)]
    Dm = [aview(R2 + 2048 + i * 2048, [128, 4, 128], F32) for i in range(2)]
    Et = [aview(R2 + 6144 + i * 2048, [128, 4, 128], F32) for i in range(2)]
    STt = [aview(R2 + 10240 + i * 1024, [128, 4, 128], BF16) for i in range(2)]
    At = [aview(R2 + 12288 + i * 2048, [128, 4, 128], F32) for i in range(2)]
    qa = [aview(R2 + 16384 + i * 1024, [128, 4, 128], BF16) for i in range(2)]
    hmn = [aview(R2 + 18432 + i * 2048, [128, 4, 256], BF16) for i in range(2)]
    kw4 = [aview(R2 + 23224 + i * 1024, [128, 4, 128], BF16) for i in range(2)]
    junk = aview(R2 + 25272, [128, 256], F32)
    og_tmp = [aview(R2 + 26296 + i * 1024, [128, 512], BF16) for i in range(2)]
    Bbs = [aview(R2 + 0, [128, 4, 128], F32), Cstb.bitcast(F32)[:, :, :].rearrange("p a b -> p (a b)")[:, 0:512].rearrange("p (a b) -> p a b", a=4, b=128)]
    snapb = [xin[i].bitcast(BF16)[:, 0:4112].rearrange("p (a b c) -> p a b c", a=4, b=4, c=257) for i in range(2)]

    def snap(sn):
        return snapb[sn // 4][:, sn % 4]

    ogf = [0]

    def mo_gen():
        for i in range(8):
            def ev(ci, c0, n, pap, pres, i=i):
                os_ = ogf[0] % 2
                ogf[0] += 1
                S.add("act", lambda e: e.activation(out=og_tmp[os_][:, :], in_=pap, func=AF.Sigmoid), pres, [("og_tmp", os_)])
                o0 = c0 - OWN0
                S.add("dve", lambda e: e.tensor_scalar(out=ogm[:, i, o0:o0 + 512], in0=og_tmp[os_][:, :],
                                                       scalar1=cstT[:, C_MH + i:C_MH + i + 1], scalar2=None, op0=ALU.mult),
                      [("og_tmp", os_), ("cstT",)], [("ogm", i, o0 // 512)])
            yield from project_gen(P_MO + i, OWNCH, rhs_u, res_u, ev, banks_per=2, fixed_par=0, yield_every=8)

    if not F_QFILL:
        for h in range(4):
            for _ in convproj_gen("q", P_MQ, qT_ml, h, 0):
                pass

    def filler_gen():
        if F_QFILL:
            for h in range(4):
                yield from convproj_gen("q", P_MQ, qT_ml, h, 0, ye=8)
        yield from mo_gen()

    mo_it = filler_gen()

    def mo_step(k=1):
        for _ in range(k):
            try:
                next(mo_it)
            except StopIteration:
                return

    def state_step(blk, first, do_snap):
        ks = blk % 2
        c0 = blk * 128
        for h in range(4):
            S.add("pe", lambda e, h=h, c0=c0: e.transpose(out=psb[7][:, h * 128:(h + 1) * 128], in_=kT_ml[:, h, c0:c0 + 128],
                                                          identity=ident_b[:, :]), [("kT_ml", h), ("ident_b",)], [("ps", 7)])
        wbc = bass.AP(tensor=gts, offset=(G_W * 9 + blk) * 4, ap=[[12 * 9 * 4, 128], [1, 4], [0, 128]])
        S.add("dve", lambda e, ks=ks, wbc=wbc: e.tensor_tensor(
            out=kw4[ks], in0=psb[7][:, 0:512].rearrange("p (a b) -> p a b", a=4, b=128), in1=wbc, op=ALU.mult),
            [("ps", 7), ("gts", G_W)], [("kw", ks)])
        mo_step()
        for h in range(4):
            S.add("pe", lambda e, ks=ks, blk=blk, h=h: e.matmul(ps[3 + h][:, 0:257], lhsT=kw4[ks][:, h, :], rhs=V_ml[:, blk, h, :],
                                                               start=True, stop=True),
                  [("kw", ks), ("V_ml", blk, h, 0), ("V_ml", blk, h, 1), ("V_ml", "ones")], [("ps", 3 + h)])
        for h in range(4):
            if first:
                S.add("dve", lambda e, h=h: e.tensor_copy(out=Cst[:, h, 0:257], in_=ps[3 + h][:, 0:257]), [("ps", 3 + h)], [("Cst", h)])
            else:
                S.add("dve", lambda e, h=h, blk=blk: e.scalar_tensor_tensor(
                    out=Cst[:, h, 0:257], in0=Cst[:, h, 0:257], scalar=gts[:, G_DEC, blk, h:h + 1], in1=ps[3 + h][:, 0:257],
                    op0=ALU.mult, op1=ALU.add), [("ps", 3 + h), ("Cst", h), ("gts", G_DEC)], [("Cst", h)])
        if do_snap:
            S.add("act", lambda e, blk=blk: e.copy(out=snap(blk), in_=Cst[:, :, 0:257]), [("Cst", h) for h in range(4)], [("snap", blk)])
        mo_step()

    for blk in range(9):
        state_step(blk, first=(blk == 0), do_snap=False)
    S.add("dve", lambda e: e.tensor_reduce(out=stat[:, 16:20], in_=gts[:, G_BT].rearrange("p a b -> p b a"), axis=AX.X, op=ALU.add),
          [("gts", G_BT)], [("stat", "bt")])
    for h in range(4):
        S.add("dve", lambda e, h=h: e.tensor_copy(out=Cst[:, h, 257:258], in_=stat[:, 16 + h:17 + h]),
              [("stat", "bt"), ("Cst", h)], [("Cst", h)])
    dma("sp", sum_in.ap().rearrange("(h p) c -> p h c", p=128), Cst[:, :, :], [("Cst", h) for h in range(4)], [("sum_in",)], "sm0")
    S.add("pool", lambda e: e.collective_compute("AllGather", ALU.bypass, replica_groups=[[0, 1, 2, 3], [4, 5, 6, 7]],
                                                 ins=[sum_in.ap().opt()], outs=[sum_out.ap().opt()]),
          [("sum_in",)], [("sum_out",)], kind="cc", semkey="cc", inc=1)
    S.alias_after(["Cg"], ["xin"])
    for i2 in range(2):
        src = bass.AP(tensor=sum_out, offset=i2 * 2 * 512 * 260, ap=[[260, 128], [512 * 260, 2], [128 * 260, 4], [1, 258]])
        dma("sp", Cg[i2], src, [("sum_out",)], [("Cg", i2)], ("cg", i2))
    mo_step(42)

    def cgv(i):
        return Cg[i // 2][:, i % 2]
    ex = coef[:, 0:16].rearrange("p (a b) -> p a b", a=4, b=4)
    cf = coef[:, 16:32].rearrange("p (a b) -> p a b", a=4, b=4)
    S.add("dve", lambda e: e.memset(coef[:, 0:16], 0.0), [], [("coef",)])
    M01, M02, M12 = smalls[:, 24:25], smalls[:, 25:26], smalls[:, 26:27]
    CGR = [("Cg", 0), ("Cg", 1), ("smalls",)]
    S.add("dve", lambda e: e.tensor_scalar(out=ex[:, 0, :], in0=cgv(1)[:, :, 257], scalar1=M01, scalar2=None, op0=ALU.mult),
          CGR + [("coef",)], [("coef",)])
    S.add("dve", lambda e: e.scalar_tensor_tensor(out=ex[:, 0, :], in0=cgv(2)[:, :, 257], scalar=M02, in1=ex[:, 0, :],
                                                  op0=ALU.mult, op1=ALU.add), CGR + [("coef",)], [("coef",)])
    S.add("dve", lambda e: e.tensor_scalar(out=ex[:, 1, :], in0=cgv(2)[:, :, 257], scalar1=M12, scalar2=None, op0=ALU.mult),
          CGR + [("coef",)], [("coef",)])
    S.add("act", lambda e: e.activation(out=coef[:, 16:32], in_=coef[:, 0:16], func=AF.Exp), [("coef",)], [("coef", "c")])
    for i in range(3):
        S.add("dve", lambda e, i=i: e.tensor_scalar(out=cf[:, i, :], in0=cf[:, i, :], scalar1=smalls[:, 27 + i:28 + i],
                                                    scalar2=None, op0=ALU.mult), [("coef", "c"), ("smalls",)], [("coef", "c")])
    for h in range(4):
        S.add("dve", lambda e, h=h: e.tensor_scalar(out=Cst[:, h, 0:257], in0=cgv(0)[:, h, 0:257], scalar1=cf[:, 0, h:h + 1],
                                                    scalar2=None, op0=ALU.mult), CGR + [("coef", "c"), ("Cst", h)], [("Cst", h)])
        for i in range(1, 3):
            S.add("dve", lambda e, h=h, i=i: e.scalar_tensor_tensor(
                out=Cst[:, h, 0:257], in0=cgv(i)[:, h, 0:257], scalar=cf[:, i, h:h + 1], in1=Cst[:, h, 0:257],
                op0=ALU.mult, op1=ALU.add), CGR + [("coef", "c"), ("Cst", h)], [("Cst", h)])

    S.alias_after(["snap"], ["Cg", "xin"])
    for blk in range(8):
        state_step(blk, first=False, do_snap=True)
    mo_step(1000)
    uTf = uT.bitcast(F32)
    S.alias_after(["xpre"], ["uT"])
    xpre = [uTf[:, 4 * n:4 * n + 4, 0:512] for n in range(4)]
    for n in range(4):
        dma("sp", xpre[n], x_in[OWN0 + n * 128:OWN0 + (n + 1) * 128, :].rearrange("p (a b) -> p a b", a=4, b=512),
            [], [("xpre", n)], ("xp", n))

    S.alias_after(["Bbs", "Ulf", "Dm", "E", "ST", "At", "qa", "hmn"], ["vtmp", "cpre", "cacc", "xn", "gbc", "btab", "PT", "att_tok"])
    SCALE_ML = 128.0 ** -0.5
    LNS = float(np.log(SCALE_ML))
    maskneg_bc = bass.AP(tensor=maskneg, offset=0, ap=[[128, 128], [0, 4], [1, 128]])
    def ml_front(blk):
        c0 = blk * 128
        es_ = blk % 2
        bkq, bbb = es_, 2 + es_
        for h in range(4):
            S.add("pe", lambda e, h=h, c0=c0, bkq=bkq: e.matmul(ps[bkq][:, h * 128:(h + 1) * 128], lhsT=kT_ml[:, h, c0:c0 + 128],
                                                                rhs=qT_ml[:, h, c0:c0 + 128], start=True, stop=True),
                  [("kT_ml", h), ("qT_ml", h)], [("ps", bkq)])
        Bb = Bbs[es_]

        def bload(b_):
            dma("sp", Bbs[b_ % 2], bass.AP(tensor=brow, offset=b_ * 4 * 128, ap=[[0, 128], [128, 4], [1, 128]]),
                [("brow",)], [("Bbs", b_ % 2)], ("bb", b_ % 2))
        if blk == 1:
            bload(1)
        if blk + 1 <= 8:
            bload(blk + 1)
        pK = ps[bkq][:, :].rearrange("p (a b) -> p a b", a=4, b=128)
        S.add("act", lambda e, es_=es_, Bb=Bb: e.activation(out=At[es_], in_=Bb, func=AF.Exp, bias=LNS), [("Bbs", es_)], [("At", es_)])
        S.add("dve", lambda e, es_=es_, Bb=Bb: e.tensor_tensor(out=Dm[es_], in0=Bb, in1=maskneg_bc, op=ALU.add),
              [("Bbs", es_), ("maskneg",)], [("Dm", es_)])
        for h in range(4):
            S.add("act", lambda e, h=h, blk=blk, es_=es_: e.activation(
                out=Et[es_][:, h, :], in_=Dm[es_][:, h, :], func=AF.Exp, bias=gts[:, G_BIAS, blk, h:h + 1]),
                [("Dm", es_), ("gts", G_BIAS)], [("E", es_, h)])
        S.add("dve", lambda e, es_=es_, pK=pK: e.scalar_tensor_tensor(out=STt[es_], in0=pK, scalar=SCALE_ML, in1=Et[es_],
                                                                      op0=ALU.mult, op1=ALU.mult),
              [("ps", bkq)] + [("E", es_, h) for h in range(4)], [("ST", es_)])
        S.add("dve", lambda e, es_=es_, c0=c0: e.tensor_tensor(out=qa[es_], in0=qT_ml[:, :, c0:c0 + 128], in1=At[es_], op=ALU.mult),
              [("qT_ml", h) for h in range(4)] + [("At", es_)], [("qa", es_)])

    def ml_back(blk):
        n = blk - 1
        es_ = blk % 2
        sn = snap(blk - 1)
        for h in range(4):
            bn = 4 + h // 2
            cN = (h % 2) * 256
            S.add("pe", lambda e, h=h, es_=es_, blk=blk, bn=bn, cN=cN: e.matmul(
                ps[bn][:, cN:cN + 256], lhsT=STt[es_][:, h, :], rhs=V_ml[:, blk, h, 0:256], start=True, stop=False),
                [("ST", es_), ("V_ml", blk, h, 0), ("V_ml", blk, h, 1)], [("ps", bn)])
            S.add("pe", lambda e, h=h, es_=es_, bn=bn, cN=cN, sn=sn: e.matmul(
                ps[bn][:, cN:cN + 256], lhsT=qa[es_][:, h, :], rhs=sn[:, h, 0:256], start=False, stop=True),
                [("qa", es_), ("snap", blk - 1)], [("ps", bn)])
            S.add("pe", lambda e, h=h, es_=es_: e.matmul(ps[6][:, h:h + 1], lhsT=STt[es_][:, h, :], rhs=ones_b[:, 0:1],
                                                         start=True, stop=False), [("ST", es_), ("ones_b",)], [("ps", 6)])
            S.add("pe", lambda e, h=h, es_=es_, sn=sn: e.matmul(ps[6][:, h:h + 1], lhsT=qa[es_][:, h, :], rhs=sn[:, h, 256:257],
                                                                start=False, stop=True), [("qa", es_), ("snap", blk - 1)], [("ps", 6)])
        sd = st2[:, es_ * 16:(es_ + 1) * 16]
        SK = ("st2", es_)
        S.add("act", lambda e, sd=sd: e.copy(out=sd[:, 0:4], in_=ps[6][:, 0:4]), [("ps", 6)], [SK])
        S.add("dve", lambda e, sd=sd: e.scalar_tensor_tensor(out=sd[:, 0:4], in0=sd[:, 0:4], scalar=-1.0, in1=sd[:, 0:4],
                                                             op0=ALU.mult, op1=ALU.max), [SK], [SK])
        S.add("dve", lambda e, sd=sd: e.tensor_scalar(out=sd[:, 0:4], in0=sd[:, 0:4], scalar1=1.0, scalar2=None, op0=ALU.max), [SK], [SK])
        S.add("dve", lambda e, sd=sd: e.reciprocal(out=sd[:, 0:4], in_=sd[:, 0:4]), [SK], [SK])
        for h in range(4):
            bn = 4 + h // 2
            cN = (h % 2) * 256
            S.add("act", lambda e, h=h, bn=bn, cN=cN, sd=sd: e.activation(out=junk[:, :], in_=ps[bn][:, cN:cN + 256], func=AF.Square,
                                                                          accum_out=sd[:, 4 + h:5 + h]),
                  [("ps", bn)], [("junk",), ("st2s", es_, h)])
        SS = [("st2s", es_, h) for h in range(4)]
        S.add("dve", lambda e, sd=sd: e.tensor_tensor(out=sd[:, 8:12], in0=sd[:, 0:4], in1=sd[:, 0:4], op=ALU.mult), [SK], [("st2b", es_)])
        S.add("dve", lambda e, sd=sd: e.tensor_tensor(out=sd[:, 8:12], in0=sd[:, 8:12], in1=sd[:, 4:8], op=ALU.mult),
              [("st2b", es_)] + SS, [("st2b", es_)])
        S.add("act", lambda e, sd=sd: e.activation(out=sd[:, 8:12], in_=sd[:, 8:12], func=AF.Sqrt, scale=1.0 / 256.0, bias=EPS),
              [("st2b", es_)], [("st2b", es_)])
        S.add("dve", lambda e, sd=sd: e.reciprocal(out=sd[:, 8:12], in_=sd[:, 8:12]), [("st2b", es_)], [("st2b", es_)])
        S.add("dve", lambda e, sd=sd: e.tensor_tensor(out=sd[:, 12:16], in0=sd[:, 8:12], in1=sd[:, 0:4], op=ALU.mult),
              [("st2b", es_), SK], [("st2c", es_)])
        for hh in range(2):
            fbc = bass.AP(tensor=st2, offset=es_ * 16 + 12 + 2 * hh, ap=[[32, 128], [1, 2], [0, 256]])
            S.add("dve", lambda e, hh=hh, es_=es_, fbc=fbc: e.tensor_tensor(
                out=hmn[es_][:, 2 * hh:2 * hh + 2, :], in0=ps[4 + hh][:, :].rearrange("p (a b) -> p a b", a=2, b=256), in1=fbc,
                op=ALU.mult), [("ps", 4 + hh), ("st2c", es_)], [("hmn", es_, hh)])
        for h in range(4):
            for i2 in range(2):
                S.add("pe", lambda e, h=h, i2=i2, es_=es_: e.transpose(
                    out=psb[7][:, (2 * h + i2) * 128:(2 * h + i2 + 1) * 128], in_=hmn[es_][:, h, i2 * 128:(i2 + 1) * 128],
                    identity=ident_b[:, :]), [("hmn", es_, h // 2), ("ident_b",)], [("ps", 7)])
        S.add("dve", lambda e, n=n: e.tensor_tensor(
            out=mixT[:, 8:16, n * 128:(n + 1) * 128], in0=psb[7][:, :].rearrange("p (a b) -> p a b", a=8, b=128),
            in1=ogm[:, :, n * 128:(n + 1) * 128], op=ALU.mult),
            [("ps", 7)] + [("ogm", i, n // 4) for i in range(8)], [("mixT", 8 + i, n) for i in range(8)])

    if F_MLPIPE:
        ml_front(1)
        for blk in range(1, 9):
            if blk + 1 <= 8:
                ml_front(blk + 1)
            ml_back(blk)
    else:
        for blk in range(1, 9):
            ml_front(blk)
            ml_back(blk)

    S.alias_after(["hT", "xin"], ARENA_MIX + ["Cg"])
    def xload(n):
        dma("sp", xin[n % 2][:, 0:2048], x_in[OWN0 + n * 128:OWN0 + (n + 1) * 128, :], [], [("xin", n % 2)], ("x", n % 2))
    xload(4)
    xload(5)
    S.alias_after(["xog"], ["ogm", "kpre", "kacc"])
    ogf32 = ogm.bitcast(F32)
    xog = [ogf32[:, 4 * i:4 * i + 4, :] for i in range(2)]
    for i in range(2):
        dma("sp", xog[i], x_in[OWN0 + (6 + i) * 128:OWN0 + (7 + i) * 128, :].rearrange("p (a b) -> p a b", a=4, b=512),
            [], [("xog", i)], ("xg", i))
    xacc_i = [0]

    def xacc(n, q4):
        bank = 4 + (xacc_i[0] % 4)
        xacc_i[0] += 1
        for k4 in range(4):
            kt = q4 * 4 + k4
            if n < 4:
                xsrc, xres = xpre[n][:, q4, k4 * 128:(k4 + 1) * 128], ("xpre", n)
            elif n < 6:
                xsrc, xres = xin[n % 2][:, kt * 128:(kt + 1) * 128], ("xin", n % 2)
            else:
                xsrc, xres = xog[n - 6][:, q4, k4 * 128:(k4 + 1) * 128], ("xog", n - 6)
            S.add("pe", lambda e, xsrc=xsrc, k4=k4, bank=bank: e.transpose(
                out=ps[bank][:, k4 * 128:(k4 + 1) * 128], in_=xsrc, identity=ident_f[:, :]),
                [xres, ("ident_f",)], [("ps", bank, k4)])
        src = ps[bank][:, :].rearrange("p (a b) -> p a b", a=4, b=128)
        dst = hT[:, q4 * 4:q4 * 4 + 4, n * 128:(n + 1) * 128]
        hk = [("hT", q4 * 4 + k4, n) for k4 in range(4)]
        S.add("dve", lambda e, src=src, dst=dst: e.tensor_tensor(out=dst, in0=src, in1=dst, op=ALU.add),
              [("ps", bank, k4) for k4 in range(4)] + hk, hk)

    cpf = [0]

    def copy_hT(ct):
        def f(ci, c0, n, pap, pres):
            hres = [("hT", ct, c0 // 128 + k) for k in range(4)]
            cpf[0] += 1
            if cpf[0] % 2 == 0:
                S.add("act", lambda e: e.copy(out=hT[:, ct, c0:c0 + n], in_=pap), pres, hres)
            else:
                S.add("dve", lambda e: e.tensor_copy(out=hT[:, ct, c0:c0 + n], in_=pap), pres, hres)
        return f

    def acc_evac(ct):
        def f(ci, c0, n, pap, pres):
            hres = [("hT", ct, c0 // 128 + k) for k in range(4)]
            S.add("dve", lambda e: e.tensor_tensor(out=hT[:, ct, c0:c0 + n], in0=pap, in1=hT[:, ct, c0:c0 + n], op=ALU.add),
                  pres + hres, hres)
        return f

    def rhs_mix(kt, c0, n):
        return mixT[:, kt, c0:c0 + n]

    def res_mix(kt, c0, n):
        return [("mixT", kt, c0 // 128 + k) for k in range(4)]

    for g4 in range(4):
        for ct in range(4 * g4, 4 * g4 + 4):
            project(P_OUT + ct, CH2, rhs_mix, res_mix, copy_hT(ct), banks_per=2)
        for n in range(8):
            xacc(n, g4)

    def fm_rstd(ci, c0):
        for kt in range(KT):
            hres = [("hT", kt, c0 // 128 + k) for k in range(4)]
            sres = [("mixT", kt, c0 // 128 + k) for k in range(4)]
            S.add("act", lambda e, kt=kt: e.activation(out=sqv[:, kt, c0:c0 + 512], in_=hT[:, kt, c0:c0 + 512], func=AF.Square),
                  hres, sres)
            S.add("pe", lambda e, kt=kt: e.matmul(ps[6][:, :], lhsT=ones_b[:, :], rhs=sqv[:, kt, c0:c0 + 512],
                                                  start=(kt == 0), stop=(kt == KT - 1)), sres + [("ones_b",)], [("ps", 6)])
        S.add("act", lambda e: e.activation(out=rstd_bc[ci][:, :], in_=ps[6][:, :], func=AF.Sqrt, scale=1.0 / D, bias=EPS),
              [("ps", 6)], [("rstd", ci)])
        S.add("dve", lambda e: e.reciprocal(out=rstd_bc[ci][:, :], in_=rstd_bc[ci][:, :]), [("rstd", ci)], [("rstd", ci)])

    S.alias_after(["uT"], ["xpre"])
    for ci, (c0, n) in enumerate(CH2):
        fm_rstd(ci, c0)
        for kt in range(KT):
            hres = [("hT", kt, c0 // 128 + k) for k in range(4)]
            ures = [("uT", kt // 4, c0 // 128 + k) for k in range(4)]
            S.add("dve", lambda e, kt=kt, c0=c0, ci=ci: e.scalar_tensor_tensor(
                out=uT[:, kt, c0:c0 + 512], in0=hT[:, kt, c0:c0 + 512], scalar=cstT[:, C_NMLP + kt:C_NMLP + kt + 1],
                in1=rstd_bc[ci][:, :], op0=ALU.mult, op1=ALU.mult), hres + [("rstd", ci), ("cstT",)], ures)

    relu_t = rstd_bc
    rf = [0]
    first_evs = []
    pidx = P_FFN
    for fc in range(4):
        for ft in range(16):
            def ev(ci, c0, n, pap, pres, ft=ft):
                rs_ = rf[0] % 2
                rf[0] += 1
                S.add("act", lambda e: e.activation(out=relu_t[rs_][:, :], in_=pap, func=AF.Relu), pres, [("rstd", rs_)])
                S.add("dve", lambda e: e.tensor_tensor(out=mixT[:, ft, c0:c0 + n], in0=relu_t[rs_][:, :], in1=relu_t[rs_][:, :],
                                                       op=ALU.mult), [("rstd", rs_)], [("mixT", ft, c0 // 128 + k) for k in range(4)])
            if fc == 0 and ft < 4:
                first_evs.append(ev)
                if ft == 3:
                    for f4 in range(4):
                        project(pidx - 3 + f4, [CH2[0]], rhs_u, res_u, first_evs[f4], banks_per=2, no_prefetch=True)
                    for f4 in range(4):
                        project(pidx - 3 + f4, [CH2[1]], rhs_u, res_u, first_evs[f4], banks_per=2, no_prefetch=(f4 < 3))
                pidx += 1
                continue
            project(pidx, CH2, rhs_u, res_u, ev, banks_per=2)
            pidx += 1
        for ct in range(16):
            project(pidx, CH2, rhs_mix, res_mix, acc_evac(ct), banks_per=2)
            pidx += 1

    mixb = mixT.bitcast(F32)
    for ci, (c0, n) in enumerate(CH2):
        fm_rstd(ci, c0)
    S.alias_after(["oT"], ["xin", "Cg"])
    S.alias_after(["ostage"], ["mixT"])
    for n in range(8):
        ci, lc = n // 4, (n % 4) * 128
        s = n % 2
        oTt = xin[s][:, 0:2048]
        for kt in range(KT):
            S.add("dve", lambda e, kt=kt, n=n, ci=ci, lc=lc, oTt=oTt: e.scalar_tensor_tensor(
                out=oTt[:, kt * 128:(kt + 1) * 128], in0=hT[:, kt, n * 128:(n + 1) * 128],
                scalar=cstT[:, C_NFIN + kt:C_NFIN + kt + 1], in1=rstd_bc[ci][:, lc:lc + 128], op0=ALU.mult, op1=ALU.mult),
                [("hT", kt, n), ("rstd", ci), ("cstT",)], [("oT", s, kt)])
        for q4 in range(4):
            bank = 4 + (q4 % 2)
            for k4 in range(4):
                kt = q4 * 4 + k4
                S.add("pe", lambda e, kt=kt, k4=k4, bank=bank, oTt=oTt: e.transpose(
                    out=ps[bank][:, k4 * 128:(k4 + 1) * 128], in_=oTt[:, kt * 128:(kt + 1) * 128], identity=ident_f[:, :]),
                    [("oT", s, kt), ("ident_f",)], [("ps", bank, k4)])
            dst = mixb[:, s * 4 + q4, :]
            rd = [("ps", bank, k4) for k4 in range(4)]
            if q4 % 2 == 0:
                S.add("act", lambda e, dst=dst, bank=bank: e.copy(out=dst, in_=ps[bank][:, :]), rd, [("ostage", s, q4)])
            else:
                S.add("dve", lambda e, dst=dst, bank=bank: e.tensor_copy(out=dst, in_=ps[bank][:, :]), rd, [("ostage", s, q4)])
        dma("sp", out_d[n * 128:(n + 1) * 128, :], mixb[:, s * 4:(s + 1) * 4, :].rearrange("p a b -> p (a b)"),
            [("ostage", s, q4) for q4 in range(4)], [("outd", n)], ("o", s))
    S.emit(nc, es)
    es.close()
    return nc


def _t5_bucket(d):
    d = np.maximum(d, 0)
    ratio = (np.maximum(d, 16).astype(np.float32) / np.float32(16)).astype(np.float32)
    large = 16 + (np.log(ratio).astype(np.float32) / np.float32(np.log(128 / 16)) * np.float32(16)).astype(np.int32)
    large = np.minimum(large, 31)
    return np.where(d < 16, d, large)


def _ohc(j):
    oh = np.zeros((33, NCV), np.float32)
    e = np.arange(383)
    d = e - 127
    valid = (d >= 0) & (d < 128)
    bk = _t5_bucket(d)
    for i in range(383):
        if valid[i]:
            oh[bk[i], i] = 1.0
        else:
            oh[32, i] = NEGM
    ee = np.arange(143)
    if j == 0:
        bk = _t5_bucket(ee + 1)
    else:
        bk = np.full(143, 31)
    for i in range(143):
        oh[bk[i], 383 + i] = 1.0
        oh[31, 526 + i] = 1.0
    return oh


_NC_CACHE = {}


def kernel(x, meta_tokens, w_in, conv_w, conv_b, b_igate, b_fgate, attn_sinks, rel_bias, mh_norm, w_out,
           norm_mix, norm_mlp, w_up, w_down, norm_final):
    f = lambda a: np.ascontiguousarray(np.asarray(a, dtype=np.float32))
    x, meta_tokens, w_in, w_out, w_up, w_down = f(x), f(meta_tokens), f(w_in), f(w_out), f(w_up), f(w_down)
    rows96 = np.concatenate([f(norm_mix).reshape(16, 128), f(norm_mlp).reshape(16, 128), f(norm_final).reshape(16, 128),
                             f(mh_norm).reshape(8, 128), f(conv_b).reshape(8, 128), f(conv_w).reshape(32, 128)], axis=0)
    rel33 = np.concatenate([f(rel_bias), np.ones((1, 16), np.float32)], axis=0)
    lead = np.concatenate([np.zeros((112, D), np.float32), meta_tokens], axis=0)
    in_maps = []
    for c in range(8):
        b, j = c // 4, c % 4
        prev = lead if j == 0 else x[b, 1024 * j - 128:1024 * j]
        xin = np.concatenate([prev, x[b, 1024 * j:1024 * (j + 1)], meta_tokens], axis=0)
        small = np.zeros((128, 64), np.float32)
        small[:, 0:16] = f(attn_sinks).reshape(1, 16)
        small[:, 16:20] = f(b_igate).reshape(1, 4)
        small[:, 20:24] = f(b_fgate).reshape(1, 4)
        small[:, 24] = 1.0 if j > 1 else 0.0
        small[:, 25] = 1.0 if j > 2 else 0.0
        small[:, 26] = 1.0 if j > 2 else 0.0
        for i in range(3):
            small[:, 27 + i] = 1.0 if i < j else 0.0
        small[:, 30] = 0.0 if j == 0 else 1.0
        valid = np.ones((128, 9), np.float32)
        if j == 0:
            valid[:112, 0] = 0.0
        else:
            valid[:, 0] = 0.0
        small[:, 40:49] = valid
        small[:, 49:58] = (valid - 1.0) * (-NEGM)
        in_maps.append({"xin": np.ascontiguousarray(xin), "w_in": w_in, "w_out": w_out, "w_up": w_up, "w_down": w_down,
                        "rows96": rows96, "rel33": rel33, "ohc": _ohc(j), "small": small})
    if "nc" not in _NC_CACHE:
        _NC_CACHE["nc"] = build_program()
    res = run_bass_kernel_spmd(_NC_CACHE["nc"], in_maps, core_ids=list(range(8)))
    out = np.zeros((2, 4096, D), np.float32)
    for c in range(8):
        b, j = c // 4, c % 4
        out[b, 1024 * j:1024 * (j + 1)] = res.results[c]["out"]
    return out


def _deadlock_check(S):
    engs = ["pe", "act", "dve", "pool", "sp"]
    per = {e: [o for o in S.ops if o.eng == e] for e in engs}
    ptr = {e: 0 for e in engs}
    done = set()
    prog = True
    while prog:
        prog = False
        for e in engs:
            while ptr[e] < len(per[e]):
                op = per[e][ptr[e]]
                if all(d.idx in done for d in op.deps):
                    done.add(op.idx)
                    ptr[e] += 1
                    prog = True
                else:
                    break
    stuck = {e: (ptr[e], len(per[e])) for e in engs}
    return stuck, {e: per[e][ptr[e]] for e in engs if ptr[e] < len(per[e])}
```
